# Optimizing a Trainium2 kernel written in Bass

```python
import math
import jax, jax.numpy as jnp
from jax import lax
import numpy as np

D_MODEL = 2048
BATCH = 4
SEQ = 2048
DEPTH = 4

GRID_W = 64
CTX_LEN = 256
N_MIXERS = 3
D_FF = 4 * D_MODEL
EPS = 1e-6

N_A = len(range(0, DEPTH, N_MIXERS))
N_B = len(range(1, DEPTH, N_MIXERS))
N_C = len(range(2, DEPTH, N_MIXERS))

NA_HEADS = 16
NA_HEAD_DIM = D_MODEL // NA_HEADS
NA_WIN_ROWS = 8
NA_WIN_COLS = 16

SSM_D_INNER = 2 * D_MODEL
SSM_HEAD_DIM = 64
SSM_HEADS = SSM_D_INNER // SSM_HEAD_DIM
SSM_GROUPS = 8
SSM_HEADS_PER_GROUP = SSM_HEADS // SSM_GROUPS
SSM_STATE = 128
SSM_CONV = 5
SSM_CHUNK = 128
SSM_CONV_DIM = SSM_D_INNER + 2 * SSM_GROUPS * SSM_STATE
SSM_IN_DIM = SSM_D_INNER + SSM_CONV_DIM + 2 * SSM_HEADS

DA_HEADS = 8
DA_HEAD_DIM = D_MODEL // DA_HEADS // 2
DA_Q_BLOCK = 128
DA_SUBLN_EPS = 1e-5
ROPE_BASE = 10000.0

kernel_name = 'hybrid_natten_ssd_diffattn_dit'


def rmsnorm(x, w, eps=EPS):
    xf = x.astype(jnp.float32)
    y = xf * lax.rsqrt(jnp.mean(xf * xf, axis=-1, keepdims=True) + eps)
    return (y * w.astype(jnp.float32)).astype(x.dtype)


def adaln(cond, w, b):
    m = jax.nn.silu(cond) @ w + b
    return jnp.split(m, 6, axis=-1)


def modulate(x, shift, scale):
    return x * (1 + scale) + shift


def sq_relu_mlp(u, w1, w2):
    return jnp.square(jax.nn.relu(u @ w1)) @ w2


def rope_2d_tables(n_tokens, head_dim, dtype):
    t = jnp.arange(n_tokens)
    pos = jnp.stack([t // GRID_W, t % GRID_W], axis=-1).astype(jnp.float32)
    n_freq = head_dim // 4
    inv_freq = ROPE_BASE ** (-jnp.arange(n_freq, dtype=jnp.float32) / n_freq)
    ang = pos[:, :, None] * inv_freq
    return jnp.cos(ang).astype(dtype), jnp.sin(ang).astype(dtype)


def apply_rope_2d(x, cos, sin):
    Bsz, S, H, dh = x.shape
    xa = x.reshape(Bsz, S, H, 2, 2, dh // 4)
    x1, x2 = xa[..., 0, :], xa[..., 1, :]
    cs, sn = cos[None, :, None], sin[None, :, None]
    return jnp.stack([x1 * cs - x2 * sn, x2 * cs + x1 * sn], axis=-2).reshape(Bsz, S, H, dh)


def neighbourhood_attention(u, uc, w_qkv, w_o, rpb, with_ctx_out):
    Bsz, S, _ = u.shape
    rows = S // GRID_W
    kr = min(NA_WIN_ROWS, rows)
    H, dh = NA_HEADS, NA_HEAD_DIM
    hd = H * dh
    scale = dh ** -0.5
    f32 = jnp.float32
    q, k, v = jnp.split(u @ w_qkv, 3, axis=-1)
    q, k, v = (t.reshape(Bsz, rows, GRID_W, H, dh) for t in (q, k, v))
    kc, vc = jnp.split(uc @ w_qkv[:, hd:], 2, axis=-1)
    Lc = uc.shape[1]
    kc, vc = kc.reshape(Bsz, Lc, H, dh), vc.reshape(Bsz, Lc, H, dh)

    col = jnp.arange(GRID_W)
    col_start = jnp.clip(col - NA_WIN_COLS // 2, 0, GRID_W - NA_WIN_COLS)
    col_mask = (col[None, :] >= col_start[:, None]) & (col[None, :] < col_start[:, None] + NA_WIN_COLS)
    dc_idx = jnp.clip(col[None, :] - col[:, None], -(NA_WIN_COLS - 1), NA_WIN_COLS - 1) + NA_WIN_COLS - 1
    rpb_cols = rpb[:, :, dc_idx]

    def row_block(r):
        rs = jnp.clip(r - kr // 2, 0, rows - kr)
        q_r = lax.dynamic_index_in_dim(q, r, axis=1, keepdims=False)
        k_r = lax.dynamic_slice_in_dim(k, rs, kr, axis=1)
        v_r = lax.dynamic_slice_in_dim(v, rs, kr, axis=1)
        dr_idx = rs + jnp.arange(kr) - r + NA_WIN_ROWS - 1
        bias = jnp.take(rpb_cols, dr_idx, axis=1).transpose(0, 2, 1, 3)
        s_loc = jnp.einsum('bqhd,bjkhd->bhqjk', q_r, k_r).astype(f32) * scale + bias[None].astype(f32)
        s_loc = jnp.where(col_mask[:, None, :], s_loc, -jnp.inf).reshape(Bsz, H, GRID_W, kr * GRID_W)
        s_ctx = jnp.einsum('bqhd,bchd->bhqc', q_r, kc).astype(f32) * scale
        p = jax.nn.softmax(jnp.concatenate([s_loc, s_ctx], axis=-1), axis=-1).astype(v.dtype)
        p_loc = p[..., :kr * GRID_W].reshape(Bsz, H, GRID_W, kr, GRID_W)
        p_ctx = p[..., kr * GRID_W:]
        return jnp.einsum('bhqjk,bjkhd->bqhd', p_loc, v_r) + jnp.einsum('bhqc,bchd->bqhd', p_ctx, vc)

    o = lax.map(row_block, jnp.arange(rows))
    y = jnp.moveaxis(o, 0, 1).reshape(Bsz, S, hd) @ w_o
    yc = None
    if with_ctx_out:
        qc = (uc @ w_qkv[:, :hd]).reshape(Bsz, Lc, H, dh)
        s = jnp.einsum('bqhd,bkhd->bhqk', qc, kc).astype(f32) * scale
        p = jax.nn.softmax(s, axis=-1).astype(vc.dtype)
        yc = jnp.einsum('bhqk,bkhd->bqhd', p, vc).reshape(Bsz, Lc, hd) @ w_o
    return y, yc


def centred_depthwise_conv(x, w, b):
    pad = w.shape[0] // 2
    xp = jnp.pad(x, ((0, 0), (pad, pad), (0, 0)))
    y = lax.conv_general_dilated(xp, w[:, None, :], window_strides=(1,), padding='VALID',
                                 dimension_numbers=('NWC', 'WIO', 'NWC'), feature_group_count=x.shape[-1])
    return y + b


def ssd_chunked(x, dt, A, Bm, Cm, h0, with_output=True):
    f32 = jnp.float32
    Bsz, L, G, R, P = x.shape
    N = Bm.shape[-1]
    Q = SSM_CHUNK
    nc = L // Q
    xdt = (x.astype(f32) * dt[..., None]).reshape(Bsz, nc, Q, G, R, P)
    Bc = Bm.astype(f32).reshape(Bsz, nc, Q, G, N)
    Cc = Cm.astype(f32).reshape(Bsz, nc, Q, G, N)
    a_cum = jnp.cumsum((dt * A).reshape(Bsz, nc, Q, G, R), axis=2)
    a_last = a_cum[:, :, -1]
    decay_to_end = jnp.exp(a_last[:, :, None] - a_cum)
    states = jnp.einsum('bcjgn,bcjgr,bcjgrp->bcgrpn', Bc, decay_to_end, xdt)

    def step(h, inp):
        dec, st = inp
        return dec[..., None, None] * h + st, h

    h_final, h_starts = lax.scan(step, h0, (jnp.moveaxis(jnp.exp(a_last), 1, 0), jnp.moveaxis(states, 1, 0)))
    if not with_output:
        return None, h_final
    h_starts = jnp.moveaxis(h_starts, 0, 1)
    seg = a_cum[:, :, :, None] - a_cum[:, :, None]
    lower = jnp.tril(jnp.ones((Q, Q), dtype=bool))[:, :, None, None]
    lmat = jnp.exp(jnp.where(lower, seg, -jnp.inf))
    cb = jnp.einsum('bcign,bcjgn->bcijg', Cc, Bc)
    y_diag = jnp.einsum('bcijg,bcijgr,bcjgrp->bcigrp', cb, lmat, xdt)
    y_off = jnp.einsum('bcign,bcgrpn,bcigr->bcigrp', Cc, h_starts, jnp.exp(a_cum))
    y = (y_diag + y_off).reshape(Bsz, L, G, R, P).astype(x.dtype)
    return y, h_final


def mamba2_bidirectional(u, uc, w_in, conv_w, conv_b, dt_bias, a_log, d_skip, norm_w, w_out, with_ctx_out):
    f32 = jnp.float32
    G, R, P, N, DI = SSM_GROUPS, SSM_HEADS_PER_GROUP, SSM_HEAD_DIM, SSM_STATE, SSM_D_INNER
    A = -jnp.exp(a_log.astype(f32)).reshape(2, G, R)
    dtb = dt_bias.astype(f32).reshape(2, G, R)

    def project(h, need_gate):
        Bsz, L, _ = h.shape
        proj = h @ (w_in if need_gate else w_in[:, DI:])
        z = proj[..., :DI] if need_gate else None
        proj = proj[..., DI:] if need_gate else proj
        xbc = jax.nn.silu(centred_depthwise_conv(proj[..., :SSM_CONV_DIM], conv_w, conv_b))
        xs = xbc[..., :DI].reshape(Bsz, L, G, R, P)
        Bm = xbc[..., DI:DI + G * N].reshape(Bsz, L, G, N)
        Cm = xbc[..., DI + G * N:].reshape(Bsz, L, G, N)
        dt = jax.nn.softplus(proj[..., SSM_CONV_DIM:].astype(f32).reshape(Bsz, L, 2, G, R) + dtb)
        return z, xs, Bm, Cm, dt

    zl, xl, Bl, Cl, dtl = project(u, True)
    zc, xc, Bc, Cc, dtc = project(uc, with_ctx_out)
    h0 = jnp.zeros((u.shape[0], G, R, P, N), f32)
    flip = lambda t: jnp.flip(t, axis=1)
    yc_f, hc_f = ssd_chunked(xc, dtc[:, :, 0], A[0], Bc, Cc, h0, with_ctx_out)
    yc_b, hc_b = ssd_chunked(flip(xc), flip(dtc[:, :, 1]), A[1], flip(Bc), flip(Cc), h0, with_ctx_out)
    yl_f, _ = ssd_chunked(xl, dtl[:, :, 0], A[0], Bl, Cl, hc_f)
    yl_b, _ = ssd_chunked(flip(xl), flip(dtl[:, :, 1]), A[1], flip(Bl), flip(Cl), hc_b)

    def finish(y, xs, z):
        Bsz, L = y.shape[:2]
        y = (y + xs * d_skip.reshape(G, R, 1)).reshape(Bsz, L, DI) * jax.nn.silu(z)
        y = rmsnorm(y.reshape(Bsz, L, G, DI // G), norm_w.reshape(G, DI // G)).reshape(Bsz, L, DI)
        return y @ w_out

    y = finish(yl_f + flip(yl_b), xl, zl)
    yc = finish(yc_f + flip(yc_b), xc, zc) if with_ctx_out else None
    return y, yc


def diff_attention(u, uc, w_qkv, w_o, lam, subln_w, lambda_init, cos, sin, with_ctx_out):
    f32 = jnp.float32
    Bsz, S, _ = u.shape
    Lc = uc.shape[1]
    H, dh = DA_HEADS, DA_HEAD_DIM
    qk_w = 2 * H * dh
    q, k, v = jnp.split(u @ w_qkv, [qk_w, 2 * qk_w], axis=-1)
    q = apply_rope_2d(q.reshape(Bsz, S, 2 * H, dh), cos, sin)
    k = apply_rope_2d(k.reshape(Bsz, S, 2 * H, dh), cos, sin)
    v = v.reshape(Bsz, S, H, 2 * dh)
    kc, vc = jnp.split(uc @ w_qkv[:, qk_w:], [qk_w], axis=-1)
    kc, vc = kc.reshape(Bsz, Lc, 2 * H, dh), vc.reshape(Bsz, Lc, H, 2 * dh)
    lam = lam.astype(f32)
    lam_full = jnp.exp(jnp.sum(lam[0] * lam[1])) - jnp.exp(jnp.sum(lam[2] * lam[3])) + lambda_init
    scale = dh ** -0.5

    def attend(qb, keys, vals):
        Lq = qb.shape[1]
        s = jnp.einsum('bqhd,bkhd->bhqk', qb, keys).astype(f32) * scale
        p = jax.nn.softmax(s, axis=-1).reshape(Bsz, H, 2, Lq, keys.shape[1])
        a = (p[:, :, 0] - lam_full * p[:, :, 1]).astype(vals.dtype)
        return jnp.einsum('bhqk,bkhe->bqhe', a, vals)

    def finish(o, L):
        o = rmsnorm(o, subln_w, DA_SUBLN_EPS) * (1 - lambda_init)
        return o.reshape(Bsz, L, H * 2 * dh) @ w_o

    keys = jnp.concatenate([k, kc], axis=1)
    vals = jnp.concatenate([v, vc], axis=1)
    nb = S // DA_Q_BLOCK
    qb = jnp.moveaxis(q.reshape(Bsz, nb, DA_Q_BLOCK, 2 * H, dh), 1, 0)
    o = lax.map(lambda t: attend(t, keys, vals), qb)
    y = finish(jnp.moveaxis(o, 0, 1).reshape(Bsz, S, H, 2 * dh), S)
    yc = None
    if with_ctx_out:
        qc = (uc @ w_qkv[:, :qk_w]).reshape(Bsz, Lc, 2 * H, dh)
        yc = finish(attend(qc, kc, vc), Lc)
    return y, yc


def setup_inputs(seed: int = 0) -> dict:
    key = jax.random.key(seed)
    ks = iter(jax.random.split(key, 32))
    nrm = lambda shape, s: jax.random.normal(next(ks), shape, jnp.float32) * s
    D = D_MODEL
    x = nrm((BATCH, SEQ, D), 1.0)
    c = nrm((BATCH, D), 1.0)
    ctx = nrm((BATCH, CTX_LEN, D), 1.0)
    c_ctx = nrm((D,), 1.0)
    ada_w = nrm((DEPTH, D, 6 * D), 0.5 * D ** -0.5)
    ada_b = nrm((DEPTH, 6 * D), 0.01)
    norm_mix_w = 1.0 + nrm((DEPTH, D), 0.01)
    norm_mlp_w = 1.0 + nrm((DEPTH, D), 0.01)
    mlp_w1 = nrm((DEPTH, D, D_FF), D ** -0.5)
    mlp_w2 = nrm((DEPTH, D_FF, D), D_FF ** -0.5)
    na_w_qkv = nrm((N_A, D, 3 * NA_HEADS * NA_HEAD_DIM), D ** -0.5)
    na_w_o = nrm((N_A, NA_HEADS * NA_HEAD_DIM, D), (NA_HEADS * NA_HEAD_DIM) ** -0.5)
    na_rpb = nrm((N_A, NA_HEADS, 2 * NA_WIN_ROWS - 1, 2 * NA_WIN_COLS - 1), 0.1)
    ssm_w_in = nrm((N_B, D, SSM_IN_DIM), D ** -0.5)
    ssm_conv_w = nrm((N_B, SSM_CONV, SSM_CONV_DIM), SSM_CONV ** -0.5)
    ssm_conv_b = nrm((N_B, SSM_CONV_DIM), 0.01)
    dt0 = jnp.exp(jax.random.uniform(next(ks), (N_B, 2, SSM_HEADS), jnp.float32, math.log(1e-3), math.log(1e-1)))
    ssm_dt_bias = dt0 + jnp.log(-jnp.expm1(-dt0))
    ssm_a_log = jnp.log(jax.random.uniform(next(ks), (N_B, 2, SSM_HEADS), jnp.float32, 1.0, 16.0))
    ssm_d = 1.0 + nrm((N_B, SSM_HEADS), 0.01)
    ssm_norm_w = 1.0 + nrm((N_B, SSM_D_INNER), 0.01)
    ssm_w_out = nrm((N_B, SSM_D_INNER, D), SSM_D_INNER ** -0.5)
    da_w_qkv = nrm((N_C, D, 3 * 2 * DA_HEADS * DA_HEAD_DIM), D ** -0.5)
    da_w_o = nrm((N_C, 2 * DA_HEADS * DA_HEAD_DIM, D), (2 * DA_HEADS * DA_HEAD_DIM) ** -0.5)
    da_lambda = nrm((N_C, 4, DA_HEAD_DIM), 0.1)
    da_subln_w = 1.0 + nrm((N_C, 2 * DA_HEAD_DIM), 0.01)
    final_norm_w = 1.0 + nrm((D,), 0.01)
    return {'x': x, 'c': c, 'ctx': ctx, 'c_ctx': c_ctx, 'ada_w': ada_w, 'ada_b': ada_b,
            'norm_mix_w': norm_mix_w, 'norm_mlp_w': norm_mlp_w, 'mlp_w1': mlp_w1, 'mlp_w2': mlp_w2,
            'na_w_qkv': na_w_qkv, 'na_w_o': na_w_o, 'na_rpb': na_rpb,
            'ssm_w_in': ssm_w_in, 'ssm_conv_w': ssm_conv_w, 'ssm_conv_b': ssm_conv_b,
            'ssm_dt_bias': ssm_dt_bias, 'ssm_a_log': ssm_a_log, 'ssm_d': ssm_d,
            'ssm_norm_w': ssm_norm_w, 'ssm_w_out': ssm_w_out,
            'da_w_qkv': da_w_qkv, 'da_w_o': da_w_o, 'da_lambda': da_lambda, 'da_subln_w': da_subln_w,
            'final_norm_w': final_norm_w}


def reference(x, c, ctx, c_ctx, ada_w, ada_b, norm_mix_w, norm_mlp_w, mlp_w1, mlp_w2,
              na_w_qkv, na_w_o, na_rpb, ssm_w_in, ssm_conv_w, ssm_conv_b, ssm_dt_bias, ssm_a_log,
              ssm_d, ssm_norm_w, ssm_w_out, da_w_qkv, da_w_o, da_lambda, da_subln_w, final_norm_w):
    h, hc = x, ctx
    cos, sin = rope_2d_tables(x.shape[1], DA_HEAD_DIM, x.dtype)
    for i in range(DEPTH):
        last = i == DEPTH - 1
        mixer, j = i % N_MIXERS, i // N_MIXERS
        sh_a, sc_a, g_a, sh_m, sc_m, g_m = adaln(c[:, None, :], ada_w[i], ada_b[i])
        csh_a, csc_a, cg_a, csh_m, csc_m, cg_m = adaln(c_ctx, ada_w[i], ada_b[i])
        u = modulate(rmsnorm(h, norm_mix_w[i]), sh_a, sc_a)
        uc = modulate(rmsnorm(hc, norm_mix_w[i]), csh_a, csc_a)
        if mixer == 0:
            y, yc = neighbourhood_attention(u, uc, na_w_qkv[j], na_w_o[j], na_rpb[j], not last)
        elif mixer == 1:
            y, yc = mamba2_bidirectional(u, uc, ssm_w_in[j], ssm_conv_w[j], ssm_conv_b[j], ssm_dt_bias[j],
                                         ssm_a_log[j], ssm_d[j], ssm_norm_w[j], ssm_w_out[j], not last)
        else:
            lambda_init = 0.8 - 0.6 * math.exp(-0.3 * i)
            y, yc = diff_attention(u, uc, da_w_qkv[j], da_w_o[j], da_lambda[j], da_subln_w[j],
                                   lambda_init, cos, sin, not last)
        h = h + g_a * y
        h = h + g_m * sq_relu_mlp(modulate(rmsnorm(h, norm_mlp_w[i]), sh_m, sc_m), mlp_w1[i], mlp_w2[i])
        if not last:
            hc = hc + cg_a * yc
            hc = hc + cg_m * sq_relu_mlp(modulate(rmsnorm(hc, norm_mlp_w[i]), csh_m, csc_m), mlp_w1[i], mlp_w2[i])
    return rmsnorm(h, final_norm_w)
```

```python
import numpy as np
from contextlib import ExitStack
import concourse.bass as bass
import concourse.mybir as mybir
from concourse.bass_utils import run_bass_kernel_spmd

F32 = mybir.dt.float32
F32R = mybir.dt.float32r
BF16 = mybir.dt.bfloat16
AF = mybir.ActivationFunctionType
ALU = mybir.AluOpType
AX = mybir.AxisListType


class Buf:
    def __init__(self, kb, t, name):
        self.kb, self.t, self.name = kb, t, name
        self.last_w = None
        self.readers = []
        self.dsem = None
        self.dcnt = 0

    def __getitem__(self, idx):
        return self.t[idx]


class KB:
    def __init__(self):
        self.nc = bass.Bass("TRN2", target_bir_lowering=False)
        self.es = ExitStack()
        nc = self.nc
        self.E = {"pe": nc.tensor, "act": nc.scalar, "dve": nc.vector, "pool": nc.gpsimd, "sp": nc.sync}
        self.sems = {}
        self.cnt = {}
        for e in self.E:
            self.sems[e] = self.es.enter_context(nc.semaphore("s_" + e))
            self.cnt[e] = 0
        self.seen = {e: {} for e in self.E}
        self.nbuf = 0
        self.out_deps = []
        self.stack = [self.es]
        self.io = {}
        self.prefix = ""
        self.dpool = []
        self.scope_bufs = []
        self.fused = False
        self.bar = self.es.enter_context(nc.semaphore("s_bar"))
        self.barcnt = 0
        self.nsem = 0

    def push(self, prefix):
        self.prefix = prefix
        self.stack.append(ExitStack())
        self.scope_bufs = []

    def pop(self):
        for b in self.scope_bufs:
            if b.dsem is not None:
                self.dpool.append((b.dsem, b.dcnt))
                b.dsem = None
        self.scope_bufs = []
        self.stack.pop().close()
        self.prefix = ""

    def barrier(self):
        for d in self.out_deps:
            self._wait("sp", d)
        self.out_deps = []
        for e in self.E:
            if e != "sp" and self.cnt[e] > 0:
                self._wait("sp", (e, self.cnt[e]))
        self.nc.sync.sem_inc(self.bar, 1)
        self.barcnt += 1
        for e in self.E:
            if e != "sp":
                self.E[e].wait_ge(self.bar, self.barcnt)

    def sb(self, name, shape, dtype=F32):
        t = self.stack[-1].enter_context(self.nc.sbuf_tensor(self.prefix + name, list(shape), dtype))
        b = Buf(self, t, self.prefix + name)
        self.scope_bufs.append(b)
        return b

    def ps(self, name, shape, dtype=F32):
        t = self.stack[-1].enter_context(self.nc.psum_tensor(self.prefix + name, list(shape), dtype))
        b = Buf(self, t, self.prefix + name)
        self.scope_bufs.append(b)
        return b

    def din(self, name, shape, dtype=F32):
        if name in self.io:
            return self.io[name]
        return self.nc.dram_tensor(name, list(shape), dtype, kind="ExternalInput")

    def dout(self, name, shape, dtype=F32):
        if name in self.io:
            return self.io[name]
        return self.nc.dram_tensor(name, list(shape), dtype, kind="ExternalOutput")

    def scratch(self, name, shape, dtype=F32):
        return self.nc.dram_tensor(name, list(shape), dtype)

    def _semobj(self, key):
        return self.sems[key] if isinstance(key, str) else key

    def _wait(self, eng, dep):
        key, val = dep
        kid = key if isinstance(key, str) else id(key)
        if not isinstance(key, str) and not isinstance(key, Buf):
            kid = id(key)
        if isinstance(key, str):
            assert val <= self.cnt[key], f"wait on unissued inc {key} {val}>{self.cnt[key]}"
        if self.seen[eng].get(kid, 0) >= val:
            return
        self.E[eng].wait_ge(self._semobj(key), val)
        self.seen[eng][kid] = val

    def _sync(self, eng, reads, writes):
        deps = []
        for b in reads:
            if b.last_w is not None:
                deps.append(b.last_w)
        for b in writes:
            if b.last_w is not None:
                deps.append(b.last_w)
            deps.extend(b.readers)
        for d in deps:
            if eng == "pe" and d[0] == "pe":
                continue
            self._wait(eng, d)

    def op(self, eng, fn, reads=(), writes=(), inc=True):
        self._sync(eng, reads, writes)
        inst = fn(self.E[eng])
        if inc:
            inst.then_inc(self.sems[eng], 1)
            self.cnt[eng] += 1
            val = self.cnt[eng]
        else:
            val = self.cnt[eng] + 1
        dep = (eng, val)
        for b in reads:
            b.readers.append(dep)
        for b in writes:
            b.last_w = dep
            b.readers = []
        return inst

    def dram(self, name, shape, dtype=F32):
        t = self.nc.dram_tensor(name, list(shape), dtype)
        return Buf(self, t, name)

    def _dsem(self, b):
        if b.dsem is None:
            if self.dpool:
                b.dsem, b.dcnt = self.dpool.pop()
            else:
                self.nsem += 1
                b.dsem = self.es.enter_context(self.nc.semaphore(f"d_{self.nsem}"))
                b.dcnt = 0
        return b.dsem

    def dma(self, q, out, in_, sbuf=None, load=None, reads=(), writes=(), is_out=False):
        reads, writes = list(reads), list(writes)
        if sbuf is not None:
            if load:
                writes = [sbuf] + writes
            else:
                reads = [sbuf] + reads
                is_out = True
        owner = writes[0] if writes else reads[0]
        self._dsem(owner)
        self._sync(q, reads, writes)
        inst = self.E[q].dma_start(out=out, in_=in_)
        inst.then_inc(owner.dsem, 16)
        owner.dcnt += 16
        dep = (owner.dsem, owner.dcnt)
        for b in reads:
            b.readers.append(dep)
        for b in writes:
            b.last_w = dep
            b.readers = []
        if is_out:
            self.out_deps.append(dep)
        return inst

    def allgather(self, out_buf, in_buf, groups):
        self._dsem(out_buf)
        self._sync("pool", [in_buf], [out_buf])
        inst = self.nc.gpsimd.collective_compute("AllGather", ALU.bypass, replica_groups=groups,
                                                 ins=[in_buf[:]], outs=[out_buf[:]])
        inst.then_inc(out_buf.dsem, 16)
        out_buf.dcnt += 16
        dep = (out_buf.dsem, out_buf.dcnt)
        in_buf.readers.append(dep)
        out_buf.last_w = dep
        out_buf.readers = []
        return inst

    def finish(self, eng="sp"):
        if self.fused:
            return self.barrier()
        for d in self.out_deps:
            self._wait(eng, d)
        for e in self.E:
            if e != eng and self.cnt[e] > 0:
                self._wait(eng, (e, self.cnt[e]))

    def mm(self, out_ap, lhsT, rhs, start, stop, reads, writes, inc=None):
        return self.op("pe", lambda e: e.matmul(out_ap, lhsT, rhs, start=start, stop=stop),
                       reads=reads, writes=writes, inc=stop if inc is None else inc)

    def act(self, out_ap, in_ap, func, reads, writes, bias=None, scale=1.0, eng="act", **kw):
        def f(e):
            kws = dict(kw)
            if bias is not None:
                kws["bias"] = bias
            return e.activation(out=out_ap, in_=in_ap, func=func, scale=scale, **kws)
        return self.op(eng, f, reads=reads, writes=writes)


def run(kb, in_maps, n=8, trace=False):
    res = run_bass_kernel_spmd(kb.nc, in_maps, core_ids=list(range(n)), trace=trace)
    return res


D = 2048
NCORES = 8


def build_ada(kb_=None):
    kb = kb_ if kb_ is not None else KB()
    CW = 1536
    condT = kb.din("condT", [128, 16, 5])
    w = kb.din("w", [4, D, CW])
    b = kb.din("b", [4, CW])
    o = kb.dout("o", [4, 5, CW])
    ct = kb.sb("ct", [128, 16, 5])
    cs = kb.sb("cs", [128, 16, 5])
    ones = kb.sb("ones", [1, 8])
    bt = kb.sb("bt", [1, 4, CW])
    wt = [kb.sb(f"wt{i}", [128, CW]) for i in range(4)]
    res = [kb.sb(f"res{i}", [5, CW]) for i in range(2)]
    pss = [kb.ps(f"ps{i}", [128, 512]) for i in range(6)]
    kb.dma("sp", ct[:], condT.ap(), ct, True)
    kb.dma("sp", bt[:], b.ap().rearrange("(o l) c -> o l c", o=1), bt, True)
    kb.op("dve", lambda e: e.memset(ones[:], 1.0), writes=[ones])
    kb.act(cs[:], ct[:], AF.Silu, [ct], [cs])
    i = 0
    for l in range(4):
        pb = pss[(l % 2) * 3:(l % 2) * 3 + 3]
        for t in range(3):
            kb.mm(pb[t][0:5, :], ones[0:1, 0:5], bt[0:1, l, t * 512:(t + 1) * 512], True, False, [ones, bt], [pb[t]])
        for k in range(16):
            wb_ = wt[i % 4]
            i += 1
            kb.dma("sp" if k % 2 == 0 else "act", wb_[:], w.ap()[l, k * 128:(k + 1) * 128, :], wb_, True)
            for t in range(3):
                kb.mm(pb[t][0:5, :], cs[:, k, :], wb_[:, t * 512:(t + 1) * 512], False, k == 15, [cs, wb_], [pb[t]],
                      inc=(t == 2 or k == 15))
        r = res[l % 2]
        for t in range(3):
            kb.op("dve", lambda e: e.tensor_copy(out=r[:, t * 512:(t + 1) * 512], in_=pb[t][0:5, :]), reads=[pb[t]], writes=[r])
        kb.dma("sp", o.ap()[l], r[:], r, False)
    kb.finish()
    return kb


def run_ada(inp):
    cond = np.concatenate([inp["c"], inp["c_ctx"][None]], 0)
    condT = np.ascontiguousarray(cond.T.reshape(16, 128, 5).transpose(1, 0, 2))
    kb = build_ada()
    maps = []
    for c in range(NCORES):
        sl = slice(1536 * c, 1536 * (c + 1))
        maps.append({"condT": condT, "w": np.ascontiguousarray(inp["ada_w"][:, :, sl]),
                     "b": np.ascontiguousarray(inp["ada_b"][:, sl])})
    res = run(kb, maps)
    mod = np.concatenate([r["o"] for r in res.results], axis=-1)
    return mod


NT = 2304
TB = 768
SUB = 384
NBLK = NT // TB
EPS = 1e-6


def col_ranges(blk):
    lo, hi = blk * TB, (blk + 1) * TB
    out = []
    if lo < 2048:
        out.append((0, min(hi, 2048) - lo, 0))
    if hi > 2048:
        out.append((max(lo, 2048) - lo, hi - lo, 1))
    return out


class TokStage:
    def __init__(self, kb, nmod):
        self.kb = kb
        self.ones = kb.sb("ones", [128, 128])
        self.epsb = kb.sb("epsb", [128, 1])
        kb.op("dve", lambda e: e.memset(self.ones[:], 1.0), writes=[self.ones])
        kb.op("dve", lambda e: e.memset(self.epsb[:], EPS), writes=[self.epsb])
        self.pss = [kb.ps(f"ps{i}", [128, 512]) for i in range(8)]
        self.pi = 0
        self.sq = [kb.sb(f"sq{i}", [128, TB]) for i in range(2)]
        self.rstd = kb.sb("rstd", [128, TB])
        self.tmp = [kb.sb(f"tmp{i}", [128, TB]) for i in range(2)]
        self.ti = 0

    def psum(self):
        p = self.pss[self.pi % 8]
        self.pi += 1
        return p

    def load_mod(self, modT_d, normw_d, idx_shift, idx_scale):
        kb = self.kb
        mt = kb.sb(f"modT{idx_shift}", [128, 2, 6, 16])
        nw = kb.sb(f"nw{idx_shift}", [128, 16])
        kb.dma("sp", mt[:], modT_d.ap(), sbuf=mt, load=True)
        kb.dma("sp", nw[:], normw_d.ap(), sbuf=nw, load=True)
        A = kb.sb(f"A{idx_shift}", [128, 2, 16])
        for w in range(2):
            kb.op("dve", lambda e: e.scalar_tensor_tensor(out=A[:, w, :], in0=mt[:, w, idx_scale, :], scalar=1.0,
                                                          in1=nw[:], op0=ALU.add, op1=ALU.mult),
                  reads=[mt, nw], writes=[A])
        return mt, A

    def norm_mod(self, hk, blk, A, mt, idx_shift, outs):
        kb = self.kb
        pb = [self.psum() for _ in range(TB // SUB)]
        for k in range(16):
            s = self.sq[k % 2]
            kb.act(s[:], hk[k][:], AF.Square, [hk[k]], [s])
            for t in range(TB // SUB):
                kb.mm(pb[t][:, 0:SUB], self.ones[:], s[:, t * SUB:(t + 1) * SUB], k == 0, k == 15, [self.ones, s], [pb[t]],
                      inc=(k == 15 or t == TB // SUB - 1))
        for t in range(TB // SUB):
            sl = slice(t * SUB, (t + 1) * SUB)
            kb.act(self.rstd[:, sl], pb[t][:, 0:SUB], AF.Sqrt, [pb[t], self.epsb], [self.rstd], bias=self.epsb[:], scale=1.0 / D)
        kb.op("dve", lambda e: e.reciprocal(out=self.rstd[:], in_=self.rstd[:]), reads=[self.rstd], writes=[self.rstd])
        for k in range(16):
            tm = self.tmp[self.ti % 2]
            self.ti += 1
            for (lo, hi, w) in col_ranges(blk):
                kb.op("dve", lambda e: e.scalar_tensor_tensor(out=tm[:, lo:hi], in0=hk[k][:, lo:hi], scalar=A[:, w, k:k + 1],
                                                              in1=self.rstd[:, lo:hi], op0=ALU.mult, op1=ALU.mult),
                      reads=[hk[k], A, self.rstd], writes=[tm])
            for (lo, hi, w) in col_ranges(blk):
                kb.act(outs[k][:, lo:hi], tm[:, lo:hi], AF.Identity, [tm, mt], [outs[k]], bias=mt[:, w, idx_shift, k:k + 1])


class WStream:
    def __init__(self, kb, name, nk, nbuf=2):
        self.kb, self.nk = kb, nk
        self.st = [kb.sb(f"{name}_st{i}", [128, nk, 128]) for i in range(nbuf)]
        self.wb = [kb.sb(f"{name}_wb{i}", [128, nk, 128], BF16) for i in range(nbuf)]
        self.i = 0
        self.nbuf = nbuf

    def load(self, w_ap_2d, col0, q="sp", ceng="pool"):
        kb = self.kb
        st, wb = self.st[self.i % self.nbuf], self.wb[self.i % self.nbuf]
        self.i += 1
        src = w_ap_2d[:, col0:col0 + 128].rearrange("(k p) c -> p k c", p=128)
        kb.dma(q, st[:], src, sbuf=st, load=True)
        kb.op(ceng, lambda e: e.tensor_copy(out=wb[:], in_=st[:]), reads=[st], writes=[wb])
        return wb


def build_pre(F, kb_=None):
    kb = kb_ if kb_ is not None else KB()
    hT = kb.din("hT", [16, 128, NT])
    modT = kb.din("modT", [128, 2, 6, 16])
    normw = kb.din("normw", [128, 16])
    W = kb.din("W", [D, F])
    YT = kb.dout("YT", [F // 128, 128, NT])
    ts = TokStage(kb, 1)
    mt, A = ts.load_mod(modT, normw, 0, 1)
    hk = [kb.sb(f"h{k}", [128, TB]) for k in range(16)]
    uk = [kb.sb(f"u{k}", [128, TB], BF16) for k in range(16)]
    ws = WStream(kb, "w", 16, nbuf=3)
    yo = [kb.sb(f"yo{i}", [128, TB]) for i in range(3)]
    for blk in range(NBLK):
        cs = slice(blk * TB, (blk + 1) * TB)
        for k in range(16):
            kb.dma("act" if k % 2 else "sp", hk[k][:], hT.ap()[k][:, cs], sbuf=hk[k], load=True)
        ts.norm_mod(hk, blk, A, mt, 0, uk)
        for m in range(F // 128):
            wb = ws.load(W.ap(), m * 128, q="sp", ceng=("pool" if m % 2 else "dve"))
            y = yo[m % 3]
            for t in range(TB // SUB):
                p = ts.psum()
                for k in range(16):
                    kb.mm(p[:, 0:SUB], wb[:, k, :], uk[k][:, t * SUB:(t + 1) * SUB], k == 0, k == 15, [wb, uk[k]], [p])
                kb.act(y[:, t * SUB:(t + 1) * SUB], p[:, 0:SUB], AF.Copy, [p], [y])
            kb.dma("act", YT.ap()[m][:, cs], y[:], sbuf=y, load=False)
    kb.finish()
    return kb


def to_fm(a):
    t, f = a.shape
    return np.ascontiguousarray(a.T.reshape(f // 128, 128, t))


def from_fm(a):
    c, p, t = a.shape
    return np.ascontiguousarray(a.reshape(c * p, t).T)


def mod_layout(mod_l, b):
    m = np.stack([mod_l[b], mod_l[4]], 0).reshape(2, 6, 16, 128)
    return np.ascontiguousarray(m.transpose(3, 0, 1, 2))


def vec_layout(v):
    return np.ascontiguousarray(v.reshape(-1, 128).T)


def build_post(Fin, last, kb_=None):
    kb = kb_ if kb_ is not None else KB()
    NKI = Fin // 128
    G = 4
    hT = kb.din("hT", [16, 128, NT])
    OT = kb.din("OT", [NKI, 128, NT])
    modT = kb.din("modT", [128, 2, 6, 16])
    normw = kb.din("normw", [128, 16])
    Wo = kb.din("Wo", [Fin, D])
    W1 = kb.din("W1", [D, 4 * D])
    W2 = kb.din("W2", [4 * D, D])
    if last:
        fnw = kb.din("fnw", [128, 16])
    HO = kb.dout("HO", [16, 128, NT])
    ts = TokStage(kb, 1)
    mt, A = ts.load_mod(modT, normw, 3, 4)
    hk = [kb.sb(f"h{k}", [128, TB]) for k in range(16)]
    ob = [kb.sb(f"ob{k}", [128, TB], BF16) for k in range(max(NKI, 16 + G))]
    ost = [kb.sb(f"ost{i}", [128, TB]) for i in range(2 if NKI <= 16 else 1)]
    wso = WStream(kb, "wo", NKI, nbuf=(2 if NKI <= 16 else 1))
    ws1 = WStream(kb, "w1", 16, nbuf=2)
    w2st = [kb.sb(f"w2st{i}", [128, D]) for i in range(2 if NKI <= 16 else 1)]
    w2b = [kb.sb(f"w2b{i}", [128, D], BF16) for i in range(2 * G)]
    if last:
        fw = kb.sb("fw", [128, 16])
        kb.dma("sp", fw[:], fnw.ap(), sbuf=fw, load=True)
        ones16 = kb.sb("ones16", [128, 2, 16])
        kb.op("dve", lambda e: e.memset(ones16[:], 0.0), writes=[ones16])
        zer = kb.sb("zer", [128, 2, 6, 16])
        kb.op("dve", lambda e: e.memset(zer[:], 0.0), writes=[zer])
        fA = kb.sb("fA", [128, 2, 16])
        for w in range(2):
            kb.op("dve", lambda e: e.tensor_copy(out=fA[:, w, :], in_=fw[:]), reads=[fw], writes=[fA])
    NS = TB // SUB
    i2 = 0
    for blk in range(NBLK):
        cs = slice(blk * TB, (blk + 1) * TB)
        rngs = col_ranges(blk)
        for k in range(16):
            kb.dma("act" if k % 2 else "sp", hk[k][:], hT.ap()[k][:, cs], sbuf=hk[k], load=True)
        for k in range(NKI):
            st = ost[k % len(ost)]
            kb.dma("sp", st[:], OT.ap()[k][:, cs], sbuf=st, load=True)
            kb.op("pool", lambda e: e.tensor_copy(out=ob[k][:], in_=st[:]), reads=[st], writes=[ob[k]])
        for m in range(16):
            wb = wso.load(Wo.ap(), m * 128, q="sp", ceng="dve")
            for t in range(NS):
                p = ts.psum()
                for k in range(NKI):
                    kb.mm(p[:, 0:SUB], wb[:, k, :], ob[k][:, t * SUB:(t + 1) * SUB], k == 0, k == NKI - 1, [wb, ob[k]], [p])
                for (lo, hi, w) in rngs:
                    a, b_ = max(lo, t * SUB), min(hi, (t + 1) * SUB)
                    if a >= b_:
                        continue
                    kb.op("dve", lambda e: e.scalar_tensor_tensor(out=hk[m][:, a:b_], in0=p[:, a - t * SUB:b_ - t * SUB],
                                                                  scalar=mt[:, w, 2, m:m + 1], in1=hk[m][:, a:b_],
                                                                  op0=ALU.mult, op1=ALU.add),
                          reads=[p, mt, hk[m]], writes=[hk[m]])
        ts.norm_mod(hk, blk, A, mt, 3, ob[0:16])
        for g in range(4 * D // 128 // G):
            av = ob[16:16 + G]
            for j in range(G):
                jj = g * G + j
                wb = ws1.load(W1.ap(), jj * 128, q="sp", ceng="pool")
                for t in range(NS):
                    p = ts.psum()
                    for k in range(16):
                        kb.mm(p[:, 0:SUB], wb[:, k, :], ob[k][:, t * SUB:(t + 1) * SUB], k == 0, k == 15, [wb, ob[k]], [p])
                    tm = ts.tmp[ts.ti % 2]
                    ts.ti += 1
                    kb.act(tm[:, 0:SUB], p[:, 0:SUB], AF.Relu, [p], [tm])
                    kb.op("pool", lambda e: e.tensor_tensor(out=av[j][:, t * SUB:(t + 1) * SUB], in0=tm[:, 0:SUB], in1=tm[:, 0:SUB],
                                                            op=ALU.mult), reads=[tm], writes=[av[j]])
                st = w2st[i2 % len(w2st)]
                wv = w2b[i2 % (2 * G)]
                i2 += 1
                kb.dma("act", st[:], W2.ap()[jj * 128:(jj + 1) * 128, :], sbuf=st, load=True)
                kb.op("dve", lambda e: e.tensor_copy(out=wv[:], in_=st[:]), reads=[st], writes=[wv])
            wvs = [w2b[(i2 - G + j) % (2 * G)] for j in range(G)]
            for m in range(16):
                for t in range(NS):
                    p = ts.psum()
                    for j in range(G):
                        kb.mm(p[:, 0:SUB], wvs[j][:, m * 128:(m + 1) * 128], av[j][:, t * SUB:(t + 1) * SUB], j == 0, j == G - 1,
                              [wvs[j], av[j]], [p])
                    for (lo, hi, w) in rngs:
                        a, b_ = max(lo, t * SUB), min(hi, (t + 1) * SUB)
                        if a >= b_:
                            continue
                        kb.op("dve", lambda e: e.scalar_tensor_tensor(out=hk[m][:, a:b_], in0=p[:, a - t * SUB:b_ - t * SUB],
                                                                      scalar=mt[:, w, 5, m:m + 1], in1=hk[m][:, a:b_],
                                                                      op0=ALU.mult, op1=ALU.add),
                              reads=[p, mt, hk[m]], writes=[hk[m]])
        if last:
            fo = ts.tmp
            fouts = []
            class _O:
                pass
            _final_norm(kb, ts, hk, blk, fA, zer, HO, cs, fo)
        else:
            for k in range(16):
                kb.dma("sp", HO.ap()[k][:, cs], hk[k][:], sbuf=hk[k], load=False)
    kb.finish()
    return kb


def _final_norm(kb, ts, hk, blk, fA, zer, HO, cs, fo):
    pb = [ts.psum() for _ in range(TB // SUB)]
    for k in range(16):
        s = ts.sq[k % 2]
        kb.act(s[:], hk[k][:], AF.Square, [hk[k]], [s])
        for t in range(TB // SUB):
            kb.mm(pb[t][:, 0:SUB], ts.ones[:], s[:, t * SUB:(t + 1) * SUB], k == 0, k == 15, [ts.ones, s], [pb[t]],
                  inc=(k == 15 or t == TB // SUB - 1))
    for t in range(TB // SUB):
        sl = slice(t * SUB, (t + 1) * SUB)
        kb.act(ts.rstd[:, sl], pb[t][:, 0:SUB], AF.Sqrt, [pb[t], ts.epsb], [ts.rstd], bias=ts.epsb[:], scale=1.0 / D)
    kb.op("dve", lambda e: e.reciprocal(out=ts.rstd[:], in_=ts.rstd[:]), reads=[ts.rstd], writes=[ts.rstd])
    for k in range(16):
        o = fo[k % 2]
        kb.op("dve", lambda e: e.scalar_tensor_tensor(out=o[:], in0=hk[k][:], scalar=fA[:, 0, k:k + 1], in1=ts.rstd[:],
                                                      op0=ALU.mult, op1=ALU.mult), reads=[hk[k], fA, ts.rstd], writes=[o])
        kb.dma("sp", HO.ap()[k][:, cs], o[:], sbuf=o, load=False)


def na_tables(rpb):
    col = np.arange(64)
    cs = np.clip(col - 8, 0, 48)
    cmask = (col[None, :] >= cs[:, None]) & (col[None, :] < cs[:, None] + 16)
    dc = np.clip(col[None, :] - col[:, None], -15, 15) + 15
    t = rpb[:, :, dc]
    rpbT = np.ascontiguousarray(t.transpose(0, 3, 1, 2))
    mask = np.ascontiguousarray(np.broadcast_to(cmask.T[:, None, :], (64, 15, 64)).astype(np.float32))
    return rpbT, mask


def build_na(kb_=None):
    kb = kb_ if kb_ is not None else KB()
    H = 16
    SC = 128 ** -0.5
    QK = kb.din("QK", [32, 128, NT])
    Vr = kb.din("Vr", [64, 36, D])
    Vc = kb.din("Vc", [128, 2, D])
    RP = kb.din("RP", [H, 64, 15 * 64])
    MK = kb.din("MK", [64, 15 * 64])
    OT = kb.dout("OT", [H, 128, NT])
    ones = kb.sb("ones", [128, 128], BF16)
    kb.op("dve", lambda e: e.memset(ones[:], 1.0), writes=[ones])
    mk = kb.sb("mk", [64, 960])
    kb.dma("sp", mk[:], MK.ap(), sbuf=mk, load=True)
    qst = [kb.sb(f"qst{i}", [128, NT]) for i in range(2)]
    qb = [kb.sb(f"qb{i}", [128, NT], BF16) for i in range(2)]
    kbb = [kb.sb(f"kbb{i}", [128, NT], BF16) for i in range(2)]
    vst = kb.sb("vst", [64, 36, 128])
    vb = [kb.sb(f"vb{i}", [64, 36, 128], BF16) for i in range(2)]
    vcst = kb.sb("vcst", [128, 2, 128])
    vcb = [kb.sb(f"vcb{i}", [128, 2, 128], BF16) for i in range(2)]
    rst = kb.sb("rst", [64, 960])
    eb = [kb.sb(f"eb{i}", [64, 960]) for i in range(2)]
    E = [kb.sb(f"E{i}", [64, 512]) for i in range(2)]
    E2 = [kb.sb(f"E2{i}", [64, 512], BF16) for i in range(3)]
    Ec = [kb.sb(f"Ec{i}", [128, 512], BF16) for i in range(3)]
    rec = [kb.sb(f"rec{i}", [128, 512]) for i in range(2)]
    oo = [kb.sb(f"oo{i}", [128, 512]) for i in range(2)]
    PO = [kb.ps(f"PO{i}", [128, 512]) for i in range(2)]
    PS = [kb.ps(f"PS{i}", [128, 512]) for i in range(2)]
    PT = [kb.ps(f"PT{i}", [128, 512]) for i in range(2)]
    PC = [kb.ps(f"PC{i}", [128, 512]) for i in range(2)]
    ie = 0
    for h in range(H):
        q_, k_, v_, vc_, eb_ = qb[h % 2], kbb[h % 2], vb[h % 2], vcb[h % 2], eb[h % 2]
        kb.dma("sp", qst[0][:], QK.ap()[h], sbuf=qst[0], load=True)
        kb.op("pool", lambda e: e.tensor_copy(out=q_[:], in_=qst[0][:]), reads=[qst[0]], writes=[q_])
        kb.dma("sp", qst[1][:], QK.ap()[16 + h], sbuf=qst[1], load=True)
        kb.op("pool", lambda e: e.tensor_copy(out=k_[:], in_=qst[1][:]), reads=[qst[1]], writes=[k_])
        kb.dma("act", vst[:], Vr.ap()[:, :, h * 128:(h + 1) * 128], sbuf=vst, load=True)
        kb.op("pool", lambda e: e.tensor_copy(out=v_[:], in_=vst[:]), reads=[vst], writes=[v_])
        kb.dma("act", vcst[:], Vc.ap()[:, :, h * 128:(h + 1) * 128], sbuf=vcst, load=True)
        kb.op("pool", lambda e: e.tensor_copy(out=vc_[:], in_=vcst[:]), reads=[vcst], writes=[vc_])
        kb.dma("sp", rst[:], RP.ap()[h], sbuf=rst, load=True)
        kb.act(rst[:], rst[:], AF.Exp, [rst], [rst])
        kb.op("dve", lambda e: e.tensor_tensor(out=eb_[:], in0=rst[:], in1=mk[:], op=ALU.mult), reads=[rst, mk], writes=[eb_])
        for rg in range(5):
            po, ps_ = PO[rg % 2], PS[rg % 2]
            if rg < 4:
                for rr in range(8):
                    r = rg * 8 + rr
                    rs = min(max(r - 4, 0), 24)
                    dr0 = rs - r + 7
                    pt, pc = PT[ie % 2], PC[ie % 2]
                    e1, e2, ec = E[ie % 2], E2[ie % 3], Ec[ie % 3]
                    ie += 1
                    qs = q_[:, r * 64:(r + 1) * 64]
                    for j in range(8):
                        kb.mm(pt[0:64, j * 64:(j + 1) * 64], k_[:, (rs + j) * 64:(rs + j + 1) * 64], qs, True, True, [k_, q_], [pt],
                              inc=(j == 7))
                    for t in range(2):
                        kb.mm(pc[:, t * 64:(t + 1) * 64], k_[:, 2048 + t * 128:2048 + (t + 1) * 128], qs, True, True, [k_, q_], [pc],
                              inc=(t == 1))
                    kb.act(e1[:], pt[0:64, :], AF.Exp, [pt], [e1], scale=SC)
                    kb.op("pool" if ie % 2 else "dve",
                          lambda e: e.tensor_tensor(out=e2[:], in0=e1[:], in1=eb_[:, dr0 * 64:dr0 * 64 + 512], op=ALU.mult),
                          reads=[e1, eb_], writes=[e2])
                    kb.act(ec[:, 0:128], pc[:, 0:128], AF.Exp, [pc], [ec], scale=SC)
                    cs = slice(rr * 64, (rr + 1) * 64)
                    for j in range(8):
                        kb.mm(po[:, cs], v_[0:64, rs + j, :], e2[:, j * 64:(j + 1) * 64], j == 0, False, [v_, e2], [po])
                    for t in range(2):
                        kb.mm(po[:, cs], vc_[:, t, :], ec[:, t * 64:(t + 1) * 64], False, t == 1, [vc_, ec], [po], inc=False)
                    for j in range(8):
                        kb.mm(ps_[:, cs], ones[0:64, :], e2[:, j * 64:(j + 1) * 64], j == 0, False, [ones, e2], [ps_])
                    for t in range(2):
                        kb.mm(ps_[:, cs], ones[:, :], ec[:, t * 64:(t + 1) * 64], False, t == 1, [ones, ec], [ps_], inc=(t == 1))
                ncol, c0 = 512, rg * 512
            else:
                pc = PC[ie % 2]
                ec = Ec[ie % 3]
                ie += 1
                qs = q_[:, 2048:2304]
                for t in range(2):
                    kb.mm(pc[:, t * 256:(t + 1) * 256], k_[:, 2048 + t * 128:2048 + (t + 1) * 128], qs, True, True, [k_, q_], [pc],
                          inc=(t == 1))
                kb.act(ec[:], pc[:], AF.Exp, [pc], [ec], scale=SC)
                for t in range(2):
                    kb.mm(po[:, 0:256], vc_[:, t, :], ec[:, t * 256:(t + 1) * 256], t == 0, t == 1, [vc_, ec], [po], inc=False)
                for t in range(2):
                    kb.mm(ps_[:, 0:256], ones[:, :], ec[:, t * 256:(t + 1) * 256], t == 0, t == 1, [ones, ec], [ps_], inc=(t == 1))
                ncol, c0 = 256, 2048
            rc, o_ = rec[rg % 2], oo[rg % 2]
            kb.op("dve", lambda e: e.reciprocal(out=rc[:, 0:ncol], in_=ps_[:, 0:ncol]), reads=[ps_], writes=[rc])
            kb.op("dve", lambda e: e.tensor_tensor(out=o_[:, 0:ncol], in0=po[:, 0:ncol], in1=rc[:, 0:ncol], op=ALU.mult),
                  reads=[po, rc], writes=[o_])
            kb.dma("sp", OT.ap()[h][:, c0:c0 + ncol], o_[:, 0:ncol], sbuf=o_, load=False)
    kb.finish()
    return kb


def na_inputs(Y, rpb):
    YT = to_fm(Y)
    V = Y[:, 4096:6144]
    Vr = np.ascontiguousarray(V.reshape(36, 64, D).transpose(1, 0, 2))
    Vc = np.ascontiguousarray(V[2048:].reshape(2, 128, D).transpose(1, 0, 2))
    rpbT, mask = na_tables(rpb)
    return {"QK": np.ascontiguousarray(YT[0:32]), "Vr": Vr, "Vc": Vc,
            "RP": rpbT.reshape(16, 64, 960), "MK": mask.reshape(64, 960)}


def rope_tables():
    t = np.arange(2048)
    pos = np.stack([t // 64, t % 64], -1).astype(np.float32)
    inv = (10000.0 ** (-np.arange(32, dtype=np.float32) / 32)).astype(np.float32)
    ang = pos[:, :, None] * inv
    cos, sin = np.cos(ang).astype(np.float32), np.sin(ang).astype(np.float32)
    C = np.zeros((128, 2048), np.float32)
    S = np.zeros((128, 2048), np.float32)
    for a in range(2):
        for b in range(2):
            sl = slice(a * 64 + b * 32, a * 64 + b * 32 + 32)
            C[sl] = cos[:, a, :].T
            S[sl] = (-1.0 if b == 0 else 1.0) * sin[:, a, :].T
    P = np.zeros((128, 128), np.float32)
    for d in range(128):
        P[d ^ 32, d] = 1.0
    return C, S, P


def build_da(lambda_init, kb_=None):
    kb = kb_ if kb_ is not None else KB()
    SC = 128 ** -0.5
    QK = kb.din("QK", [32, 128, NT])
    Vt = kb.din("Vt", [128, 18, D])
    COS = kb.din("COS", [128, 2048])
    SIN = kb.din("SIN", [128, 2048])
    PM = kb.din("PM", [128, 128])
    LAM = kb.din("LAM", [1, 512])
    SW = kb.din("SW", [128, 2])
    OT = kb.dout("OT", [16, 128, NT])
    onesb = kb.sb("onesb", [128, 128], BF16)
    onesf = kb.sb("onesf", [128, 128])
    kb.op("dve", lambda e: e.memset(onesb[:], 1.0), writes=[onesb])
    kb.op("dve", lambda e: e.memset(onesf[:], 1.0), writes=[onesf])
    eps5 = kb.sb("eps5", [128, 1])
    kb.op("dve", lambda e: e.memset(eps5[:], 1e-5), writes=[eps5])
    cos = kb.sb("cos", [128, 2048]); sin = kb.sb("sin", [128, 2048]); pm = kb.sb("pm", [128, 128])
    kb.dma("sp", cos[:], COS.ap(), sbuf=cos, load=True)
    kb.dma("act", sin[:], SIN.ap(), sbuf=sin, load=True)
    kb.dma("sp", pm[:], PM.ap(), sbuf=pm, load=True)
    lam = kb.sb("lam", [1, 512]); sw = kb.sb("sw", [128, 2])
    kb.dma("sp", lam[:], LAM.ap(), sbuf=lam, load=True)
    kb.dma("sp", sw[:], SW.ap(), sbuf=sw, load=True)
    kb.op("dve", lambda e: e.tensor_scalar(out=sw[:], in0=sw[:], scalar1=1.0 - lambda_init, scalar2=None, op0=ALU.mult),
          reads=[sw], writes=[sw])
    lp = kb.sb("lp", [1, 256]); ls = kb.sb("ls", [1, 2]); lf = kb.sb("lf", [1, 2]); nl = kb.sb("nl", [128, 2])
    kb.op("dve", lambda e: e.tensor_tensor(out=lp[:, 0:128], in0=lam[:, 0:128], in1=lam[:, 128:256], op=ALU.mult), reads=[lam], writes=[lp])
    kb.op("dve", lambda e: e.tensor_tensor(out=lp[:, 128:256], in0=lam[:, 256:384], in1=lam[:, 384:512], op=ALU.mult), reads=[lam], writes=[lp])
    kb.op("dve", lambda e: e.reduce_sum(out=ls[:], in_=lp[:].rearrange("o (a b) -> o a b", a=2), axis=AX.X), reads=[lp], writes=[ls])
    kb.act(ls[:], ls[:], AF.Exp, [ls], [ls])
    for c in range(2):
        kb.op("dve", lambda e: e.scalar_tensor_tensor(out=lf[:, c:c + 1], in0=ls[:, 1:2], scalar=-lambda_init, in1=ls[:, 0:1],
                                                      op0=ALU.add, op1=ALU.subtract), reads=[ls], writes=[lf])
    PSb = [kb.ps(f"P{i}", [128, 512]) for i in range(8)]
    kb.mm(PSb[0][:, 0:2], onesf[0:1, :], lf[:], True, True, [onesf, lf], [PSb[0]])
    kb.op("dve", lambda e: e.tensor_copy(out=nl[:], in_=PSb[0][:, 0:2]), reads=[PSb[0]], writes=[nl])
    st = [kb.sb(f"st{i}", [128, NT]) for i in range(2)]
    t1 = [kb.sb(f"t1{i}", [128, 512]) for i in range(2)]
    t2 = [kb.sb(f"t2{i}", [128, 512]) for i in range(2)]
    qb = [kb.sb(f"qb{i}", [128, NT], BF16) for i in range(4)]
    kbb = [kb.sb(f"kbb{i}", [128, NT], BF16) for i in range(4)]
    vst = kb.sb("vst", [128, 18, 256])
    vb = [kb.sb(f"vb{i}", [128, 18, 256], BF16) for i in range(2)]
    E = [kb.sb(f"E{i}", [128, 512], BF16) for i in range(3)]
    rc = [kb.sb(f"rc{i}", [128, 512]) for i in range(2)]
    a0 = kb.sb("a0", [128, 512]); a1 = kb.sb("a1", [128, 512])
    oe = [kb.sb(f"oe{i}", [128, 512]) for i in range(2)]
    sq = kb.sb("sq", [128, 512]); rs_ = kb.sb("rs_", [128, 512])
    fo = [kb.sb(f"fo{i}", [128, 512]) for i in range(2)]
    ist = 0; ie = 0; it = 0

    def load_rope(dst, chunk):
        nonlocal ist, it
        s = st[ist % 2]; ist += 1
        kb.dma("sp", s[:], QK.ap()[chunk], sbuf=s, load=True)
        for b4 in range(4):
            cs = slice(b4 * 512, (b4 + 1) * 512)
            p = PSb[6 + (it % 2)]
            a, b_ = t1[it % 2], t2[it % 2]; it += 1
            kb.mm(p[:], pm[:], s[:, cs], True, True, [pm, s], [p])
            kb.op("dve", lambda e: e.tensor_tensor(out=a[:], in0=s[:, cs], in1=cos[:, cs], op=ALU.mult), reads=[s, cos], writes=[a])
            kb.op("dve", lambda e: e.tensor_tensor(out=b_[:], in0=p[:], in1=sin[:, cs], op=ALU.mult), reads=[p, sin], writes=[b_])
            kb.op("pool", lambda e: e.tensor_tensor(out=dst[:, cs], in0=a[:], in1=b_[:], op=ALU.add), reads=[a, b_], writes=[dst])
        kb.op("pool", lambda e: e.tensor_copy(out=dst[:, 2048:NT], in_=s[:, 2048:NT]), reads=[s], writes=[dst])

    for h in range(8):
        par = h % 2
        for c in range(2):
            load_rope(qb[par * 2 + c], 2 * h + c)
            load_rope(kbb[par * 2 + c], 16 + 2 * h + c)
        v_ = vb[par]
        kb.dma("act", vst[:], Vt.ap()[:, :, h * 256:(h + 1) * 256], sbuf=vst, load=True)
        kb.op("pool", lambda e: e.tensor_copy(out=v_[:], in_=vst[:]), reads=[vst], writes=[v_])
        for qblk in range(5):
            c0 = qblk * 512
            ncol = 512 if qblk < 4 else 256
            kts = list(range(18)) if qblk < 4 else [16, 17]
            for c in range(2):
                q_, k_ = qb[par * 2 + c], kbb[par * 2 + c]
                po = [PSb[c * 3], PSb[c * 3 + 1]]
                ps_ = PSb[c * 3 + 2]
                for i, kt in enumerate(kts):
                    pt = PSb[6 + (it % 2)]; it += 1
                    e_ = E[ie % 3]; ie += 1
                    kb.mm(pt[:, 0:ncol], k_[:, kt * 128:(kt + 1) * 128], q_[:, c0:c0 + ncol], True, True, [k_, q_], [pt])
                    kb.act(e_[:, 0:ncol], pt[:, 0:ncol], AF.Exp, [pt], [e_], scale=SC)
                    fst, lst = i == 0, i == len(kts) - 1
                    for e2 in range(2):
                        kb.mm(po[e2][:, 0:ncol], v_[:, kt, e2 * 128:(e2 + 1) * 128], e_[:, 0:ncol], fst, lst, [v_, e_], [po[e2]], inc=False)
                    kb.mm(ps_[:, 0:ncol], onesb[:], e_[:, 0:ncol], fst, lst, [onesb, e_], [ps_], inc=True)
            n = slice(0, ncol)
            for c in range(2):
                kb.op("dve", lambda e: e.reciprocal(out=rc[c][:, n], in_=PSb[c * 3 + 2][:, n]), reads=[PSb[c * 3 + 2]], writes=[rc[c]])
            pss = PSb[6 + (it % 2)]; it += 1
            for e2 in range(2):
                kb.op("dve", lambda e: e.tensor_tensor(out=a0[:, n], in0=PSb[e2][:, n], in1=rc[0][:, n], op=ALU.mult),
                      reads=[PSb[e2], rc[0]], writes=[a0])
                kb.op("dve", lambda e: e.tensor_tensor(out=a1[:, n], in0=PSb[3 + e2][:, n], in1=rc[1][:, n], op=ALU.mult),
                      reads=[PSb[3 + e2], rc[1]], writes=[a1])
                kb.op("dve", lambda e: e.scalar_tensor_tensor(out=oe[e2][:, n], in0=a1[:, n], scalar=nl[:, 0:1], in1=a0[:, n],
                                                               op0=ALU.mult, op1=ALU.add), reads=[a1, nl, a0], writes=[oe[e2]])
                kb.act(sq[:, n], oe[e2][:, n], AF.Square, [oe[e2]], [sq])
                kb.mm(pss[:, n], onesf[:], sq[:, n], e2 == 0, e2 == 1, [onesf, sq], [pss], inc=True)
            kb.act(rs_[:, n], pss[:, n], AF.Sqrt, [pss, eps5], [rs_], bias=eps5[:], scale=1.0 / 256)
            kb.op("dve", lambda e: e.reciprocal(out=rs_[:, n], in_=rs_[:, n]), reads=[rs_], writes=[rs_])
            for e2 in range(2):
                kb.op("dve", lambda e: e.scalar_tensor_tensor(out=fo[e2][:, n], in0=oe[e2][:, n], scalar=sw[:, e2:e2 + 1], in1=rs_[:, n],
                                                              op0=ALU.mult, op1=ALU.mult), reads=[oe[e2], sw, rs_], writes=[fo[e2]])
                kb.dma("sp", OT.ap()[2 * h + e2][:, c0:c0 + ncol], fo[e2][:, n], sbuf=fo[e2], load=False)
    kb.finish()
    return kb


def da_inputs(Y, lam, subw):
    YT = to_fm(Y)
    V = Y[:, 4096:6144]
    Vt = np.ascontiguousarray(V.reshape(18, 128, D).transpose(1, 0, 2))
    C, S, P = rope_tables()
    return {"QK": np.ascontiguousarray(YT[0:32]), "Vt": Vt, "COS": C, "SIN": S, "PM": P,
            "LAM": np.ascontiguousarray(lam.reshape(1, 512)), "SW": np.ascontiguousarray(subw.reshape(2, 128).T)}


def build_ssm_a(kb_=None):
    kb = kb_ if kb_ is not None else KB()
    XI = kb.din("XI", [49, 128, NT])
    CW = kb.din("CW", [128, 48, 5])
    CB = kb.din("CB", [128, 48])
    DTB = kb.din("DTB", [128, 1])
    ALOG = kb.din("ALOG", [128, 1])
    XO = kb.dout("XO", [48, 128, NT])
    DA_ = kb.dout("DA", [2, 128, NT])
    cw = kb.sb("cw", [128, 48, 5]); cb = kb.sb("cb", [128, 48]); dtb = kb.sb("dtb", [128, 1]); al = kb.sb("al", [128, 1])
    one = kb.sb("one", [128, 1])
    kb.op("dve", lambda e: e.memset(one[:], 1.0), writes=[one])
    for t_, d_ in ((cw, CW), (cb, CB), (dtb, DTB), (al, ALOG)):
        kb.dma("sp", t_[:], d_.ap(), sbuf=t_, load=True)
    kb.act(al[:], al[:], AF.Exp, [al], [al])
    kb.op("dve", lambda e: e.tensor_scalar(out=al[:], in0=al[:], scalar1=-1.0, scalar2=None, op0=ALU.mult), reads=[al], writes=[al])
    xin = [kb.sb(f"xin{i}", [128, NT]) for i in range(2)]
    acc = [kb.sb(f"acc{i}", [128, NT]) for i in range(2)]
    for c in range(48):
        xi, ac = xin[c % 2], acc[c % 2]
        kb.dma("sp" if c % 2 else "act", xi[:], XI.ap()[c], sbuf=xi, load=True)
        kb.op("dve", lambda e: e.tensor_scalar(out=ac[:], in0=xi[:], scalar1=cw[:, c, 2:3], scalar2=cb[:, c:c + 1],
                                               op0=ALU.mult, op1=ALU.add), reads=[xi, cw, cb], writes=[ac])
        for (lo, hi) in ((0, 2048), (2048, NT)):
            for k in (0, 1, 3, 4):
                o = k - 2
                a, b_ = lo + max(0, -o), hi - max(0, o)
                kb.op("dve", lambda e: e.scalar_tensor_tensor(out=ac[:, a:b_], in0=xi[:, a + o:b_ + o], scalar=cw[:, c, k:k + 1],
                                                              in1=ac[:, a:b_], op0=ALU.mult, op1=ALU.add),
                      reads=[xi, cw, ac], writes=[ac])
        kb.act(ac[:], ac[:], AF.Silu, [ac], [ac])
        kb.dma("sp", XO.ap()[c], ac[:], sbuf=ac, load=False)
    xi, ac = xin[0], acc[0]
    kb.dma("sp", xi[:], XI.ap()[48], sbuf=xi, load=True)
    kb.act(xi[:], xi[:], AF.Exp, [xi, dtb], [xi], bias=dtb[:])
    kb.act(xi[:], xi[:], AF.Ln, [xi, one], [xi], bias=one[:])
    kb.dma("sp", DA_.ap()[0], xi[:], sbuf=xi, load=False)
    kb.op("dve", lambda e: e.tensor_scalar(out=ac[:], in0=xi[:], scalar1=al[:, 0:1], scalar2=None, op0=ALU.mult), reads=[xi, al], writes=[ac])
    kb.dma("sp", DA_.ap()[1], ac[:], sbuf=ac, load=False)
    kb.finish()
    return kb


ORD_F = [16, 17] + list(range(16))
ORD_B = [17, 16] + list(range(15, -1, -1))


def ssm_consts():
    i = np.arange(128)
    tri_f = (i[:, None] <= i[None, :]).astype(np.float32)
    tri_b = (i[:, None] >= i[None, :]).astype(np.float32)
    t = np.arange(512)
    mk = np.zeros((8, 128, 512), np.float32)
    for q in range(4):
        mk[q] = ((t[None, :] - 128 * q) >= i[:, None])
        mk[4 + q] = ((t[None, :] - 128 * q) <= i[:, None])
    return tri_f, tri_b, np.eye(128, dtype=np.float32), np.ascontiguousarray(mk.transpose(1, 0, 2))


def build_ssm_b(kb_=None):
    kb = kb_ if kb_ is not None else KB()
    XT_ = kb.din("XTOK", [128, 18, 4096])
    BT = kb.din("BT", [8, 128, NT]); CT = kb.din("CT", [8, 128, NT])
    DTT = kb.din("DTT", [128, 18, 128]); ATT = kb.din("ATT", [128, 18, 128])
    XH = kb.din("XH", [64, 64, NT]); ZH = kb.din("ZH", [64, 64, NT])
    NWH = kb.din("NWH", [64, 64]); DSK = kb.din("DSK", [64, 64])
    TF = kb.din("TF", [128, 128]); TBm = kb.din("TB", [128, 128]); ID = kb.din("ID", [128, 128]); MK = kb.din("MK", [128, 8, 512])
    OT = kb.dout("OT", [64, 64, NT])
    onesf = kb.sb("onesf", [128, 128]); kb.op("dve", lambda e: e.memset(onesf[:], 1.0), writes=[onesf])
    eps = kb.sb("eps", [128, 1]); kb.op("dve", lambda e: e.memset(eps[:], EPS), writes=[eps])
    tf = kb.sb("tf", [128, 128]); tb_ = kb.sb("tb_", [128, 128]); idt = kb.sb("idt", [128, 128]); mk = kb.sb("mk", [128, 8, 512])
    dtt = kb.sb("dtt", [128, 18, 128]); att = kb.sb("att", [128, 18, 128]); nwh = kb.sb("nwh", [64, 64]); dsk = kb.sb("dsk", [64, 64])
    for t_, d_ in ((tf, TF), (tb_, TBm), (idt, ID), (mk, MK), (dtt, DTT), (att, ATT), (nwh, NWH), (dsk, DSK)):
        kb.dma("sp", t_[:], d_.ap(), sbuf=t_, load=True)
    P = [kb.ps(f"P{i}", [128, 512]) for i in range(8)]
    cum = [kb.sb(f"cum{d}", [128, 18, 128]) for d in range(2)]
    cumT = [kb.sb(f"cumT{d}", [128, NT]) for d in range(2)]
    ip = 0
    for d, (order, tri) in enumerate(((ORD_F, tf), (ORD_B, tb_))):
        for oi, n in enumerate(order):
            p = P[ip % 2]; ip += 1
            kb.mm(p[:, 0:128], tri[:], att[:, n, :], True, oi == 0, [tri, att], [p], inc=(oi == 0))
            for mi, m in enumerate(order[:oi]):
                kb.mm(p[:, 0:128], onesf[:], att[:, m, :], False, mi == oi - 1, [onesf, att], [p], inc=(mi == oi - 1))
            kb.op("dve", lambda e: e.tensor_copy(out=cum[d][:, n, :], in_=p[:, 0:128]), reads=[p], writes=[cum[d]])
        for n in range(18):
            p = P[ip % 2]; ip += 1
            kb.op("pe", lambda e: e.transpose(out=p[:, 0:128], in_=cum[d][:, n, :], identity=idt[:]), reads=[cum[d], idt], writes=[p])
            kb.act(cumT[d][:, n * 128:(n + 1) * 128], p[:, 0:128], AF.Copy, [p], [cumT[d]])
    xst = kb.sb("xst", [128, 18, 512])
    xdt = [kb.sb(f"xdt{d}", [128, 18, 512], BF16) for d in range(2)]
    bst = kb.sb("bst", [128, NT])
    bb = kb.sb("bb", [128, NT], BF16); cbf = kb.sb("cbf", [128, NT], BF16)
    sel = [kb.sb(f"sel{i}", [128, 128]) for i in range(2)]
    acb = [kb.sb(f"acb{i}", [128, 512]) for i in range(2)]
    Dt = [kb.sb(f"Dt{i}", [128, 512]) for i in range(2)]
    Mt = [kb.sb(f"Mt{i}", [128, 512], BF16) for i in range(3)]
    yz = kb.sb("yz", [64, 8, 512])
    xh = [kb.sb(f"xh{i}", [64, 512]) for i in range(2)]
    zh = [kb.sb(f"zh{i}", [64, 512]) for i in range(2)]
    sqs = kb.sb("sqs", [64, 512]); rst = kb.sb("rst", [64, 512])
    fo = [kb.sb(f"fo{i}", [64, 512]) for i in range(2)]
    isel = 0; iw = 0; ih = 0
    for g in range(8):
        kb.dma("sp", xst[:], XT_.ap()[:, :, g * 512:(g + 1) * 512], sbuf=xst, load=True)
        for d in range(2):
            for n in range(18):
                for r in range(8):
                    col = d * 64 + g * 8 + r
                    kb.op("pool" if (n + r) % 2 else "dve",
                          lambda e: e.tensor_scalar(out=xdt[d][:, n, r * 64:(r + 1) * 64], in0=xst[:, n, r * 64:(r + 1) * 64],
                                                    scalar1=dtt[:, n, col:col + 1], scalar2=None, op0=ALU.mult),
                          reads=[xst, dtt], writes=[xdt[d]])
        kb.dma("sp", bst[:], BT.ap()[g], sbuf=bst, load=True)
        kb.op("pool", lambda e: e.tensor_copy(out=bb[:], in_=bst[:]), reads=[bst], writes=[bb])
        kb.dma("sp", bst[:], CT.ap()[g], sbuf=bst, load=True)
        kb.op("pool", lambda e: e.tensor_copy(out=cbf[:], in_=bst[:]), reads=[bst], writes=[cbf])
        for tb in range(5):
            c0 = tb * 512
            ncol = 512 if tb < 4 else 256
            n_ = slice(0, ncol)
            ttiles = list(range(4 * tb, 4 * tb + 4)) if tb < 4 else [16, 17]
            for r in range(8):
                hd = g * 8 + r
                py = P[2 + (r % 2)]
                first = True
                for d in range(2):
                    col = d * 64 + hd
                    s_ = sel[isel % 2]; a_ = acb[isel % 2]; isel += 1
                    kb.op("dve", lambda e: e.tensor_scalar(out=s_[:], in0=onesf[:], scalar1=idt[:, col:col + 1], scalar2=None, op0=ALU.mult),
                          reads=[onesf, idt], writes=[s_])
                    pa = P[4 + (isel % 2)]
                    kb.mm(pa[:, n_], s_[:], cumT[d][:, c0:c0 + ncol], True, True, [s_, cumT[d]], [pa])
                    kb.act(a_[:, n_], pa[:, n_], AF.Copy, [pa], [a_])
                    if tb < 4:
                        full = [16, 17] + (list(range(0, 4 * tb)) if d == 0 else list(range(4 * tb + 4, 16)))
                    else:
                        full = []
                    pairs = [(s, None) for s in full] + [(s, qi) for qi, s in enumerate(ttiles)]
                    for pi_, (s, qi) in enumerate(pairs):
                        pc = P[6 + (iw % 2)]
                        D_ = Dt[iw % 2]; M_ = Mt[iw % 3]; iw += 1
                        kb.mm(pc[:, n_], bb[:, s * 128:(s + 1) * 128], cbf[:, c0:c0 + ncol], True, True, [bb, cbf], [pc])
                        kb.op("dve", lambda e: e.tensor_scalar(out=D_[:, n_], in0=a_[:, n_], scalar1=cum[d][:, s, col:col + 1], scalar2=0.0,
                                                               op0=ALU.subtract, op1=ALU.min), reads=[a_, cum[d]], writes=[D_])
                        kb.act(D_[:, n_], D_[:, n_], AF.Exp, [D_], [D_])
                        if qi is None:
                            kb.op("dve", lambda e: e.tensor_tensor(out=M_[:, n_], in0=pc[:, n_], in1=D_[:, n_], op=ALU.mult),
                                  reads=[pc, D_], writes=[M_])
                        else:
                            kb.op("pool", lambda e: e.tensor_tensor(out=D_[:, n_], in0=D_[:, n_], in1=mk[:, d * 4 + qi, n_], op=ALU.mult),
                                  reads=[D_, mk], writes=[D_])
                            kb.op("dve", lambda e: e.tensor_tensor(out=M_[:, n_], in0=pc[:, n_], in1=D_[:, n_], op=ALU.mult),
                                  reads=[pc, D_], writes=[M_])
                        lastp = (d == 1 and pi_ == len(pairs) - 1)
                        kb.mm(py[0:64, n_], xdt[d][:, s, r * 64:(r + 1) * 64], M_[:, n_], first, lastp, [xdt[d], M_], [py], inc=True)
                        first = False
                x_, z_ = xh[ih % 2], zh[ih % 2]; ih += 1
                kb.dma("sp", x_[:, n_], XH.ap()[hd][:, c0:c0 + ncol], sbuf=x_, load=True)
                kb.dma("act", z_[:, n_], ZH.ap()[hd][:, c0:c0 + ncol], sbuf=z_, load=True)
                kb.act(z_[:, n_], z_[:, n_], AF.Silu, [z_], [z_])
                kb.op("dve", lambda e: e.scalar_tensor_tensor(out=x_[:, n_], in0=x_[:, n_], scalar=dsk[:, hd:hd + 1], in1=py[0:64, n_],
                                                              op0=ALU.mult, op1=ALU.add), reads=[x_, dsk, py], writes=[x_])
                kb.op("pool", lambda e: e.tensor_tensor(out=yz[:, r, n_], in0=x_[:, n_], in1=z_[:, n_], op=ALU.mult),
                      reads=[x_, z_], writes=[yz])
            pn = P[4]
            for r in range(8):
                kb.act(sqs[:, n_], yz[:, r, n_], AF.Square, [yz], [sqs])
                kb.mm(pn[0:64, n_], onesf[0:64, 0:64], sqs[:, n_], r == 0, r == 7, [onesf, sqs], [pn], inc=True)
            kb.act(rst[:, n_], pn[0:64, n_], AF.Sqrt, [pn, eps], [rst], bias=eps[0:64, :], scale=1.0 / 512)
            kb.op("dve", lambda e: e.reciprocal(out=rst[:, n_], in_=rst[:, n_]), reads=[rst], writes=[rst])
            for r in range(8):
                hd = g * 8 + r
                f_ = fo[r % 2]
                kb.op("dve", lambda e: e.scalar_tensor_tensor(out=f_[:, n_], in0=yz[:, r, n_], scalar=nwh[:, hd:hd + 1], in1=rst[:, n_],
                                                              op0=ALU.mult, op1=ALU.mult), reads=[yz, nwh, rst], writes=[f_])
                kb.dma("sp", OT.ap()[hd][:, c0:c0 + ncol], f_[:, n_], sbuf=f_, load=False)
    kb.finish()
    return kb


class View:
    def __init__(self, ap):
        self._ap = ap

    def ap(self):
        return self._ap


def emit_xpose(kb, ident_d, get_in, put_out, nblk_p, nblk_f, pin=128):
    idt = kb.sb("idt", [128, 128])
    kb.dma("sp", idt[:], ident_d.ap(), sbuf=idt, load=True)
    W = nblk_f * 128
    src = [kb.sb(f"src{i}", [128, W]) for i in range(2)]
    ps = [kb.ps(f"tp{i}", [128, 512]) for i in range(4)]
    GB = 4
    dst = [kb.sb(f"dst{i}", [128, GB, nblk_p * pin]) for i in range(1)]
    ip = 0
    for b0 in range(0, nblk_f, GB):
        nb = min(GB, nblk_f - b0)
        d = dst[0]
        for a in range(nblk_p):
            s_ = src[a % 2]
            kb.dma("sp" if a % 2 else "act", s_[0:pin, 0:nb * 128], get_in(a)[:, b0 * 128:(b0 + nb) * 128], sbuf=s_, load=True)
            p = ps[ip % 4]; ip += 1
            for bb in range(nb):
                kb.op("pe", lambda e: e.transpose(out=p[:, bb * pin:(bb + 1) * pin], in_=s_[0:pin, bb * 128:(bb + 1) * 128],
                                                  identity=idt[0:pin, 0:pin]), reads=[s_, idt], writes=[p], inc=(bb == nb - 1))
            for bb in range(nb):
                kb.act(d[:, bb, a * pin:(a + 1) * pin], p[:, bb * pin:(bb + 1) * pin], AF.Copy, [p], [d])
        for bb in range(nb):
            kb.dma("sp", put_out(b0 + bb), d[:, bb, :], sbuf=d, load=False)
    kb.finish()


def emit_ada_fm(kb, condT_d, w_d, b_d, ident_d, MOD):
    ct = kb.sb("ct", [128, 16, 2]); cs = kb.sb("cs", [128, 16, 2])
    idt = kb.sb("idt", [128, 128])
    kb.dma("sp", idt[:], ident_d.ap(), sbuf=idt, load=True)
    kb.dma("sp", ct[:], condT_d.ap(), sbuf=ct, load=True)
    kb.act(cs[:], ct[:], AF.Silu, [ct], [cs])
    wst = [kb.sb(f"wst{i}", [128, 16, 128]) for i in range(3)]
    bst = kb.sb("bst", [96, 128]); bT = kb.sb("bT", [128, 96])
    msb = [kb.sb(f"msb{i}", [128, 2, 96]) for i in range(2)]
    ps = [kb.ps(f"ap{i}", [128, 512]) for i in range(4)]
    i = 0
    for l in range(4):
        kb.dma("act", bst[:], b_d.ap()[l].rearrange("(m p) -> m p", p=128), sbuf=bst, load=True)
        pb = ps[3]
        kb.op("pe", lambda e: e.transpose(out=pb[:, 0:96], in_=bst[:], identity=idt[0:96, 0:96]), reads=[bst, idt], writes=[pb])
        kb.op("dve", lambda e: e.tensor_copy(out=bT[:], in_=pb[:, 0:96]), reads=[pb], writes=[bT])
        ms = msb[l % 2]
        for m in range(96):
            st = wst[i % 3]
            p = ps[i % 3]; i += 1
            kb.dma("sp" if m % 2 else "act", st[:], w_d.ap()[l][:, m * 128:(m + 1) * 128].rearrange("(k p) c -> p k c", p=128),
                   sbuf=st, load=True)
            for k in range(16):
                kb.mm(p[:, 0:2], st[:, k, :], cs[:, k, :], k == 0, k == 15, [st, cs], [p])
            kb.op("dve", lambda e: e.tensor_scalar(out=ms[:, :, m], in0=p[:, 0:2], scalar1=bT[:, m:m + 1], scalar2=None, op0=ALU.add),
                  reads=[p, bT], writes=[ms])
        kb.dma("sp", MOD.ap()[l], ms[:], sbuf=ms, load=False)
    kb.finish()


def build_fused():
    kb = KB()
    kb.fused = True
    nc = kb.nc
    ext = lambda n, sh: nc.dram_tensor(n, list(sh), F32, kind="ExternalInput")
    XTM = ext("XTM", [NT, D]); CONDT = ext("CONDT", [128, 16, 2]); ADAW = ext("ADAW", [4, D, 6 * D]); ADAB = ext("ADAB", [4, 6 * D])
    NMIX = ext("NMIX", [4, 128, 16]); NMLP = ext("NMLP", [4, 128, 16]); FNW = ext("FNW", [128, 16])
    W1 = ext("W1", [4, D, 4 * D]); W2 = ext("W2", [4, 4 * D, D])
    NAQKV = ext("NAQKV", [2, D, 6144]); NAWO = ext("NAWO", [2, D, D]); RP = ext("RP", [2, 16, 64, 960]); MKN = ext("MKN", [64, 960])
    SSWIN = ext("SSWIN", [D, 10368]); SSWO = ext("SSWO", [4096, D]); CW = ext("CW", [128, 48, 5]); CB = ext("CB", [128, 48])
    DTB = ext("DTB", [128, 1]); ALOG = ext("ALOG", [128, 1]); NWH = ext("NWH", [64, 64]); DSK = ext("DSK", [64, 64])
    TF = ext("TF", [128, 128]); TBm = ext("TBM", [128, 128]); ID = ext("ID", [128, 128]); MKS = ext("MKS", [128, 8, 512])
    DAQKV = ext("DAQKV", [D, 6144]); DAWO = ext("DAWO", [D, D]); COS = ext("COS", [128, 2048]); SIN = ext("SIN", [128, 2048])
    PM = ext("PM", [128, 128]); LAM = ext("LAM", [1, 512]); SW = ext("SW", [128, 2])
    OUT = nc.dram_tensor("OUT", [2048, D], F32, kind="ExternalOutput")
    hA = kb.scratch("hA", [16, 128, NT]); hB = kb.scratch("hB", [16, 128, NT])
    YT = kb.scratch("YT", [81, 128, NT]); VTM = kb.scratch("VTM", [NT, 4096]); XO = kb.scratch("XO", [48, 128, NT])
    DAo = kb.scratch("DAo", [2, 128, NT]); DTM = kb.scratch("DTM", [NT, 256]); OTs = kb.scratch("OTs", [32, 128, NT])
    MOD = kb.scratch("MOD", [4, 128, 2, 96])

    def stage(prefix, fn):
        kb.push(prefix)
        fn()
        kb.pop()

    stage("ti_", lambda: emit_xpose(kb, ID, lambda a: XTM.ap()[a * 128:(a + 1) * 128, :],
                                    lambda b: hA.ap()[b], 18, 16))
    stage("ad_", lambda: emit_ada_fm(kb, CONDT, ADAW, ADAB, ID, MOD))
    hcur, hnxt = hA, hB
    import math
    for i in range(4):
        last = i == 3
        mixer, j = i % 3, i // 3
        modv = View(MOD.ap()[i].rearrange("p w (a c) -> p w a c", a=6))
        Wt = (View(NAQKV.ap()[j]), SSWIN, DAQKV)[mixer]
        F = (6144, 10368, 6144)[mixer]
        ytv = View(YT.ap()[0:F // 128])
        kb.io = {"hT": hcur, "modT": modv, "normw": View(NMIX.ap()[i]), "W": Wt, "YT": ytv}
        stage(f"pr{i}_", lambda: build_pre(F, kb_=kb))
        if mixer in (0, 2):
            stage(f"xv{i}_", lambda: emit_xpose(kb, ID, lambda a: YT.ap()[32 + a], lambda b: VTM.ap()[b * 128:(b + 1) * 128, 0:2048],
                                                16, 18))
            vt = VTM.ap()[:, 0:2048]
        if mixer == 0:
            kb.io = {"QK": View(YT.ap()[0:32]), "Vr": View(vt.rearrange("(r c) f -> c r f", c=64)),
                     "Vc": View(vt[2048:NT].rearrange("(t p) f -> p t f", p=128)), "RP": View(RP.ap()[j]), "MK": MKN,
                     "OT": View(OTs.ap()[0:16])}
            stage(f"na{i}_", lambda: build_na(kb_=kb))
            Wo, Fin = View(NAWO.ap()[j]), 2048
        elif mixer == 2:
            kb.io = {"QK": View(YT.ap()[0:32]), "Vt": View(vt.rearrange("(t p) f -> p t f", p=128)), "COS": COS, "SIN": SIN, "PM": PM,
                     "LAM": LAM, "SW": SW, "OT": View(OTs.ap()[0:16])}
            li = 0.8 - 0.6 * math.exp(-0.3 * i)
            stage(f"da{i}_", lambda: build_da(li, kb_=kb))
            Wo, Fin = DAWO, 2048
        else:
            kb.io = {"XI": View(YT.ap()[32:81]), "CW": CW, "CB": CB, "DTB": DTB, "ALOG": ALOG, "XO": XO, "DA": DAo}
            stage(f"sa{i}_", lambda: build_ssm_a(kb_=kb))
            stage(f"xx{i}_", lambda: emit_xpose(kb, ID, lambda a: XO.ap()[a], lambda b: VTM.ap()[b * 128:(b + 1) * 128, :], 32, 18))
            stage(f"xd{i}_", lambda: emit_xpose(kb, ID, lambda a: DAo.ap()[a], lambda b: DTM.ap()[b * 128:(b + 1) * 128, :], 2, 18))
            tokv = lambda ap_: View(ap_.rearrange("(t p) f -> p t f", p=128))
            kb.io = {"XTOK": tokv(VTM.ap()), "BT": View(XO.ap()[32:40]), "CT": View(XO.ap()[40:48]),
                     "DTT": tokv(DTM.ap()[:, 0:128]), "ATT": tokv(DTM.ap()[:, 128:256]),
                     "XH": View(XO.ap()[0:32].rearrange("c (two p) t -> (c two) p t", two=2)),
                     "ZH": View(YT.ap()[0:32].rearrange("c (two p) t -> (c two) p t", two=2)),
                     "NWH": NWH, "DSK": DSK, "TF": TF, "TB": TBm, "ID": ID, "MK": MKS,
                     "OT": View(OTs.ap().rearrange("c (two p) t -> (c two) p t", two=2))}
            stage(f"sb{i}_", lambda: build_ssm_b(kb_=kb))
            Wo, Fin = SSWO, 4096
        kb.io = {"hT": hcur, "OT": View(OTs.ap()[0:Fin // 128]), "modT": modv, "normw": View(NMLP.ap()[i]), "Wo": Wo,
                 "W1": View(W1.ap()[i]), "W2": View(W2.ap()[i]), "fnw": FNW, "HO": hnxt}
        stage(f"po{i}_", lambda: build_post(Fin, last, kb_=kb))
        hcur, hnxt = hnxt, hcur
    kb.io = {}
    stage("to_", lambda: emit_xpose(kb, ID, lambda a: hcur.ap()[a][:, 0:2048], lambda b: OUT.ap()[b * 128:(b + 1) * 128, :], 16, 16))
    kb.fused = False
    kb.finish()
    return kb


def fused_inputs(inp, b):
    import math
    cond = np.stack([inp["c"][b], inp["c_ctx"]], 0)
    condT = np.ascontiguousarray(cond.T.reshape(16, 128, 2).transpose(1, 0, 2))
    tri_f, tri_b, ident, masks = ssm_consts()
    C, S, P = rope_tables()
    rp = np.stack([na_tables(inp["na_rpb"][j])[0].reshape(16, 64, 960) for j in range(2)], 0)
    mkn = na_tables(inp["na_rpb"][0])[1].reshape(64, 960)
    vl = lambda a: np.stack([vec_layout(a[i]) for i in range(a.shape[0])], 0)
    return {
        "XTM": np.ascontiguousarray(np.concatenate([inp["x"][b], inp["ctx"][b]], 0)), "CONDT": condT,
        "ADAW": inp["ada_w"], "ADAB": inp["ada_b"], "NMIX": vl(inp["norm_mix_w"]), "NMLP": vl(inp["norm_mlp_w"]),
        "FNW": vec_layout(inp["final_norm_w"]), "W1": inp["mlp_w1"], "W2": inp["mlp_w2"],
        "NAQKV": inp["na_w_qkv"], "NAWO": inp["na_w_o"], "RP": np.ascontiguousarray(rp), "MKN": np.ascontiguousarray(mkn),
        "SSWIN": inp["ssm_w_in"][0], "SSWO": inp["ssm_w_out"][0],
        "CW": np.ascontiguousarray(inp["ssm_conv_w"][0].reshape(5, 48, 128).transpose(2, 1, 0)),
        "CB": np.ascontiguousarray(inp["ssm_conv_b"][0].reshape(48, 128).T),
        "DTB": np.ascontiguousarray(inp["ssm_dt_bias"][0].reshape(128, 1)), "ALOG": np.ascontiguousarray(inp["ssm_a_log"][0].reshape(128, 1)),
        "NWH": np.ascontiguousarray(inp["ssm_norm_w"][0].reshape(64, 64).T),
        "DSK": np.ascontiguousarray(np.broadcast_to(inp["ssm_d"][0][None, :], (64, 64))),
        "TF": tri_f, "TBM": tri_b, "ID": ident, "MKS": masks,
        "DAQKV": inp["da_w_qkv"][0], "DAWO": inp["da_w_o"][0], "COS": C, "SIN": S, "PM": P,
        "LAM": np.ascontiguousarray(inp["da_lambda"][0].reshape(1, 512)), "SW": np.ascontiguousarray(inp["da_subln_w"][0].reshape(2, 128).T),
    }


def kernel(**inputs):
    inp = {k: np.asarray(v) for k, v in inputs.items()}
    NB = 4
    kb = build_fused()
    res = run(kb, [fused_inputs(inp, b) for b in range(NB)], n=NB)
    return np.stack([res.results[b]["OUT"] for b in range(NB)], 0).astype(np.float32)
```

```python
import numpy as np
from contextlib import ExitStack
import concourse.bass as bass
import concourse.mybir as mybir
from concourse.bass_utils import run_bass_kernel_spmd

F32 = mybir.dt.float32
F32R = mybir.dt.float32r
BF16 = mybir.dt.bfloat16
AF = mybir.ActivationFunctionType
ALU = mybir.AluOpType
AX = mybir.AxisListType


class Buf:
    def __init__(self, kb, t, name):
        self.kb, self.t, self.name = kb, t, name
        self.last_w = None
        self.readers = []
        self.dsem = None
        self.dcnt = 0

    def __getitem__(self, idx):
        return self.t[idx]


class KB:
    def __init__(self):
        self.nc = bass.Bass("TRN2", target_bir_lowering=False)
        self.es = ExitStack()
        nc = self.nc
        self.E = {"pe": nc.tensor, "act": nc.scalar, "dve": nc.vector, "pool": nc.gpsimd, "sp": nc.sync}
        self.sems = {}
        self.cnt = {}
        for e in self.E:
            self.sems[e] = self.es.enter_context(nc.semaphore("s_" + e))
            self.cnt[e] = 0
        self.seen = {e: {} for e in self.E}
        self.nbuf = 0
        self.out_deps = []
        self.stack = [self.es]
        self.io = {}
        self.prefix = ""
        self.dpool = []
        self.scope_bufs = []
        self.fused = False
        self.bar = self.es.enter_context(nc.semaphore("s_bar"))
        self.barcnt = 0
        self.nsem = 0

    def push(self, prefix):
        self.prefix = prefix
        self.stack.append(ExitStack())
        self.scope_bufs = []

    def pop(self):
        for b in self.scope_bufs:
            if b.dsem is not None:
                self.dpool.append((b.dsem, b.dcnt))
                b.dsem = None
        self.scope_bufs = []
        self.stack.pop().close()
        self.prefix = ""

    def barrier(self):
        for d in self.out_deps:
            self._wait("sp", d)
        self.out_deps = []
        for e in self.E:
            if e != "sp" and self.cnt[e] > 0:
                self._wait("sp", (e, self.cnt[e]))
        self.nc.sync.sem_inc(self.bar, 1)
        self.barcnt += 1
        for e in self.E:
            if e != "sp":
                self.E[e].wait_ge(self.bar, self.barcnt)

    def sb(self, name, shape, dtype=F32):
        t = self.stack[-1].enter_context(self.nc.sbuf_tensor(self.prefix + name, list(shape), dtype))
        b = Buf(self, t, self.prefix + name)
        self.scope_bufs.append(b)
        return b

    def ps(self, name, shape, dtype=F32):
        t = self.stack[-1].enter_context(self.nc.psum_tensor(self.prefix + name, list(shape), dtype))
        b = Buf(self, t, self.prefix + name)
        self.scope_bufs.append(b)
        return b

    def din(self, name, shape, dtype=F32):
        if name in self.io:
            return self.io[name]
        return self.nc.dram_tensor(name, list(shape), dtype, kind="ExternalInput")

    def dout(self, name, shape, dtype=F32):
        if name in self.io:
            return self.io[name]
        return self.nc.dram_tensor(name, list(shape), dtype, kind="ExternalOutput")

    def scratch(self, name, shape, dtype=F32):
        return self.nc.dram_tensor(name, list(shape), dtype)

    def _semobj(self, key):
        return self.sems[key] if isinstance(key, str) else key

    def _wait(self, eng, dep):
        key, val = dep
        kid = key if isinstance(key, str) else id(key)
        if not isinstance(key, str) and not isinstance(key, Buf):
            kid = id(key)
        if isinstance(key, str):
            assert val <= self.cnt[key], f"wait on unissued inc {key} {val}>{self.cnt[key]}"
        if self.seen[eng].get(kid, 0) >= val:
            return
        self.E[eng].wait_ge(self._semobj(key), val)
        self.seen[eng][kid] = val

    def _sync(self, eng, reads, writes):
        deps = []
        for b in reads:
            if b.last_w is not None:
                deps.append(b.last_w)
        for b in writes:
            if b.last_w is not None:
                deps.append(b.last_w)
            deps.extend(b.readers)
        for d in deps:
            if eng == "pe" and d[0] == "pe":
                continue
            self._wait(eng, d)

    def op(self, eng, fn, reads=(), writes=(), inc=True):
        self._sync(eng, reads, writes)
        inst = fn(self.E[eng])
        if inc:
            inst.then_inc(self.sems[eng], 1)
            self.cnt[eng] += 1
            val = self.cnt[eng]
        else:
            val = self.cnt[eng] + 1
        dep = (eng, val)
        for b in reads:
            b.readers.append(dep)
        for b in writes:
            b.last_w = dep
            b.readers = []
        return inst

    def dram(self, name, shape, dtype=F32):
        t = self.nc.dram_tensor(name, list(shape), dtype)
        return Buf(self, t, name)

    def _dsem(self, b):
        if b.dsem is None:
            if self.dpool:
                b.dsem, b.dcnt = self.dpool.pop()
            else:
                self.nsem += 1
                b.dsem = self.es.enter_context(self.nc.semaphore(f"d_{self.nsem}"))
                b.dcnt = 0
        return b.dsem

    def dma(self, q, out, in_, sbuf=None, load=None, reads=(), writes=(), is_out=False):
        reads, writes = list(reads), list(writes)
        if sbuf is not None:
            if load:
                writes = [sbuf] + writes
            else:
                reads = [sbuf] + reads
                is_out = True
        owner = writes[0] if writes else reads[0]
        self._dsem(owner)
        self._sync(q, reads, writes)
        inst = self.E[q].dma_start(out=out, in_=in_)
        inst.then_inc(owner.dsem, 16)
        owner.dcnt += 16
        dep = (owner.dsem, owner.dcnt)
        for b in reads:
            b.readers.append(dep)
        for b in writes:
            b.last_w = dep
            b.readers = []
        if is_out:
            self.out_deps.append(dep)
        return inst

    def allgather(self, out_buf, in_buf, groups):
        self._dsem(out_buf)
        self._sync("pool", [in_buf], [out_buf])
        inst = self.nc.gpsimd.collective_compute("AllGather", ALU.bypass, replica_groups=groups,
                                                 ins=[in_buf[:]], outs=[out_buf[:]])
        inst.then_inc(out_buf.dsem, 16)
        out_buf.dcnt += 16
        dep = (out_buf.dsem, out_buf.dcnt)
        in_buf.readers.append(dep)
        out_buf.last_w = dep
        out_buf.readers = []
        return inst

    def finish(self, eng="sp"):
        if self.fused:
            return self.barrier()
        for d in self.out_deps:
            self._wait(eng, d)
        for e in self.E:
            if e != eng and self.cnt[e] > 0:
                self._wait(eng, (e, self.cnt[e]))

    def mm(self, out_ap, lhsT, rhs, start, stop, reads, writes, inc=None):
        return self.op("pe", lambda e: e.matmul(out_ap, lhsT, rhs, start=start, stop=stop),
                       reads=reads, writes=writes, inc=stop if inc is None else inc)

    def act(self, out_ap, in_ap, func, reads, writes, bias=None, scale=1.0, eng="act", **kw):
        def f(e):
            kws = dict(kw)
            if bias is not None:
                kws["bias"] = bias
            return e.activation(out=out_ap, in_=in_ap, func=func, scale=scale, **kws)
        return self.op(eng, f, reads=reads, writes=writes)


def run(kb, in_maps, n=8, trace=False):
    res = run_bass_kernel_spmd(kb.nc, in_maps, core_ids=list(range(n)), trace=trace)
    return res


D = 2048
NCORES = 8


def build_ada(kb_=None):
    kb = kb_ if kb_ is not None else KB()
    CW = 1536
    condT = kb.din("condT", [128, 16, 5])
    w = kb.din("w", [4, D, CW])
    b = kb.din("b", [4, CW])
    o = kb.dout("o", [4, 5, CW])
    ct = kb.sb("ct", [128, 16, 5])
    cs = kb.sb("cs", [128, 16, 5])
    ones = kb.sb("ones", [1, 8])
    bt = kb.sb("bt", [1, 4, CW])
    wt = [kb.sb(f"wt{i}", [128, CW]) for i in range(4)]
    res = [kb.sb(f"res{i}", [5, CW]) for i in range(2)]
    pss = [kb.ps(f"ps{i}", [128, 512]) for i in range(6)]
    kb.dma("sp", ct[:], condT.ap(), ct, True)
    kb.dma("sp", bt[:], b.ap().rearrange("(o l) c -> o l c", o=1), bt, True)
    kb.op("dve", lambda e: e.memset(ones[:], 1.0), writes=[ones])
    kb.act(cs[:], ct[:], AF.Silu, [ct], [cs])
    i = 0
    for l in range(4):
        pb = pss[(l % 2) * 3:(l % 2) * 3 + 3]
        for t in range(3):
            kb.mm(pb[t][0:5, :], ones[0:1, 0:5], bt[0:1, l, t * 512:(t + 1) * 512], True, False, [ones, bt], [pb[t]])
        for k in range(16):
            wb_ = wt[i % 4]
            i += 1
            kb.dma("sp" if k % 2 == 0 else "act", wb_[:], w.ap()[l, k * 128:(k + 1) * 128, :], wb_, True)
            for t in range(3):
                kb.mm(pb[t][0:5, :], cs[:, k, :], wb_[:, t * 512:(t + 1) * 512], False, k == 15, [cs, wb_], [pb[t]],
                      inc=(t == 2 or k == 15))
        r = res[l % 2]
        for t in range(3):
            kb.op("dve", lambda e: e.tensor_copy(out=r[:, t * 512:(t + 1) * 512], in_=pb[t][0:5, :]), reads=[pb[t]], writes=[r])
        kb.dma("sp", o.ap()[l], r[:], r, False)
    kb.finish()
    return kb


def run_ada(inp):
    cond = np.concatenate([inp["c"], inp["c_ctx"][None]], 0)
    condT = np.ascontiguousarray(cond.T.reshape(16, 128, 5).transpose(1, 0, 2))
    kb = build_ada()
    maps = []
    for c in range(NCORES):
        sl = slice(1536 * c, 1536 * (c + 1))
        maps.append({"condT": condT, "w": np.ascontiguousarray(inp["ada_w"][:, :, sl]),
                     "b": np.ascontiguousarray(inp["ada_b"][:, sl])})
    res = run(kb, maps)
    mod = np.concatenate([r["o"] for r in res.results], axis=-1)
    return mod


NT = 2304
TB = 768
SUB = 384
NBLK = NT // TB
EPS = 1e-6


def col_ranges(blk):
    lo, hi = blk * TB, (blk + 1) * TB
    out = []
    if lo < 2048:
        out.append((0, min(hi, 2048) - lo, 0))
    if hi > 2048:
        out.append((max(lo, 2048) - lo, hi - lo, 1))
    return out


class TokStage:
    def __init__(self, kb, nmod):
        self.kb = kb
        self.ones = kb.sb("ones", [128, 128])
        self.epsb = kb.sb("epsb", [128, 1])
        kb.op("dve", lambda e: e.memset(self.ones[:], 1.0), writes=[self.ones])
        kb.op("dve", lambda e: e.memset(self.epsb[:], EPS), writes=[self.epsb])
        self.pss = [kb.ps(f"ps{i}", [128, 512]) for i in range(8)]
        self.pi = 0
        self.sq = [kb.sb(f"sq{i}", [128, TB]) for i in range(2)]
        self.rstd = kb.sb("rstd", [128, TB])
        self.tmp = [kb.sb(f"tmp{i}", [128, TB]) for i in range(2)]
        self.ti = 0

    def psum(self):
        p = self.pss[self.pi % 8]
        self.pi += 1
        return p

    def load_mod(self, modT_d, normw_d, idx_shift, idx_scale):
        kb = self.kb
        mt = kb.sb(f"modT{idx_shift}", [128, 2, 6, 16])
        nw = kb.sb(f"nw{idx_shift}", [128, 16])
        kb.dma("sp", mt[:], modT_d.ap(), sbuf=mt, load=True)
        kb.dma("sp", nw[:], normw_d.ap(), sbuf=nw, load=True)
        A = kb.sb(f"A{idx_shift}", [128, 2, 16])
        for w in range(2):
            kb.op("dve", lambda e: e.scalar_tensor_tensor(out=A[:, w, :], in0=mt[:, w, idx_scale, :], scalar=1.0,
                                                          in1=nw[:], op0=ALU.add, op1=ALU.mult),
                  reads=[mt, nw], writes=[A])
        return mt, A

    def norm_mod(self, hk, blk, A, mt, idx_shift, outs):
        kb = self.kb
        pb = [self.psum() for _ in range(TB // SUB)]
        for k in range(16):
            s = self.sq[k % 2]
            kb.act(s[:], hk[k][:], AF.Square, [hk[k]], [s])
            for t in range(TB // SUB):
                kb.mm(pb[t][:, 0:SUB], self.ones[:], s[:, t * SUB:(t + 1) * SUB], k == 0, k == 15, [self.ones, s], [pb[t]],
                      inc=(k == 15 or t == TB // SUB - 1))
        for t in range(TB // SUB):
            sl = slice(t * SUB, (t + 1) * SUB)
            kb.act(self.rstd[:, sl], pb[t][:, 0:SUB], AF.Sqrt, [pb[t], self.epsb], [self.rstd], bias=self.epsb[:], scale=1.0 / D)
        kb.op("dve", lambda e: e.reciprocal(out=self.rstd[:], in_=self.rstd[:]), reads=[self.rstd], writes=[self.rstd])
        for k in range(16):
            tm = self.tmp[self.ti % 2]
            self.ti += 1
            for (lo, hi, w) in col_ranges(blk):
                kb.op("dve", lambda e: e.scalar_tensor_tensor(out=tm[:, lo:hi], in0=hk[k][:, lo:hi], scalar=A[:, w, k:k + 1],
                                                              in1=self.rstd[:, lo:hi], op0=ALU.mult, op1=ALU.mult),
                      reads=[hk[k], A, self.rstd], writes=[tm])
            for (lo, hi, w) in col_ranges(blk):
                kb.act(outs[k][:, lo:hi], tm[:, lo:hi], AF.Identity, [tm, mt], [outs[k]], bias=mt[:, w, idx_shift, k:k + 1])


class WStream:
    def __init__(self, kb, name, nk, nbuf=2):
        self.kb, self.nk = kb, nk
        self.st = [kb.sb(f"{name}_st{i}", [128, nk, 128]) for i in range(nbuf)]
        self.wb = [kb.sb(f"{name}_wb{i}", [128, nk, 128], BF16) for i in range(nbuf)]
        self.i = 0
        self.nbuf = nbuf

    def load(self, w_ap_2d, col0, q="sp", ceng="pool"):
        kb = self.kb
        st, wb = self.st[self.i % self.nbuf], self.wb[self.i % self.nbuf]
        self.i += 1
        src = w_ap_2d[:, col0:col0 + 128].rearrange("(k p) c -> p k c", p=128)
        kb.dma(q, st[:], src, sbuf=st, load=True)
        kb.op(ceng, lambda e: e.tensor_copy(out=wb[:], in_=st[:]), reads=[st], writes=[wb])
        return wb


def build_pre(F, kb_=None):
    kb = kb_ if kb_ is not None else KB()
    hT = kb.din("hT", [16, 128, NT])
    modT = kb.din("modT", [128, 2, 6, 16])
    normw = kb.din("normw", [128, 16])
    W = kb.din("W", [D, F])
    YT = kb.dout("YT", [F // 128, 128, NT])
    ts = TokStage(kb, 1)
    mt, A = ts.load_mod(modT, normw, 0, 1)
    hk = [kb.sb(f"h{k}", [128, TB]) for k in range(16)]
    uk = [kb.sb(f"u{k}", [128, TB], BF16) for k in range(16)]
    ws = WStream(kb, "w", 16, nbuf=3)
    yo = [kb.sb(f"yo{i}", [128, TB]) for i in range(3)]
    for blk in range(NBLK):
        cs = slice(blk * TB, (blk + 1) * TB)
        for k in range(16):
            kb.dma("act" if k % 2 else "sp", hk[k][:], hT.ap()[k][:, cs], sbuf=hk[k], load=True)
        ts.norm_mod(hk, blk, A, mt, 0, uk)
        for m in range(F // 128):
            wb = ws.load(W.ap(), m * 128, q="sp", ceng=("pool" if m % 2 else "dve"))
            y = yo[m % 3]
            for t in range(TB // SUB):
                p = ts.psum()
                for k in range(16):
                    kb.mm(p[:, 0:SUB], wb[:, k, :], uk[k][:, t * SUB:(t + 1) * SUB], k == 0, k == 15, [wb, uk[k]], [p])
                kb.act(y[:, t * SUB:(t + 1) * SUB], p[:, 0:SUB], AF.Copy, [p], [y])
            kb.dma("act", YT.ap()[m][:, cs], y[:], sbuf=y, load=False)
    kb.finish()
    return kb


def to_fm(a):
    t, f = a.shape
    return np.ascontiguousarray(a.T.reshape(f // 128, 128, t))


def from_fm(a):
    c, p, t = a.shape
    return np.ascontiguousarray(a.reshape(c * p, t).T)


def mod_layout(mod_l, b):
    m = np.stack([mod_l[b], mod_l[4]], 0).reshape(2, 6, 16, 128)
    return np.ascontiguousarray(m.transpose(3, 0, 1, 2))


def vec_layout(v):
    return np.ascontiguousarray(v.reshape(-1, 128).T)


def build_post(Fin, last, kb_=None):
    kb = kb_ if kb_ is not None else KB()
    NKI = Fin // 128
    G = 4
    hT = kb.din("hT", [16, 128, NT])
    OT = kb.din("OT", [NKI, 128, NT])
    modT = kb.din("modT", [128, 2, 6, 16])
    normw = kb.din("normw", [128, 16])
    Wo = kb.din("Wo", [Fin, D])
    W1 = kb.din("W1", [D, 4 * D])
    W2 = kb.din("W2", [4 * D, D])
    if last:
        fnw = kb.din("fnw", [128, 16])
    HO = kb.dout("HO", [16, 128, NT])
    ts = TokStage(kb, 1)
    mt, A = ts.load_mod(modT, normw, 3, 4)
    hk = [kb.sb(f"h{k}", [128, TB]) for k in range(16)]
    ob = [kb.sb(f"ob{k}", [128, TB], BF16) for k in range(max(NKI, 16 + G))]
    ost = [kb.sb(f"ost{i}", [128, TB]) for i in range(2 if NKI <= 16 else 1)]
    wso = WStream(kb, "wo", NKI, nbuf=(2 if NKI <= 16 else 1))
    ws1 = WStream(kb, "w1", 16, nbuf=2)
    w2st = [kb.sb(f"w2st{i}", [128, D]) for i in range(2 if NKI <= 16 else 1)]
    w2b = [kb.sb(f"w2b{i}", [128, D], BF16) for i in range(2 * G)]
    if last:
        fw = kb.sb("fw", [128, 16])
        kb.dma("sp", fw[:], fnw.ap(), sbuf=fw, load=True)
        ones16 = kb.sb("ones16", [128, 2, 16])
        kb.op("dve", lambda e: e.memset(ones16[:], 0.0), writes=[ones16])
        zer = kb.sb("zer", [128, 2, 6, 16])
        kb.op("dve", lambda e: e.memset(zer[:], 0.0), writes=[zer])
        fA = kb.sb("fA", [128, 2, 16])
        for w in range(2):
            kb.op("dve", lambda e: e.tensor_copy(out=fA[:, w, :], in_=fw[:]), reads=[fw], writes=[fA])
    NS = TB // SUB
    i2 = 0
    for blk in range(NBLK):
        cs = slice(blk * TB, (blk + 1) * TB)
        rngs = col_ranges(blk)
        for k in range(16):
            kb.dma("act" if k % 2 else "sp", hk[k][:], hT.ap()[k][:, cs], sbuf=hk[k], load=True)
        for k in range(NKI):
            st = ost[k % len(ost)]
            kb.dma("sp", st[:], OT.ap()[k][:, cs], sbuf=st, load=True)
            kb.op("pool", lambda e: e.tensor_copy(out=ob[k][:], in_=st[:]), reads=[st], writes=[ob[k]])
        for m in range(16):
            wb = wso.load(Wo.ap(), m * 128, q="sp", ceng="dve")
            for t in range(NS):
                p = ts.psum()
                for k in range(NKI):
                    kb.mm(p[:, 0:SUB], wb[:, k, :], ob[k][:, t * SUB:(t + 1) * SUB], k == 0, k == NKI - 1, [wb, ob[k]], [p])
                for (lo, hi, w) in rngs:
                    a, b_ = max(lo, t * SUB), min(hi, (t + 1) * SUB)
                    if a >= b_:
                        continue
                    kb.op("dve", lambda e: e.scalar_tensor_tensor(out=hk[m][:, a:b_], in0=p[:, a - t * SUB:b_ - t * SUB],
                                                                  scalar=mt[:, w, 2, m:m + 1], in1=hk[m][:, a:b_],
                                                                  op0=ALU.mult, op1=ALU.add),
                          reads=[p, mt, hk[m]], writes=[hk[m]])
        ts.norm_mod(hk, blk, A, mt, 3, ob[0:16])
        for g in range(4 * D // 128 // G):
            av = ob[16:16 + G]
            for j in range(G):
                jj = g * G + j
                wb = ws1.load(W1.ap(), jj * 128, q="sp", ceng="pool")
                for t in range(NS):
                    p = ts.psum()
                    for k in range(16):
                        kb.mm(p[:, 0:SUB], wb[:, k, :], ob[k][:, t * SUB:(t + 1) * SUB], k == 0, k == 15, [wb, ob[k]], [p])
                    tm = ts.tmp[ts.ti % 2]
                    ts.ti += 1
                    kb.act(tm[:, 0:SUB], p[:, 0:SUB], AF.Relu, [p], [tm])
                    kb.op("pool", lambda e: e.tensor_tensor(out=av[j][:, t * SUB:(t + 1) * SUB], in0=tm[:, 0:SUB], in1=tm[:, 0:SUB],
                                                            op=ALU.mult), reads=[tm], writes=[av[j]])
                st = w2st[i2 % len(w2st)]
                wv = w2b[i2 % (2 * G)]
                i2 += 1
                kb.dma("act", st[:], W2.ap()[jj * 128:(jj + 1) * 128, :], sbuf=st, load=True)
                kb.op("dve", lambda e: e.tensor_copy(out=wv[:], in_=st[:]), reads=[st], writes=[wv])
            wvs = [w2b[(i2 - G + j) % (2 * G)] for j in range(G)]
            for m in range(16):
                for t in range(NS):
                    p = ts.psum()
                    for j in range(G):
                        kb.mm(p[:, 0:SUB], wvs[j][:, m * 128:(m + 1) * 128], av[j][:, t * SUB:(t + 1) * SUB], j == 0, j == G - 1,
                              [wvs[j], av[j]], [p])
                    for (lo, hi, w) in rngs:
                        a, b_ = max(lo, t * SUB), min(hi, (t + 1) * SUB)
                        if a >= b_:
                            continue
                        kb.op("dve", lambda e: e.scalar_tensor_tensor(out=hk[m][:, a:b_], in0=p[:, a - t * SUB:b_ - t * SUB],
                                                                      scalar=mt[:, w, 5, m:m + 1], in1=hk[m][:, a:b_],
                                                                      op0=ALU.mult, op1=ALU.add),
                              reads=[p, mt, hk[m]], writes=[hk[m]])
        if last:
            fo = ts.tmp
            fouts = []
            class _O:
                pass
            _final_norm(kb, ts, hk, blk, fA, zer, HO, cs, fo)
        else:
            for k in range(16):
                kb.dma("sp", HO.ap()[k][:, cs], hk[k][:], sbuf=hk[k], load=False)
    kb.finish()
    return kb


def _final_norm(kb, ts, hk, blk, fA, zer, HO, cs, fo):
    pb = [ts.psum() for _ in range(TB // SUB)]
    for k in range(16):
        s = ts.sq[k % 2]
        kb.act(s[:], hk[k][:], AF.Square, [hk[k]], [s])
        for t in range(TB // SUB):
            kb.mm(pb[t][:, 0:SUB], ts.ones[:], s[:, t * SUB:(t + 1) * SUB], k == 0, k == 15, [ts.ones, s], [pb[t]],
                  inc=(k == 15 or t == TB // SUB - 1))
    for t in range(TB // SUB):
        sl = slice(t * SUB, (t + 1) * SUB)
        kb.act(ts.rstd[:, sl], pb[t][:, 0:SUB], AF.Sqrt, [pb[t], ts.epsb], [ts.rstd], bias=ts.epsb[:], scale=1.0 / D)
    kb.op("dve", lambda e: e.reciprocal(out=ts.rstd[:], in_=ts.rstd[:]), reads=[ts.rstd], writes=[ts.rstd])
    for k in range(16):
        o = fo[k % 2]
        kb.op("dve", lambda e: e.scalar_tensor_tensor(out=o[:], in0=hk[k][:], scalar=fA[:, 0, k:k + 1], in1=ts.rstd[:],
                                                      op0=ALU.mult, op1=ALU.mult), reads=[hk[k], fA, ts.rstd], writes=[o])
        kb.dma("sp", HO.ap()[k][:, cs], o[:], sbuf=o, load=False)


def na_tables(rpb):
    col = np.arange(64)
    cs = np.clip(col - 8, 0, 48)
    cmask = (col[None, :] >= cs[:, None]) & (col[None, :] < cs[:, None] + 16)
    dc = np.clip(col[None, :] - col[:, None], -15, 15) + 15
    t = rpb[:, :, dc]
    rpbT = np.ascontiguousarray(t.transpose(0, 3, 1, 2))
    mask = np.ascontiguousarray(np.broadcast_to(cmask.T[:, None, :], (64, 15, 64)).astype(np.float32))
    return rpbT, mask


def build_na(kb_=None):
    kb = kb_ if kb_ is not None else KB()
    H = 16
    SC = 128 ** -0.5
    QK = kb.din("QK", [32, 128, NT])
    Vr = kb.din("Vr", [64, 36, D])
    Vc = kb.din("Vc", [128, 2, D])
    RP = kb.din("RP", [H, 64, 15 * 64])
    MK = kb.din("MK", [64, 15 * 64])
    OT = kb.dout("OT", [H, 128, NT])
    ones = kb.sb("ones", [128, 128], BF16)
    kb.op("dve", lambda e: e.memset(ones[:], 1.0), writes=[ones])
    mk = kb.sb("mk", [64, 960])
    kb.dma("sp", mk[:], MK.ap(), sbuf=mk, load=True)
    qst = [kb.sb(f"qst{i}", [128, NT]) for i in range(2)]
    qb = [kb.sb(f"qb{i}", [128, NT], BF16) for i in range(2)]
    kbb = [kb.sb(f"kbb{i}", [128, NT], BF16) for i in range(2)]
    vst = kb.sb("vst", [64, 36, 128])
    vb = [kb.sb(f"vb{i}", [64, 36, 128], BF16) for i in range(2)]
    vcst = kb.sb("vcst", [128, 2, 128])
    vcb = [kb.sb(f"vcb{i}", [128, 2, 128], BF16) for i in range(2)]
    rst = kb.sb("rst", [64, 960])
    eb = [kb.sb(f"eb{i}", [64, 960]) for i in range(2)]
    E = [kb.sb(f"E{i}", [64, 512]) for i in range(2)]
    E2 = [kb.sb(f"E2{i}", [64, 512], BF16) for i in range(3)]
    Ec = [kb.sb(f"Ec{i}", [128, 512], BF16) for i in range(3)]
    rec = [kb.sb(f"rec{i}", [128, 512]) for i in range(2)]
    oo = [kb.sb(f"oo{i}", [128, 512]) for i in range(2)]
    PO = [kb.ps(f"PO{i}", [128, 512]) for i in range(2)]
    PS = [kb.ps(f"PS{i}", [128, 512]) for i in range(2)]
    PT = [kb.ps(f"PT{i}", [128, 512]) for i in range(2)]
    PC = [kb.ps(f"PC{i}", [128, 512]) for i in range(2)]
    ie = 0
    for h in range(H):
        q_, k_, v_, vc_, eb_ = qb[h % 2], kbb[h % 2], vb[h % 2], vcb[h % 2], eb[h % 2]
        kb.dma("sp", qst[0][:], QK.ap()[h], sbuf=qst[0], load=True)
        kb.op("pool", lambda e: e.tensor_copy(out=q_[:], in_=qst[0][:]), reads=[qst[0]], writes=[q_])
        kb.dma("sp", qst[1][:], QK.ap()[16 + h], sbuf=qst[1], load=True)
        kb.op("pool", lambda e: e.tensor_copy(out=k_[:], in_=qst[1][:]), reads=[qst[1]], writes=[k_])
        kb.dma("act", vst[:], Vr.ap()[:, :, h * 128:(h + 1) * 128], sbuf=vst, load=True)
        kb.op("pool", lambda e: e.tensor_copy(out=v_[:], in_=vst[:]), reads=[vst], writes=[v_])
        kb.dma("act", vcst[:], Vc.ap()[:, :, h * 128:(h + 1) * 128], sbuf=vcst, load=True)
        kb.op("pool", lambda e: e.tensor_copy(out=vc_[:], in_=vcst[:]), reads=[vcst], writes=[vc_])
        kb.dma("sp", rst[:], RP.ap()[h], sbuf=rst, load=True)
        kb.act(rst[:], rst[:], AF.Exp, [rst], [rst])
        kb.op("dve", lambda e: e.tensor_tensor(out=eb_[:], in0=rst[:], in1=mk[:], op=ALU.mult), reads=[rst, mk], writes=[eb_])
        for rg in range(5):
            po, ps_ = PO[rg % 2], PS[rg % 2]
            if rg < 4:
                for rr in range(8):
                    r = rg * 8 + rr
                    rs = min(max(r - 4, 0), 24)
                    dr0 = rs - r + 7
                    pt, pc = PT[ie % 2], PC[ie % 2]
                    e1, e2, ec = E[ie % 2], E2[ie % 3], Ec[ie % 3]
                    ie += 1
                    qs = q_[:, r * 64:(r + 1) * 64]
                    for j in range(8):
                        kb.mm(pt[0:64, j * 64:(j + 1) * 64], k_[:, (rs + j) * 64:(rs + j + 1) * 64], qs, True, True, [k_, q_], [pt],
                              inc=(j == 7))
                    for t in range(2):
                        kb.mm(pc[:, t * 64:(t + 1) * 64], k_[:, 2048 + t * 128:2048 + (t + 1) * 128], qs, True, True, [k_, q_], [pc],
                              inc=(t == 1))
                    kb.act(e1[:], pt[0:64, :], AF.Exp, [pt], [e1], scale=SC)
                    kb.op("pool" if ie % 2 else "dve",
                          lambda e: e.tensor_tensor(out=e2[:], in0=e1[:], in1=eb_[:, dr0 * 64:dr0 * 64 + 512], op=ALU.mult),
                          reads=[e1, eb_], writes=[e2])
                    kb.act(ec[:, 0:128], pc[:, 0:128], AF.Exp, [pc], [ec], scale=SC)
                    cs = slice(rr * 64, (rr + 1) * 64)
                    for j in range(8):
                        kb.mm(po[:, cs], v_[0:64, rs + j, :], e2[:, j * 64:(j + 1) * 64], j == 0, False, [v_, e2], [po])
                    for t in range(2):
                        kb.mm(po[:, cs], vc_[:, t, :], ec[:, t * 64:(t + 1) * 64], False, t == 1, [vc_, ec], [po], inc=False)
                    for j in range(8):
                        kb.mm(ps_[:, cs], ones[0:64, :], e2[:, j * 64:(j + 1) * 64], j == 0, False, [ones, e2], [ps_])
                    for t in range(2):
                        kb.mm(ps_[:, cs], ones[:, :], ec[:, t * 64:(t + 1) * 64], False, t == 1, [ones, ec], [ps_], inc=(t == 1))
                ncol, c0 = 512, rg * 512
            else:
                pc = PC[ie % 2]
                ec = Ec[ie % 3]
                ie += 1
                qs = q_[:, 2048:2304]
                for t in range(2):
                    kb.mm(pc[:, t * 256:(t + 1) * 256], k_[:, 2048 + t * 128:2048 + (t + 1) * 128], qs, True, True, [k_, q_], [pc],
                          inc=(t == 1))
                kb.act(ec[:], pc[:], AF.Exp, [pc], [ec], scale=SC)
                for t in range(2):
                    kb.mm(po[:, 0:256], vc_[:, t, :], ec[:, t * 256:(t + 1) * 256], t == 0, t == 1, [vc_, ec], [po], inc=False)
                for t in range(2):
                    kb.mm(ps_[:, 0:256], ones[:, :], ec[:, t * 256:(t + 1) * 256], t == 0, t == 1, [ones, ec], [ps_], inc=(t == 1))
                ncol, c0 = 256, 2048
            rc, o_ = rec[rg % 2], oo[rg % 2]
            kb.op("dve", lambda e: e.reciprocal(out=rc[:, 0:ncol], in_=ps_[:, 0:ncol]), reads=[ps_], writes=[rc])
            kb.op("dve", lambda e: e.tensor_tensor(out=o_[:, 0:ncol], in0=po[:, 0:ncol], in1=rc[:, 0:ncol], op=ALU.mult),
                  reads=[po, rc], writes=[o_])
            kb.dma("sp", OT.ap()[h][:, c0:c0 + ncol], o_[:, 0:ncol], sbuf=o_, load=False)
    kb.finish()
    return kb


def na_inputs(Y, rpb):
    YT = to_fm(Y)
    V = Y[:, 4096:6144]
    Vr = np.ascontiguousarray(V.reshape(36, 64, D).transpose(1, 0, 2))
    Vc = np.ascontiguousarray(V[2048:].reshape(2, 128, D).transpose(1, 0, 2))
    rpbT, mask = na_tables(rpb)
    return {"QK": np.ascontiguousarray(YT[0:32]), "Vr": Vr, "Vc": Vc,
            "RP": rpbT.reshape(16, 64, 960), "MK": mask.reshape(64, 960)}


def rope_tables():
    t = np.arange(2048)
    pos = np.stack([t // 64, t % 64], -1).astype(np.float32)
    inv = (10000.0 ** (-np.arange(32, dtype=np.float32) / 32)).astype(np.float32)
    ang = pos[:, :, None] * inv
    cos, sin = np.cos(ang).astype(np.float32), np.sin(ang).astype(np.float32)
    C = np.zeros((128, 2048), np.float32)
    S = np.zeros((128, 2048), np.float32)
    for a in range(2):
        for b in range(2):
            sl = slice(a * 64 + b * 32, a * 64 + b * 32 + 32)
            C[sl] = cos[:, a, :].T
            S[sl] = (-1.0 if b == 0 else 1.0) * sin[:, a, :].T
    P = np.zeros((128, 128), np.float32)
    for d in range(128):
        P[d ^ 32, d] = 1.0
    return C, S, P


def build_da(lambda_init, kb_=None):
    kb = kb_ if kb_ is not None else KB()
    SC = 128 ** -0.5
    QK = kb.din("QK", [32, 128, NT])
    Vt = kb.din("Vt", [128, 18, D])
    COS = kb.din("COS", [128, 2048])
    SIN = kb.din("SIN", [128, 2048])
    PM = kb.din("PM", [128, 128])
    LAM = kb.din("LAM", [1, 512])
    SW = kb.din("SW", [128, 2])
    OT = kb.dout("OT", [16, 128, NT])
    onesb = kb.sb("onesb", [128, 128], BF16)
    onesf = kb.sb("onesf", [128, 128])
    kb.op("dve", lambda e: e.memset(onesb[:], 1.0), writes=[onesb])
    kb.op("dve", lambda e: e.memset(onesf[:], 1.0), writes=[onesf])
    eps5 = kb.sb("eps5", [128, 1])
    kb.op("dve", lambda e: e.memset(eps5[:], 1e-5), writes=[eps5])
    cos = kb.sb("cos", [128, 2048]); sin = kb.sb("sin", [128, 2048]); pm = kb.sb("pm", [128, 128])
    kb.dma("sp", cos[:], COS.ap(), sbuf=cos, load=True)
    kb.dma("act", sin[:], SIN.ap(), sbuf=sin, load=True)
    kb.dma("sp", pm[:], PM.ap(), sbuf=pm, load=True)
    lam = kb.sb("lam", [1, 512]); sw = kb.sb("sw", [128, 2])
    kb.dma("sp", lam[:], LAM.ap(), sbuf=lam, load=True)
    kb.dma("sp", sw[:], SW.ap(), sbuf=sw, load=True)
    kb.op("dve", lambda e: e.tensor_scalar(out=sw[:], in0=sw[:], scalar1=1.0 - lambda_init, scalar2=None, op0=ALU.mult),
          reads=[sw], writes=[sw])
    lp = kb.sb("lp", [1, 256]); ls = kb.sb("ls", [1, 2]); lf = kb.sb("lf", [1, 2]); nl = kb.sb("nl", [128, 2])
    kb.op("dve", lambda e: e.tensor_tensor(out=lp[:, 0:128], in0=lam[:, 0:128], in1=lam[:, 128:256], op=ALU.mult), reads=[lam], writes=[lp])
    kb.op("dve", lambda e: e.tensor_tensor(out=lp[:, 128:256], in0=lam[:, 256:384], in1=lam[:, 384:512], op=ALU.mult), reads=[lam], writes=[lp])
    kb.op("dve", lambda e: e.reduce_sum(out=ls[:], in_=lp[:].rearrange("o (a b) -> o a b", a=2), axis=AX.X), reads=[lp], writes=[ls])
    kb.act(ls[:], ls[:], AF.Exp, [ls], [ls])
    for c in range(2):
        kb.op("dve", lambda e: e.scalar_tensor_tensor(out=lf[:, c:c + 1], in0=ls[:, 1:2], scalar=-lambda_init, in1=ls[:, 0:1],
                                                      op0=ALU.add, op1=ALU.subtract), reads=[ls], writes=[lf])
    PSb = [kb.ps(f"P{i}", [128, 512]) for i in range(8)]
    kb.mm(PSb[0][:, 0:2], onesf[0:1, :], lf[:], True, True, [onesf, lf], [PSb[0]])
    kb.op("dve", lambda e: e.tensor_copy(out=nl[:], in_=PSb[0][:, 0:2]), reads=[PSb[0]], writes=[nl])
    st = [kb.sb(f"st{i}", [128, NT]) for i in range(2)]
    t1 = [kb.sb(f"t1{i}", [128, 512]) for i in range(2)]
    t2 = [kb.sb(f"t2{i}", [128, 512]) for i in range(2)]
    qb = [kb.sb(f"qb{i}", [128, NT], BF16) for i in range(4)]
    kbb = [kb.sb(f"kbb{i}", [128, NT], BF16) for i in range(4)]
    vst = kb.sb("vst", [128, 18, 256])
    vb = [kb.sb(f"vb{i}", [128, 18, 256], BF16) for i in range(2)]
    E = [kb.sb(f"E{i}", [128, 512], BF16) for i in range(3)]
    rc = [kb.sb(f"rc{i}", [128, 512]) for i in range(2)]
    a0 = kb.sb("a0", [128, 512]); a1 = kb.sb("a1", [128, 512])
    oe = [kb.sb(f"oe{i}", [128, 512]) for i in range(2)]
    sq = kb.sb("sq", [128, 512]); rs_ = kb.sb("rs_", [128, 512])
    fo = [kb.sb(f"fo{i}", [128, 512]) for i in range(2)]
    ist = 0; ie = 0; it = 0

    def load_rope(dst, chunk):
        nonlocal ist, it
        s = st[ist % 2]; ist += 1
        kb.dma("sp", s[:], QK.ap()[chunk], sbuf=s, load=True)
        for b4 in range(4):
            cs = slice(b4 * 512, (b4 + 1) * 512)
            p = PSb[6 + (it % 2)]
            a, b_ = t1[it % 2], t2[it % 2]; it += 1
            kb.mm(p[:], pm[:], s[:, cs], True, True, [pm, s], [p])
            kb.op("dve", lambda e: e.tensor_tensor(out=a[:], in0=s[:, cs], in1=cos[:, cs], op=ALU.mult), reads=[s, cos], writes=[a])
            kb.op("dve", lambda e: e.tensor_tensor(out=b_[:], in0=p[:], in1=sin[:, cs], op=ALU.mult), reads=[p, sin], writes=[b_])
            kb.op("pool", lambda e: e.tensor_tensor(out=dst[:, cs], in0=a[:], in1=b_[:], op=ALU.add), reads=[a, b_], writes=[dst])
        kb.op("pool", lambda e: e.tensor_copy(out=dst[:, 2048:NT], in_=s[:, 2048:NT]), reads=[s], writes=[dst])

    for h in range(8):
        par = h % 2
        for c in range(2):
            load_rope(qb[par * 2 + c], 2 * h + c)
            load_rope(kbb[par * 2 + c], 16 + 2 * h + c)
        v_ = vb[par]
        kb.dma("act", vst[:], Vt.ap()[:, :, h * 256:(h + 1) * 256], sbuf=vst, load=True)
        kb.op("pool", lambda e: e.tensor_copy(out=v_[:], in_=vst[:]), reads=[vst], writes=[v_])
        for qblk in range(5):
            c0 = qblk * 512
            ncol = 512 if qblk < 4 else 256
            kts = list(range(18)) if qblk < 4 else [16, 17]
            for c in range(2):
                q_, k_ = qb[par * 2 + c], kbb[par * 2 + c]
                po = [PSb[c * 3], PSb[c * 3 + 1]]
                ps_ = PSb[c * 3 + 2]
                for i, kt in enumerate(kts):
                    pt = PSb[6 + (it % 2)]; it += 1
                    e_ = E[ie % 3]; ie += 1
                    kb.mm(pt[:, 0:ncol], k_[:, kt * 128:(kt + 1) * 128], q_[:, c0:c0 + ncol], True, True, [k_, q_], [pt])
                    kb.act(e_[:, 0:ncol], pt[:, 0:ncol], AF.Exp, [pt], [e_], scale=SC)
                    fst, lst = i == 0, i == len(kts) - 1
                    for e2 in range(2):
                        kb.mm(po[e2][:, 0:ncol], v_[:, kt, e2 * 128:(e2 + 1) * 128], e_[:, 0:ncol], fst, lst, [v_, e_], [po[e2]], inc=False)
                    kb.mm(ps_[:, 0:ncol], onesb[:], e_[:, 0:ncol], fst, lst, [onesb, e_], [ps_], inc=True)
            n = slice(0, ncol)
            for c in range(2):
                kb.op("dve", lambda e: e.reciprocal(out=rc[c][:, n], in_=PSb[c * 3 + 2][:, n]), reads=[PSb[c * 3 + 2]], writes=[rc[c]])
            pss = PSb[6 + (it % 2)]; it += 1
            for e2 in range(2):
                kb.op("dve", lambda e: e.tensor_tensor(out=a0[:, n], in0=PSb[e2][:, n], in1=rc[0][:, n], op=ALU.mult),
                      reads=[PSb[e2], rc[0]], writes=[a0])
                kb.op("dve", lambda e: e.tensor_tensor(out=a1[:, n], in0=PSb[3 + e2][:, n], in1=rc[1][:, n], op=ALU.mult),
                      reads=[PSb[3 + e2], rc[1]], writes=[a1])
                kb.op("dve", lambda e: e.scalar_tensor_tensor(out=oe[e2][:, n], in0=a1[:, n], scalar=nl[:, 0:1], in1=a0[:, n],
                                                               op0=ALU.mult, op1=ALU.add), reads=[a1, nl, a0], writes=[oe[e2]])
                kb.act(sq[:, n], oe[e2][:, n], AF.Square, [oe[e2]], [sq])
                kb.mm(pss[:, n], onesf[:], sq[:, n], e2 == 0, e2 == 1, [onesf, sq], [pss], inc=True)
            kb.act(rs_[:, n], pss[:, n], AF.Sqrt, [pss, eps5], [rs_], bias=eps5[:], scale=1.0 / 256)
            kb.op("dve", lambda e: e.reciprocal(out=rs_[:, n], in_=rs_[:, n]), reads=[rs_], writes=[rs_])
            for e2 in range(2):
                kb.op("dve", lambda e: e.scalar_tensor_tensor(out=fo[e2][:, n], in0=oe[e2][:, n], scalar=sw[:, e2:e2 + 1], in1=rs_[:, n],
                                                              op0=ALU.mult, op1=ALU.mult), reads=[oe[e2], sw, rs_], writes=[fo[e2]])
                kb.dma("sp", OT.ap()[2 * h + e2][:, c0:c0 + ncol], fo[e2][:, n], sbuf=fo[e2], load=False)
    kb.finish()
    return kb


def da_inputs(Y, lam, subw):
    YT = to_fm(Y)
    V = Y[:, 4096:6144]
    Vt = np.ascontiguousarray(V.reshape(18, 128, D).transpose(1, 0, 2))
    C, S, P = rope_tables()
    return {"QK": np.ascontiguousarray(YT[0:32]), "Vt": Vt, "COS": C, "SIN": S, "PM": P,
            "LAM": np.ascontiguousarray(lam.reshape(1, 512)), "SW": np.ascontiguousarray(subw.reshape(2, 128).T)}


def build_ssm_a(kb_=None):
    kb = kb_ if kb_ is not None else KB()
    XI = kb.din("XI", [49, 128, NT])
    CW = kb.din("CW", [128, 48, 5])
    CB = kb.din("CB", [128, 48])
    DTB = kb.din("DTB", [128, 1])
    ALOG = kb.din("ALOG", [128, 1])
    XO = kb.dout("XO", [48, 128, NT])
    DA_ = kb.dout("DA", [2, 128, NT])
    cw = kb.sb("cw", [128, 48, 5]); cb = kb.sb("cb", [128, 48]); dtb = kb.sb("dtb", [128, 1]); al = kb.sb("al", [128, 1])
    one = kb.sb("one", [128, 1])
    kb.op("dve", lambda e: e.memset(one[:], 1.0), writes=[one])
    for t_, d_ in ((cw, CW), (cb, CB), (dtb, DTB), (al, ALOG)):
        kb.dma("sp", t_[:], d_.ap(), sbuf=t_, load=True)
    kb.act(al[:], al[:], AF.Exp, [al], [al])
    kb.op("dve", lambda e: e.tensor_scalar(out=al[:], in0=al[:], scalar1=-1.0, scalar2=None, op0=ALU.mult), reads=[al], writes=[al])
    xin = [kb.sb(f"xin{i}", [128, NT]) for i in range(2)]
    acc = [kb.sb(f"acc{i}", [128, NT]) for i in range(2)]
    for c in range(48):
        xi, ac = xin[c % 2], acc[c % 2]
        kb.dma("sp" if c % 2 else "act", xi[:], XI.ap()[c], sbuf=xi, load=True)
        kb.op("dve", lambda e: e.tensor_scalar(out=ac[:], in0=xi[:], scalar1=cw[:, c, 2:3], scalar2=cb[:, c:c + 1],
                                               op0=ALU.mult, op1=ALU.add), reads=[xi, cw, cb], writes=[ac])
        for (lo, hi) in ((0, 2048), (2048, NT)):
            for k in (0, 1, 3, 4):
                o = k - 2
                a, b_ = lo + max(0, -o), hi - max(0, o)
                kb.op("dve", lambda e: e.scalar_tensor_tensor(out=ac[:, a:b_], in0=xi[:, a + o:b_ + o], scalar=cw[:, c, k:k + 1],
                                                              in1=ac[:, a:b_], op0=ALU.mult, op1=ALU.add),
                      reads=[xi, cw, ac], writes=[ac])
        kb.act(ac[:], ac[:], AF.Silu, [ac], [ac])
        kb.dma("sp", XO.ap()[c], ac[:], sbuf=ac, load=False)
    xi, ac = xin[0], acc[0]
    kb.dma("sp", xi[:], XI.ap()[48], sbuf=xi, load=True)
    kb.act(xi[:], xi[:], AF.Exp, [xi, dtb], [xi], bias=dtb[:])
    kb.act(xi[:], xi[:], AF.Ln, [xi, one], [xi], bias=one[:])
    kb.dma("sp", DA_.ap()[0], xi[:], sbuf=xi, load=False)
    kb.op("dve", lambda e: e.tensor_scalar(out=ac[:], in0=xi[:], scalar1=al[:, 0:1], scalar2=None, op0=ALU.mult), reads=[xi, al], writes=[ac])
    kb.dma("sp", DA_.ap()[1], ac[:], sbuf=ac, load=False)
    kb.finish()
    return kb


ORD_F = [16, 17] + list(range(16))
ORD_B = [17, 16] + list(range(15, -1, -1))


def ssm_consts():
    i = np.arange(128)
    tri_f = (i[:, None] <= i[None, :]).astype(np.float32)
    tri_b = (i[:, None] >= i[None, :]).astype(np.float32)
    t = np.arange(512)
    mk = np.zeros((8, 128, 512), np.float32)
    for q in range(4):
        mk[q] = ((t[None, :] - 128 * q) >= i[:, None])
        mk[4 + q] = ((t[None, :] - 128 * q) <= i[:, None])
    return tri_f, tri_b, np.eye(128, dtype=np.float32), np.ascontiguousarray(mk.transpose(1, 0, 2))


def build_ssm_b(kb_=None):
    kb = kb_ if kb_ is not None else KB()
    XT_ = kb.din("XTOK", [128, 18, 4096])
    BT = kb.din("BT", [8, 128, NT]); CT = kb.din("CT", [8, 128, NT])
    DTT = kb.din("DTT", [128, 18, 128]); ATT = kb.din("ATT", [128, 18, 128])
    XH = kb.din("XH", [64, 64, NT]); ZH = kb.din("ZH", [64, 64, NT])
    NWH = kb.din("NWH", [64, 64]); DSK = kb.din("DSK", [64, 64])
    TF = kb.din("TF", [128, 128]); TBm = kb.din("TB", [128, 128]); ID = kb.din("ID", [128, 128]); MK = kb.din("MK", [128, 8, 512])
    OT = kb.dout("OT", [64, 64, NT])
    onesf = kb.sb("onesf", [128, 128]); kb.op("dve", lambda e: e.memset(onesf[:], 1.0), writes=[onesf])
    eps = kb.sb("eps", [128, 1]); kb.op("dve", lambda e: e.memset(eps[:], EPS), writes=[eps])
    tf = kb.sb("tf", [128, 128]); tb_ = kb.sb("tb_", [128, 128]); idt = kb.sb("idt", [128, 128]); mk = kb.sb("mk", [128, 8, 512])
    dtt = kb.sb("dtt", [128, 18, 128]); att = kb.sb("att", [128, 18, 128]); nwh = kb.sb("nwh", [64, 64]); dsk = kb.sb("dsk", [64, 64])
    for t_, d_ in ((tf, TF), (tb_, TBm), (idt, ID), (mk, MK), (dtt, DTT), (att, ATT), (nwh, NWH), (dsk, DSK)):
        kb.dma("sp", t_[:], d_.ap(), sbuf=t_, load=True)
    P = [kb.ps(f"P{i}", [128, 512]) for i in range(8)]
    cum = [kb.sb(f"cum{d}", [128, 18, 128]) for d in range(2)]
    cumT = [kb.sb(f"cumT{d}", [128, NT]) for d in range(2)]
    ip = 0
    for d, (order, tri) in enumerate(((ORD_F, tf), (ORD_B, tb_))):
        for oi, n in enumerate(order):
            p = P[ip % 2]; ip += 1
            kb.mm(p[:, 0:128], tri[:], att[:, n, :], True, oi == 0, [tri, att], [p], inc=(oi == 0))
            for mi, m in enumerate(order[:oi]):
                kb.mm(p[:, 0:128], onesf[:], att[:, m, :], False, mi == oi - 1, [onesf, att], [p], inc=(mi == oi - 1))
            kb.op("dve", lambda e: e.tensor_copy(out=cum[d][:, n, :], in_=p[:, 0:128]), reads=[p], writes=[cum[d]])
        for n in range(18):
            p = P[ip % 2]; ip += 1
            kb.op("pe", lambda e: e.transpose(out=p[:, 0:128], in_=cum[d][:, n, :], identity=idt[:]), reads=[cum[d], idt], writes=[p])
            kb.act(cumT[d][:, n * 128:(n + 1) * 128], p[:, 0:128], AF.Copy, [p], [cumT[d]])
    xst = [kb.sb(f"xst{i}", [128, 512]) for i in range(3)]
    xdt = [kb.sb(f"xdt{d}", [128, 18, 512], BF16) for d in range(2)]
    bst = kb.sb("bst", [128, NT])
    bb = kb.sb("bb", [128, NT], BF16); cbf = kb.sb("cbf", [128, NT], BF16)
    ncum = [kb.sb(f"ncum{d}", [128, 18, 128]) for d in range(2)]
    for d in range(2):
        kb.op("dve", lambda e: e.tensor_scalar(out=ncum[d][:], in0=cum[d][:], scalar1=-1.0, scalar2=None, op0=ALU.mult),
              reads=[cum[d]], writes=[ncum[d]])
    sel = [kb.sb(f"sel{i}", [128, 128]) for i in range(2)]
    acb = [kb.sb(f"acb{i}", [128, 512]) for i in range(2)]
    Dt = [kb.sb(f"Dt{i}", [128, 512]) for i in range(4)]
    Mt = [kb.sb(f"Mt{i}", [128, 512], BF16) for i in range(3)]
    yz = kb.sb("yz", [64, 8, 512])
    xh = [kb.sb(f"xh{i}", [64, 512]) for i in range(2)]
    zh = [kb.sb(f"zh{i}", [64, 512]) for i in range(2)]
    sqs = kb.sb("sqs", [64, 512]); rst = kb.sb("rst", [64, 512])
    fo = [kb.sb(f"fo{i}", [64, 512]) for i in range(2)]
    PC = [P[6], P[7], P[0], P[1]]
    isel = 0; iw = 0; ih = 0; ix = 0
    for g in range(8):
        for n in range(18):
            xs_ = xst[ix % 3]; ix += 1
            kb.dma("sp" if n % 2 else "act", xs_[:], XT_.ap()[:, n, g * 512:(g + 1) * 512], sbuf=xs_, load=True)
            for d in range(2):
                for r in range(8):
                    col = d * 64 + g * 8 + r
                    kb.op("dve",
                          lambda e: e.tensor_scalar(out=xdt[d][:, n, r * 64:(r + 1) * 64], in0=xs_[:, r * 64:(r + 1) * 64],
                                                    scalar1=dtt[:, n, col:col + 1], scalar2=None, op0=ALU.mult),
                          reads=[xs_, dtt], writes=[xdt[d]])
        kb.dma("sp", bst[:], BT.ap()[g], sbuf=bst, load=True)
        kb.op("pool", lambda e: e.tensor_copy(out=bb[:], in_=bst[:]), reads=[bst], writes=[bb])
        kb.dma("sp", bst[:], CT.ap()[g], sbuf=bst, load=True)
        kb.op("pool", lambda e: e.tensor_copy(out=cbf[:], in_=bst[:]), reads=[bst], writes=[cbf])
        for tb in range(5):
            c0 = tb * 512
            ncol = 512 if tb < 4 else 256
            n_ = slice(0, ncol)
            ttiles = list(range(4 * tb, 4 * tb + 4)) if tb < 4 else [16, 17]
            for r in range(8):
                hd = g * 8 + r
                py = P[2 + (r % 2)]
                work = []
                for d in range(2):
                    col = d * 64 + hd
                    s_ = sel[isel % 2]; a_ = acb[isel % 2]; isel += 1
                    kb.op("dve", lambda e: e.tensor_scalar(out=s_[:], in0=onesf[:], scalar1=idt[:, col:col + 1], scalar2=None, op0=ALU.mult),
                          reads=[onesf, idt], writes=[s_])
                    pa = P[4 + (isel % 2)]
                    kb.mm(pa[:, n_], s_[:], cumT[d][:, c0:c0 + ncol], True, True, [s_, cumT[d]], [pa])
                    kb.act(a_[:, n_], pa[:, n_], AF.Copy, [pa], [a_])
                    if tb < 4:
                        full = [16, 17] + (list(range(0, 4 * tb)) if d == 0 else list(range(4 * tb + 4, 16)))
                    else:
                        full = []
                    work += [(d, col, a_, s, None) for s in full] + [(d, col, a_, s, qi) for qi, s in enumerate(ttiles)]
                pend = None

                def second(item, first, lastp):
                    d, col, a_, s, qi, pc, D_, M_ = item
                    kb.op("dve", lambda e: e.tensor_tensor(out=M_[:, n_], in0=pc[:, n_], in1=D_[:, n_], op=ALU.mult),
                          reads=[pc, D_], writes=[M_])
                    kb.mm(py[0:64, n_], xdt[d][:, s, r * 64:(r + 1) * 64], M_[:, n_], first, lastp, [xdt[d], M_], [py], inc=True)

                for wi, (d, col, a_, s, qi) in enumerate(work):
                    pc = PC[iw % 4]
                    D_ = Dt[iw % 4]; M_ = Mt[iw % 3]; iw += 1
                    kb.mm(pc[:, n_], bb[:, s * 128:(s + 1) * 128], cbf[:, c0:c0 + ncol], True, True, [bb, cbf], [pc])
                    if qi is None:
                        kb.act(D_[:, n_], a_[:, n_], AF.Exp, [a_, ncum[d]], [D_], bias=ncum[d][:, s, col:col + 1])
                    else:
                        kb.op("dve", lambda e: e.tensor_scalar(out=D_[:, n_], in0=a_[:, n_], scalar1=cum[d][:, s, col:col + 1], scalar2=0.0,
                                                                op0=ALU.subtract, op1=ALU.min), reads=[a_, cum[d]], writes=[D_])
                        kb.act(D_[:, n_], D_[:, n_], AF.Exp, [D_], [D_])
                        kb.op("pool", lambda e: e.tensor_tensor(out=D_[:, n_], in0=D_[:, n_], in1=mk[:, d * 4 + qi, n_], op=ALU.mult),
                              reads=[D_, mk], writes=[D_])
                    if pend is not None:
                        second(pend, wi == 1, False)
                    pend = (d, col, a_, s, qi, pc, D_, M_)
                second(pend, len(work) == 1, True)
                x_, z_ = xh[ih % 2], zh[ih % 2]; ih += 1
                kb.dma("sp", x_[:, n_], XH.ap()[hd][:, c0:c0 + ncol], sbuf=x_, load=True)
                kb.dma("act", z_[:, n_], ZH.ap()[hd][:, c0:c0 + ncol], sbuf=z_, load=True)
                kb.act(z_[:, n_], z_[:, n_], AF.Silu, [z_], [z_])
                kb.op("dve", lambda e: e.scalar_tensor_tensor(out=x_[:, n_], in0=x_[:, n_], scalar=dsk[:, hd:hd + 1], in1=py[0:64, n_],
                                                              op0=ALU.mult, op1=ALU.add), reads=[x_, dsk, py], writes=[x_])
                kb.op("pool", lambda e: e.tensor_tensor(out=yz[:, r, n_], in0=x_[:, n_], in1=z_[:, n_], op=ALU.mult),
                      reads=[x_, z_], writes=[yz])
            pn = P[4]
            for r in range(8):
                kb.act(sqs[:, n_], yz[:, r, n_], AF.Square, [yz], [sqs])
                kb.mm(pn[0:64, n_], onesf[0:64, 0:64], sqs[:, n_], r == 0, r == 7, [onesf, sqs], [pn], inc=True)
            kb.act(rst[:, n_], pn[0:64, n_], AF.Sqrt, [pn, eps], [rst], bias=eps[0:64, :], scale=1.0 / 512)
            kb.op("dve", lambda e: e.reciprocal(out=rst[:, n_], in_=rst[:, n_]), reads=[rst], writes=[rst])
            for r in range(8):
                hd = g * 8 + r
                f_ = fo[r % 2]
                kb.op("dve", lambda e: e.scalar_tensor_tensor(out=f_[:, n_], in0=yz[:, r, n_], scalar=nwh[:, hd:hd + 1], in1=rst[:, n_],
                                                              op0=ALU.mult, op1=ALU.mult), reads=[yz, nwh, rst], writes=[f_])
                kb.dma("sp", OT.ap()[hd][:, c0:c0 + ncol], f_[:, n_], sbuf=f_, load=False)
    kb.finish()
    return kb


class View:
    def __init__(self, ap):
        self._ap = ap

    def ap(self):
        return self._ap


def emit_xpose(kb, ident_d, get_in, put_out, nblk_p, nblk_f, pin=128):
    idt = kb.sb("idt", [128, 128])
    kb.dma("sp", idt[:], ident_d.ap(), sbuf=idt, load=True)
    W = nblk_f * 128
    src = [kb.sb(f"src{i}", [128, W]) for i in range(2)]
    ps = [kb.ps(f"tp{i}", [128, 512]) for i in range(4)]
    GB = 4
    dst = [kb.sb(f"dst{i}", [128, GB, nblk_p * pin]) for i in range(1)]
    ip = 0
    for b0 in range(0, nblk_f, GB):
        nb = min(GB, nblk_f - b0)
        d = dst[0]
        for a in range(nblk_p):
            s_ = src[a % 2]
            kb.dma("sp" if a % 2 else "act", s_[0:pin, 0:nb * 128], get_in(a)[:, b0 * 128:(b0 + nb) * 128], sbuf=s_, load=True)
            p = ps[ip % 4]; ip += 1
            for bb in range(nb):
                kb.op("pe", lambda e: e.transpose(out=p[:, bb * pin:(bb + 1) * pin], in_=s_[0:pin, bb * 128:(bb + 1) * 128],
                                                  identity=idt[0:pin, 0:pin]), reads=[s_, idt], writes=[p], inc=(bb == nb - 1))
            for bb in range(nb):
                kb.act(d[:, bb, a * pin:(a + 1) * pin], p[:, bb * pin:(bb + 1) * pin], AF.Copy, [p], [d])
        for bb in range(nb):
            kb.dma("sp", put_out(b0 + bb), d[:, bb, :], sbuf=d, load=False)
    kb.finish()


def emit_ada_fm(kb, condT_d, w_d, b_d, ident_d, MOD):
    ct = kb.sb("ct", [128, 16, 2]); cs = kb.sb("cs", [128, 16, 2])
    idt = kb.sb("idt", [128, 128])
    kb.dma("sp", idt[:], ident_d.ap(), sbuf=idt, load=True)
    kb.dma("sp", ct[:], condT_d.ap(), sbuf=ct, load=True)
    kb.act(cs[:], ct[:], AF.Silu, [ct], [cs])
    wst = [kb.sb(f"wst{i}", [128, 16, 128]) for i in range(3)]
    bst = kb.sb("bst", [96, 128]); bT = kb.sb("bT", [128, 96])
    msb = [kb.sb(f"msb{i}", [128, 2, 96]) for i in range(2)]
    ps = [kb.ps(f"ap{i}", [128, 512]) for i in range(4)]
    i = 0
    for l in range(4):
        kb.dma("act", bst[:], b_d.ap()[l].rearrange("(m p) -> m p", p=128), sbuf=bst, load=True)
        pb = ps[3]
        kb.op("pe", lambda e: e.transpose(out=pb[:, 0:96], in_=bst[:], identity=idt[0:96, 0:96]), reads=[bst, idt], writes=[pb])
        kb.op("dve", lambda e: e.tensor_copy(out=bT[:], in_=pb[:, 0:96]), reads=[pb], writes=[bT])
        ms = msb[l % 2]
        for m in range(96):
            st = wst[i % 3]
            p = ps[i % 3]; i += 1
            kb.dma("sp" if m % 2 else "act", st[:], w_d.ap()[l][:, m * 128:(m + 1) * 128].rearrange("(k p) c -> p k c", p=128),
                   sbuf=st, load=True)
            for k in range(16):
                kb.mm(p[:, 0:2], st[:, k, :], cs[:, k, :], k == 0, k == 15, [st, cs], [p])
            kb.op("dve", lambda e: e.tensor_scalar(out=ms[:, :, m], in0=p[:, 0:2], scalar1=bT[:, m:m + 1], scalar2=None, op0=ALU.add),
                  reads=[p, bT], writes=[ms])
        kb.dma("sp", MOD.ap()[l], ms[:], sbuf=ms, load=False)
    kb.finish()


def build_fused():
    kb = KB()
    kb.fused = True
    nc = kb.nc
    ext = lambda n, sh: nc.dram_tensor(n, list(sh), F32, kind="ExternalInput")
    XTM = ext("XTM", [NT, D]); CONDT = ext("CONDT", [128, 16, 2]); ADAW = ext("ADAW", [4, D, 6 * D]); ADAB = ext("ADAB", [4, 6 * D])
    NMIX = ext("NMIX", [4, 128, 16]); NMLP = ext("NMLP", [4, 128, 16]); FNW = ext("FNW", [128, 16])
    W1 = ext("W1", [4, D, 4 * D]); W2 = ext("W2", [4, 4 * D, D])
    NAQKV = ext("NAQKV", [2, D, 6144]); NAWO = ext("NAWO", [2, D, D]); RP = ext("RP", [2, 16, 64, 960]); MKN = ext("MKN", [64, 960])
    SSWIN = ext("SSWIN", [D, 10368]); SSWO = ext("SSWO", [4096, D]); CW = ext("CW", [128, 48, 5]); CB = ext("CB", [128, 48])
    DTB = ext("DTB", [128, 1]); ALOG = ext("ALOG", [128, 1]); NWH = ext("NWH", [64, 64]); DSK = ext("DSK", [64, 64])
    TF = ext("TF", [128, 128]); TBm = ext("TBM", [128, 128]); ID = ext("ID", [128, 128]); MKS = ext("MKS", [128, 8, 512])
    DAQKV = ext("DAQKV", [D, 6144]); DAWO = ext("DAWO", [D, D]); COS = ext("COS", [128, 2048]); SIN = ext("SIN", [128, 2048])
    PM = ext("PM", [128, 128]); LAM = ext("LAM", [1, 512]); SW = ext("SW", [128, 2])
    OUT = nc.dram_tensor("OUT", [2048, D], F32, kind="ExternalOutput")
    hA = kb.scratch("hA", [16, 128, NT]); hB = kb.scratch("hB", [16, 128, NT])
    YT = kb.scratch("YT", [81, 128, NT]); VTM = kb.scratch("VTM", [NT, 4096]); XO = kb.scratch("XO", [48, 128, NT])
    DAo = kb.scratch("DAo", [2, 128, NT]); DTM = kb.scratch("DTM", [NT, 256]); OTs = kb.scratch("OTs", [32, 128, NT])
    MOD = kb.scratch("MOD", [4, 128, 2, 96])

    def stage(prefix, fn):
        kb.push(prefix)
        fn()
        kb.pop()

    stage("ti_", lambda: emit_xpose(kb, ID, lambda a: XTM.ap()[a * 128:(a + 1) * 128, :],
                                    lambda b: hA.ap()[b], 18, 16))
    stage("ad_", lambda: emit_ada_fm(kb, CONDT, ADAW, ADAB, ID, MOD))
    hcur, hnxt = hA, hB
    import math
    for i in range(4):
        last = i == 3
        mixer, j = i % 3, i // 3
        modv = View(MOD.ap()[i].rearrange("p w (a c) -> p w a c", a=6))
        Wt = (View(NAQKV.ap()[j]), SSWIN, DAQKV)[mixer]
        F = (6144, 10368, 6144)[mixer]
        ytv = View(YT.ap()[0:F // 128])
        kb.io = {"hT": hcur, "modT": modv, "normw": View(NMIX.ap()[i]), "W": Wt, "YT": ytv}
        stage(f"pr{i}_", lambda: build_pre(F, kb_=kb))
        if mixer in (0, 2):
            stage(f"xv{i}_", lambda: emit_xpose(kb, ID, lambda a: YT.ap()[32 + a], lambda b: VTM.ap()[b * 128:(b + 1) * 128, 0:2048],
                                                16, 18))
            vt = VTM.ap()[:, 0:2048]
        if mixer == 0:
            kb.io = {"QK": View(YT.ap()[0:32]), "Vr": View(vt.rearrange("(r c) f -> c r f", c=64)),
                     "Vc": View(vt[2048:NT].rearrange("(t p) f -> p t f", p=128)), "RP": View(RP.ap()[j]), "MK": MKN,
                     "OT": View(OTs.ap()[0:16])}
            stage(f"na{i}_", lambda: build_na(kb_=kb))
            Wo, Fin = View(NAWO.ap()[j]), 2048
        elif mixer == 2:
            kb.io = {"QK": View(YT.ap()[0:32]), "Vt": View(vt.rearrange("(t p) f -> p t f", p=128)), "COS": COS, "SIN": SIN, "PM": PM,
                     "LAM": LAM, "SW": SW, "OT": View(OTs.ap()[0:16])}
            li = 0.8 - 0.6 * math.exp(-0.3 * i)
            stage(f"da{i}_", lambda: build_da(li, kb_=kb))
            Wo, Fin = DAWO, 2048
        else:
            kb.io = {"XI": View(YT.ap()[32:81]), "CW": CW, "CB": CB, "DTB": DTB, "ALOG": ALOG, "XO": XO, "DA": DAo}
            stage(f"sa{i}_", lambda: build_ssm_a(kb_=kb))
            stage(f"xx{i}_", lambda: emit_xpose(kb, ID, lambda a: XO.ap()[a], lambda b: VTM.ap()[b * 128:(b + 1) * 128, :], 32, 18))
            stage(f"xd{i}_", lambda: emit_xpose(kb, ID, lambda a: DAo.ap()[a], lambda b: DTM.ap()[b * 128:(b + 1) * 128, :], 2, 18))
            tokv = lambda ap_: View(ap_.rearrange("(t p) f -> p t f", p=128))
            kb.io = {"XTOK": tokv(VTM.ap()), "BT": View(XO.ap()[32:40]), "CT": View(XO.ap()[40:48]),
                     "DTT": tokv(DTM.ap()[:, 0:128]), "ATT": tokv(DTM.ap()[:, 128:256]),
                     "XH": View(XO.ap()[0:32].rearrange("c (two p) t -> (c two) p t", two=2)),
                     "ZH": View(YT.ap()[0:32].rearrange("c (two p) t -> (c two) p t", two=2)),
                     "NWH": NWH, "DSK": DSK, "TF": TF, "TB": TBm, "ID": ID, "MK": MKS,
                     "OT": View(OTs.ap().rearrange("c (two p) t -> (c two) p t", two=2))}
            stage(f"sb{i}_", lambda: build_ssm_b(kb_=kb))
            Wo, Fin = SSWO, 4096
        kb.io = {"hT": hcur, "OT": View(OTs.ap()[0:Fin // 128]), "modT": modv, "normw": View(NMLP.ap()[i]), "Wo": Wo,
                 "W1": View(W1.ap()[i]), "W2": View(W2.ap()[i]), "fnw": FNW, "HO": hnxt}
        stage(f"po{i}_", lambda: build_post(Fin, last, kb_=kb))
        hcur, hnxt = hnxt, hcur
    kb.io = {}
    stage("to_", lambda: emit_xpose(kb, ID, lambda a: hcur.ap()[a][:, 0:2048], lambda b: OUT.ap()[b * 128:(b + 1) * 128, :], 16, 16))
    kb.fused = False
    kb.finish()
    return kb


def fused_inputs(inp, b):
    import math
    cond = np.stack([inp["c"][b], inp["c_ctx"]], 0)
    condT = np.ascontiguousarray(cond.T.reshape(16, 128, 2).transpose(1, 0, 2))
    tri_f, tri_b, ident, masks = ssm_consts()
    C, S, P = rope_tables()
    rp = np.stack([na_tables(inp["na_rpb"][j])[0].reshape(16, 64, 960) for j in range(2)], 0)
    mkn = na_tables(inp["na_rpb"][0])[1].reshape(64, 960)
    vl = lambda a: np.stack([vec_layout(a[i]) for i in range(a.shape[0])], 0)
    return {
        "XTM": np.ascontiguousarray(np.concatenate([inp["x"][b], inp["ctx"][b]], 0)), "CONDT": condT,
        "ADAW": inp["ada_w"], "ADAB": inp["ada_b"], "NMIX": vl(inp["norm_mix_w"]), "NMLP": vl(inp["norm_mlp_w"]),
        "FNW": vec_layout(inp["final_norm_w"]), "W1": inp["mlp_w1"], "W2": inp["mlp_w2"],
        "NAQKV": inp["na_w_qkv"], "NAWO": inp["na_w_o"], "RP": np.ascontiguousarray(rp), "MKN": np.ascontiguousarray(mkn),
        "SSWIN": inp["ssm_w_in"][0], "SSWO": inp["ssm_w_out"][0],
        "CW": np.ascontiguousarray(inp["ssm_conv_w"][0].reshape(5, 48, 128).transpose(2, 1, 0)),
        "CB": np.ascontiguousarray(inp["ssm_conv_b"][0].reshape(48, 128).T),
        "DTB": np.ascontiguousarray(inp["ssm_dt_bias"][0].reshape(128, 1)), "ALOG": np.ascontiguousarray(inp["ssm_a_log"][0].reshape(128, 1)),
        "NWH": np.ascontiguousarray(inp["ssm_norm_w"][0].reshape(64, 64).T),
        "DSK": np.ascontiguousarray(np.broadcast_to(inp["ssm_d"][0][None, :], (64, 64))),
        "TF": tri_f, "TBM": tri_b, "ID": ident, "MKS": masks,
        "DAQKV": inp["da_w_qkv"][0], "DAWO": inp["da_w_o"][0], "COS": C, "SIN": S, "PM": P,
        "LAM": np.ascontiguousarray(inp["da_lambda"][0].reshape(1, 512)), "SW": np.ascontiguousarray(inp["da_subln_w"][0].reshape(2, 128).T),
    }


def kernel(**inputs):
    inp = {k: np.asarray(v) for k, v in inputs.items()}
    NB = 4
    kb = build_fused()
    res = run(kb, [fused_inputs(inp, b) for b in range(NB)], n=NB)
    return np.stack([res.results[b]["OUT"] for b in range(NB)], 0).astype(np.float32)
```

```python
import numpy as np
from contextlib import ExitStack
import concourse.bass as bass
import concourse.mybir as mybir
from concourse.bass_utils import run_bass_kernel_spmd

F32 = mybir.dt.float32
F32R = mybir.dt.float32r
BF16 = mybir.dt.bfloat16
AF = mybir.ActivationFunctionType
ALU = mybir.AluOpType
AX = mybir.AxisListType


class Buf:
    def __init__(self, kb, t, name):
        self.kb, self.t, self.name = kb, t, name
        self.last_w = None
        self.readers = []
        self.dsem = None
        self.dcnt = 0

    def __getitem__(self, idx):
        return self.t[idx]


class KB:
    def __init__(self):
        self.nc = bass.Bass("TRN2", target_bir_lowering=False)
        self.es = ExitStack()
        nc = self.nc
        self.E = {"pe": nc.tensor, "act": nc.scalar, "dve": nc.vector, "pool": nc.gpsimd, "sp": nc.sync}
        self.sems = {}
        self.cnt = {}
        for e in self.E:
            self.sems[e] = self.es.enter_context(nc.semaphore("s_" + e))
            self.cnt[e] = 0
        self.seen = {e: {} for e in self.E}
        self.nbuf = 0
        self.out_deps = []
        self.stack = [self.es]
        self.io = {}
        self.prefix = ""
        self.dpool = []
        self.scope_bufs = []
        self.fused = False
        self.bar = self.es.enter_context(nc.semaphore("s_bar"))
        self.barcnt = 0
        self.nsem = 0

    def push(self, prefix):
        self.prefix = prefix
        self.stack.append(ExitStack())
        self.scope_bufs = []

    def pop(self):
        for b in self.scope_bufs:
            if b.dsem is not None:
                self.dpool.append((b.dsem, b.dcnt))
                b.dsem = None
        self.scope_bufs = []
        self.stack.pop().close()
        self.prefix = ""

    def barrier(self):
        for d in self.out_deps:
            self._wait("sp", d)
        self.out_deps = []
        for e in self.E:
            if e != "sp" and self.cnt[e] > 0:
                self._wait("sp", (e, self.cnt[e]))
        self.nc.sync.sem_inc(self.bar, 1)
        self.barcnt += 1
        for e in self.E:
            if e != "sp":
                self.E[e].wait_ge(self.bar, self.barcnt)

    def sb(self, name, shape, dtype=F32):
        t = self.stack[-1].enter_context(self.nc.sbuf_tensor(self.prefix + name, list(shape), dtype))
        b = Buf(self, t, self.prefix + name)
        self.scope_bufs.append(b)
        return b

    def ps(self, name, shape, dtype=F32):
        t = self.stack[-1].enter_context(self.nc.psum_tensor(self.prefix + name, list(shape), dtype))
        b = Buf(self, t, self.prefix + name)
        self.scope_bufs.append(b)
        return b

    def din(self, name, shape, dtype=F32):
        if name in self.io:
            return self.io[name]
        return self.nc.dram_tensor(name, list(shape), dtype, kind="ExternalInput")

    def dout(self, name, shape, dtype=F32):
        if name in self.io:
            return self.io[name]
        return self.nc.dram_tensor(name, list(shape), dtype, kind="ExternalOutput")

    def scratch(self, name, shape, dtype=F32):
        return self.nc.dram_tensor(name, list(shape), dtype)

    def _semobj(self, key):
        return self.sems[key] if isinstance(key, str) else key

    def _wait(self, eng, dep):
        key, val = dep
        kid = key if isinstance(key, str) else id(key)
        if not isinstance(key, str) and not isinstance(key, Buf):
            kid = id(key)
        if isinstance(key, str):
            assert val <= self.cnt[key], f"wait on unissued inc {key} {val}>{self.cnt[key]}"
        if self.seen[eng].get(kid, 0) >= val:
            return
        self.E[eng].wait_ge(self._semobj(key), val)
        self.seen[eng][kid] = val

    def _sync(self, eng, reads, writes):
        deps = []
        for b in reads:
            if b.last_w is not None:
                deps.append(b.last_w)
        for b in writes:
            if b.last_w is not None:
                deps.append(b.last_w)
            deps.extend(b.readers)
        for d in deps:
            if eng == "pe" and d[0] == "pe":
                continue
            self._wait(eng, d)

    def op(self, eng, fn, reads=(), writes=(), inc=True):
        self._sync(eng, reads, writes)
        inst = fn(self.E[eng])
        if inc:
            inst.then_inc(self.sems[eng], 1)
            self.cnt[eng] += 1
            val = self.cnt[eng]
        else:
            val = self.cnt[eng] + 1
        dep = (eng, val)
        for b in reads:
            b.readers.append(dep)
        for b in writes:
            b.last_w = dep
            b.readers = []
        return inst

    def dram(self, name, shape, dtype=F32):
        t = self.nc.dram_tensor(name, list(shape), dtype)
        return Buf(self, t, name)

    def _dsem(self, b):
        if b.dsem is None:
            if self.dpool:
                b.dsem, b.dcnt = self.dpool.pop()
            else:
                self.nsem += 1
                b.dsem = self.es.enter_context(self.nc.semaphore(f"d_{self.nsem}"))
                b.dcnt = 0
        return b.dsem

    def dma(self, q, out, in_, sbuf=None, load=None, reads=(), writes=(), is_out=False):
        reads, writes = list(reads), list(writes)
        if sbuf is not None:
            if load:
                writes = [sbuf] + writes
            else:
                reads = [sbuf] + reads
                is_out = True
        owner = writes[0] if writes else reads[0]
        self._dsem(owner)
        self._sync(q, reads, writes)
        inst = self.E[q].dma_start(out=out, in_=in_)
        inst.then_inc(owner.dsem, 16)
        owner.dcnt += 16
        dep = (owner.dsem, owner.dcnt)
        for b in reads:
            b.readers.append(dep)
        for b in writes:
            b.last_w = dep
            b.readers = []
        if is_out:
            self.out_deps.append(dep)
        return inst

    def allgather(self, out_buf, in_buf, groups):
        self._dsem(out_buf)
        self._sync("pool", [in_buf], [out_buf])
        inst = self.nc.gpsimd.collective_compute("AllGather", ALU.bypass, replica_groups=groups,
                                                 ins=[in_buf[:]], outs=[out_buf[:]])
        inst.then_inc(out_buf.dsem, 16)
        out_buf.dcnt += 16
        dep = (out_buf.dsem, out_buf.dcnt)
        in_buf.readers.append(dep)
        out_buf.last_w = dep
        out_buf.readers = []
        return inst

    def finish(self, eng="sp"):
        if self.fused:
            return self.barrier()
        for d in self.out_deps:
            self._wait(eng, d)
        for e in self.E:
            if e != eng and self.cnt[e] > 0:
                self._wait(eng, (e, self.cnt[e]))

    def mm(self, out_ap, lhsT, rhs, start, stop, reads, writes, inc=None):
        return self.op("pe", lambda e: e.matmul(out_ap, lhsT, rhs, start=start, stop=stop),
                       reads=reads, writes=writes, inc=stop if inc is None else inc)

    def act(self, out_ap, in_ap, func, reads, writes, bias=None, scale=1.0, eng="act", **kw):
        def f(e):
            kws = dict(kw)
            if bias is not None:
                kws["bias"] = bias
            return e.activation(out=out_ap, in_=in_ap, func=func, scale=scale, **kws)
        return self.op(eng, f, reads=reads, writes=writes)


def run(kb, in_maps, n=8, trace=False):
    res = run_bass_kernel_spmd(kb.nc, in_maps, core_ids=list(range(n)), trace=trace)
    return res


D = 2048
NCORES = 8


def build_ada(kb_=None):
    kb = kb_ if kb_ is not None else KB()
    CW = 1536
    condT = kb.din("condT", [128, 16, 5])
    w = kb.din("w", [4, D, CW])
    b = kb.din("b", [4, CW])
    o = kb.dout("o", [4, 5, CW])
    ct = kb.sb("ct", [128, 16, 5])
    cs = kb.sb("cs", [128, 16, 5])
    ones = kb.sb("ones", [1, 8])
    bt = kb.sb("bt", [1, 4, CW])
    wt = [kb.sb(f"wt{i}", [128, CW]) for i in range(4)]
    res = [kb.sb(f"res{i}", [5, CW]) for i in range(2)]
    pss = [kb.ps(f"ps{i}", [128, 512]) for i in range(6)]
    kb.dma("sp", ct[:], condT.ap(), ct, True)
    kb.dma("sp", bt[:], b.ap().rearrange("(o l) c -> o l c", o=1), bt, True)
    kb.op("dve", lambda e: e.memset(ones[:], 1.0), writes=[ones])
    kb.act(cs[:], ct[:], AF.Silu, [ct], [cs])
    i = 0
    for l in range(4):
        pb = pss[(l % 2) * 3:(l % 2) * 3 + 3]
        for t in range(3):
            kb.mm(pb[t][0:5, :], ones[0:1, 0:5], bt[0:1, l, t * 512:(t + 1) * 512], True, False, [ones, bt], [pb[t]])
        for k in range(16):
            wb_ = wt[i % 4]
            i += 1
            kb.dma("sp" if k % 2 == 0 else "act", wb_[:], w.ap()[l, k * 128:(k + 1) * 128, :], wb_, True)
            for t in range(3):
                kb.mm(pb[t][0:5, :], cs[:, k, :], wb_[:, t * 512:(t + 1) * 512], False, k == 15, [cs, wb_], [pb[t]],
                      inc=(t == 2 or k == 15))
        r = res[l % 2]
        for t in range(3):
            kb.op("dve", lambda e: e.tensor_copy(out=r[:, t * 512:(t + 1) * 512], in_=pb[t][0:5, :]), reads=[pb[t]], writes=[r])
        kb.dma("sp", o.ap()[l], r[:], r, False)
    kb.finish()
    return kb


def run_ada(inp):
    cond = np.concatenate([inp["c"], inp["c_ctx"][None]], 0)
    condT = np.ascontiguousarray(cond.T.reshape(16, 128, 5).transpose(1, 0, 2))
    kb = build_ada()
    maps = []
    for c in range(NCORES):
        sl = slice(1536 * c, 1536 * (c + 1))
        maps.append({"condT": condT, "w": np.ascontiguousarray(inp["ada_w"][:, :, sl]),
                     "b": np.ascontiguousarray(inp["ada_b"][:, sl])})
    res = run(kb, maps)
    mod = np.concatenate([r["o"] for r in res.results], axis=-1)
    return mod


NT = 2304
TB = 768
SUB = 384
NBLK = NT // TB
EPS = 1e-6


def col_ranges(blk):
    lo, hi = blk * TB, (blk + 1) * TB
    out = []
    if lo < 2048:
        out.append((0, min(hi, 2048) - lo, 0))
    if hi > 2048:
        out.append((max(lo, 2048) - lo, hi - lo, 1))
    return out


class TokStage:
    def __init__(self, kb, nmod):
        self.kb = kb
        self.ones = kb.sb("ones", [128, 128])
        self.epsb = kb.sb("epsb", [128, 1])
        kb.op("dve", lambda e: e.memset(self.ones[:], 1.0), writes=[self.ones])
        kb.op("dve", lambda e: e.memset(self.epsb[:], EPS), writes=[self.epsb])
        self.pss = [kb.ps(f"ps{i}", [128, 512]) for i in range(8)]
        self.pi = 0
        self.sq = [kb.sb(f"sq{i}", [128, TB]) for i in range(2)]
        self.rstd = kb.sb("rstd", [128, TB])
        self.tmp = [kb.sb(f"tmp{i}", [128, TB]) for i in range(2)]
        self.ti = 0

    def psum(self):
        p = self.pss[self.pi % 8]
        self.pi += 1
        return p

    def load_mod(self, modT_d, normw_d, idx_shift, idx_scale):
        kb = self.kb
        mt = kb.sb(f"modT{idx_shift}", [128, 2, 6, 16])
        nw = kb.sb(f"nw{idx_shift}", [128, 16])
        kb.dma("sp", mt[:], modT_d.ap(), sbuf=mt, load=True)
        kb.dma("sp", nw[:], normw_d.ap(), sbuf=nw, load=True)
        A = kb.sb(f"A{idx_shift}", [128, 2, 16])
        for w in range(2):
            kb.op("dve", lambda e: e.scalar_tensor_tensor(out=A[:, w, :], in0=mt[:, w, idx_scale, :], scalar=1.0,
                                                          in1=nw[:], op0=ALU.add, op1=ALU.mult),
                  reads=[mt, nw], writes=[A])
        return mt, A

    def norm_mod(self, hk, blk, A, mt, idx_shift, outs):
        kb = self.kb
        pb = [self.psum() for _ in range(TB // SUB)]
        for k in range(16):
            s = self.sq[k % 2]
            kb.act(s[:], hk[k][:], AF.Square, [hk[k]], [s])
            for t in range(TB // SUB):
                kb.mm(pb[t][:, 0:SUB], self.ones[:], s[:, t * SUB:(t + 1) * SUB], k == 0, k == 15, [self.ones, s], [pb[t]],
                      inc=(k == 15 or t == TB // SUB - 1))
        for t in range(TB // SUB):
            sl = slice(t * SUB, (t + 1) * SUB)
            kb.act(self.rstd[:, sl], pb[t][:, 0:SUB], AF.Sqrt, [pb[t], self.epsb], [self.rstd], bias=self.epsb[:], scale=1.0 / D)
        kb.op("dve", lambda e: e.reciprocal(out=self.rstd[:], in_=self.rstd[:]), reads=[self.rstd], writes=[self.rstd])
        for k in range(16):
            tm = self.tmp[self.ti % 2]
            self.ti += 1
            for (lo, hi, w) in col_ranges(blk):
                kb.op("dve", lambda e: e.scalar_tensor_tensor(out=tm[:, lo:hi], in0=hk[k][:, lo:hi], scalar=A[:, w, k:k + 1],
                                                              in1=self.rstd[:, lo:hi], op0=ALU.mult, op1=ALU.mult),
                      reads=[hk[k], A, self.rstd], writes=[tm])
            for (lo, hi, w) in col_ranges(blk):
                kb.act(outs[k][:, lo:hi], tm[:, lo:hi], AF.Identity, [tm, mt], [outs[k]], bias=mt[:, w, idx_shift, k:k + 1])


class WStream:
    def __init__(self, kb, name, nk, nbuf=2):
        self.kb, self.nk = kb, nk
        self.st = [kb.sb(f"{name}_st{i}", [128, nk, 128]) for i in range(nbuf)]
        self.wb = [kb.sb(f"{name}_wb{i}", [128, nk, 128], BF16) for i in range(nbuf)]
        self.i = 0
        self.nbuf = nbuf

    def load(self, w_ap_2d, col0, q="sp", ceng="pool"):
        kb = self.kb
        st, wb = self.st[self.i % self.nbuf], self.wb[self.i % self.nbuf]
        self.i += 1
        src = w_ap_2d[:, col0:col0 + 128].rearrange("(k p) c -> p k c", p=128)
        kb.dma(q, st[:], src, sbuf=st, load=True)
        kb.op(ceng, lambda e: e.tensor_copy(out=wb[:], in_=st[:]), reads=[st], writes=[wb])
        return wb


class WGroup:
    def __init__(self, kb, name, nst=3, nbuf=2):
        self.kb = kb
        self.st = [kb.sb(f"{name}_s{i}", [128, 2048]) for i in range(nst)]
        self.wb = [kb.sb(f"{name}_g{i}", [128, 8192], BF16) for i in range(nbuf)]
        self.i = 0
        self.j = 0

    def load(self, w_ap_2d, nk, gcols, col0, ceng="act"):
        kb = self.kb
        wb = self.wb[self.i % len(self.wb)]
        self.i += 1
        view = wb[:, 0:nk * gcols].rearrange("p (k c) -> p k c", k=nk)
        KK = min(nk, 2048 // gcols)
        for kk in range(0, nk, KK):
            st = self.st[self.j % len(self.st)]
            self.j += 1
            sv = st[:, 0:KK * gcols].rearrange("p (k c) -> p k c", k=KK)
            src = w_ap_2d[kk * 128:(kk + KK) * 128, col0:col0 + gcols].rearrange("(k p) c -> p k c", p=128)
            kb.dma("sp", sv, src, sbuf=st, load=True)
            if ceng == "act":
                kb.act(view[:, kk:kk + KK, :], sv, AF.Copy, [st], [wb])
            else:
                kb.op(ceng, lambda e: e.tensor_copy(out=view[:, kk:kk + KK, :], in_=sv), reads=[st], writes=[wb])
        return wb, view


def build_pre(F, kb_=None):
    kb = kb_ if kb_ is not None else KB()
    hT = kb.din("hT", [16, 128, NT])
    modT = kb.din("modT", [128, 2, 6, 16])
    normw = kb.din("normw", [128, 16])
    W = kb.din("W", [D, F])
    YT = kb.dout("YT", [F // 128, 128, NT])
    ts = TokStage(kb, 1)
    mt, A = ts.load_mod(modT, normw, 0, 1)
    hk = [kb.sb(f"h{k}", [128, TB]) for k in range(16)]
    uk = [kb.sb(f"u{k}", [128, TB], BF16) for k in range(16)]
    ws = WStream(kb, "w", 16, nbuf=3)
    yo = [kb.sb(f"yo{i}", [128, TB]) for i in range(3)]
    for blk in range(NBLK):
        cs = slice(blk * TB, (blk + 1) * TB)
        for k in range(16):
            kb.dma("act" if k % 2 else "sp", hk[k][:], hT.ap()[k][:, cs], sbuf=hk[k], load=True)
        ts.norm_mod(hk, blk, A, mt, 0, uk)
        for m in range(F // 128):
            wb = ws.load(W.ap(), m * 128, q="sp", ceng=("pool" if m % 2 else "dve"))
            y = yo[m % 3]
            for t in range(TB // SUB):
                p = ts.psum()
                for k in range(16):
                    kb.mm(p[:, 0:SUB], wb[:, k, :], uk[k][:, t * SUB:(t + 1) * SUB], k == 0, k == 15, [wb, uk[k]], [p])
                kb.act(y[:, t * SUB:(t + 1) * SUB], p[:, 0:SUB], AF.Copy, [p], [y])
            kb.dma("act", YT.ap()[m][:, cs], y[:], sbuf=y, load=False)
    kb.finish()
    return kb


def to_fm(a):
    t, f = a.shape
    return np.ascontiguousarray(a.T.reshape(f // 128, 128, t))


def from_fm(a):
    c, p, t = a.shape
    return np.ascontiguousarray(a.reshape(c * p, t).T)


def mod_layout(mod_l, b):
    m = np.stack([mod_l[b], mod_l[4]], 0).reshape(2, 6, 16, 128)
    return np.ascontiguousarray(m.transpose(3, 0, 1, 2))


def vec_layout(v):
    return np.ascontiguousarray(v.reshape(-1, 128).T)


def build_post(Fin, last, kb_=None):
    kb = kb_ if kb_ is not None else KB()
    NKI = Fin // 128
    G = 4
    hT = kb.din("hT", [16, 128, NT])
    OT = kb.din("OT", [NKI, 128, NT])
    modT = kb.din("modT", [128, 2, 6, 16])
    normw = kb.din("normw", [128, 16])
    Wo = kb.din("Wo", [Fin, D])
    W1 = kb.din("W1", [D, 4 * D])
    W2 = kb.din("W2", [4 * D, D])
    if last:
        fnw = kb.din("fnw", [128, 16])
    HO = kb.dout("HO", [16, 128, NT])
    ts = TokStage(kb, 1)
    mt, A = ts.load_mod(modT, normw, 3, 4)
    hk = [kb.sb(f"h{k}", [128, TB]) for k in range(16)]
    ob = [kb.sb(f"ob{k}", [128, TB], BF16) for k in range(max(NKI, 16 + G))]
    ost = [kb.sb(f"ost{i}", [128, TB]) for i in range(2 if NKI <= 16 else 1)]
    wg = WGroup(kb, "wg", nst=2)
    GO = 8192 // NKI
    w2st = [kb.sb(f"w2st{i}", [128, D]) for i in range(2 if NKI <= 16 else 1)]
    w2b = [kb.sb(f"w2b{i}", [128, D], BF16) for i in range(2 * G)]
    if last:
        fw = kb.sb("fw", [128, 16])
        kb.dma("sp", fw[:], fnw.ap(), sbuf=fw, load=True)
        ones16 = kb.sb("ones16", [128, 2, 16])
        kb.op("dve", lambda e: e.memset(ones16[:], 0.0), writes=[ones16])
        zer = kb.sb("zer", [128, 2, 6, 16])
        kb.op("dve", lambda e: e.memset(zer[:], 0.0), writes=[zer])
        fA = kb.sb("fA", [128, 2, 16])
        for w in range(2):
            kb.op("dve", lambda e: e.tensor_copy(out=fA[:, w, :], in_=fw[:]), reads=[fw], writes=[fA])
    NS = TB // SUB
    i2 = 0
    for blk in range(NBLK):
        cs = slice(blk * TB, (blk + 1) * TB)
        rngs = col_ranges(blk)
        for k in range(16):
            kb.dma("act" if k % 2 else "sp", hk[k][:], hT.ap()[k][:, cs], sbuf=hk[k], load=True)
        for k in range(NKI):
            st = ost[k % len(ost)]
            kb.dma("sp", st[:], OT.ap()[k][:, cs], sbuf=st, load=True)
            kb.op("pool", lambda e: e.tensor_copy(out=ob[k][:], in_=st[:]), reads=[st], writes=[ob[k]])
        for m in range(16):
            if (m * 128) % GO == 0:
                wb, wview = wg.load(Wo.ap(), NKI, GO, m * 128, ceng="act")
            mo = (m * 128) % GO
            for t in range(NS):
                p = ts.psum()
                for k in range(NKI):
                    kb.mm(p[:, 0:SUB], wview[:, k, mo:mo + 128], ob[k][:, t * SUB:(t + 1) * SUB], k == 0, k == NKI - 1, [wb, ob[k]], [p])
                for (lo, hi, w) in rngs:
                    a, b_ = max(lo, t * SUB), min(hi, (t + 1) * SUB)
                    if a >= b_:
                        continue
                    kb.op("dve", lambda e: e.scalar_tensor_tensor(out=hk[m][:, a:b_], in0=p[:, a - t * SUB:b_ - t * SUB],
                                                                  scalar=mt[:, w, 2, m:m + 1], in1=hk[m][:, a:b_],
                                                                  op0=ALU.mult, op1=ALU.add),
                          reads=[p, mt, hk[m]], writes=[hk[m]])
        ts.norm_mod(hk, blk, A, mt, 3, ob[0:16])
        for g in range(4 * D // 128 // G):
            av = ob[16:16 + G]
            wb, wview = wg.load(W1.ap(), 16, 512, g * 512, ceng="act")
            for j in range(G):
                jj = g * G + j
                for t in range(NS):
                    p = ts.psum()
                    for k in range(16):
                        kb.mm(p[:, 0:SUB], wview[:, k, j * 128:(j + 1) * 128], ob[k][:, t * SUB:(t + 1) * SUB], k == 0, k == 15, [wb, ob[k]], [p])
                    tm = ts.tmp[ts.ti % 2]
                    ts.ti += 1
                    kb.act(tm[:, 0:SUB], p[:, 0:SUB], AF.Relu, [p], [tm])
                    kb.op("pool", lambda e: e.tensor_tensor(out=av[j][:, t * SUB:(t + 1) * SUB], in0=tm[:, 0:SUB], in1=tm[:, 0:SUB],
                                                            op=ALU.mult), reads=[tm], writes=[av[j]])
                st = w2st[i2 % len(w2st)]
                wv = w2b[i2 % (2 * G)]
                i2 += 1
                kb.dma("act", st[:], W2.ap()[jj * 128:(jj + 1) * 128, :], sbuf=st, load=True)
                kb.op("dve", lambda e: e.tensor_copy(out=wv[:], in_=st[:]), reads=[st], writes=[wv])
            wvs = [w2b[(i2 - G + j) % (2 * G)] for j in range(G)]
            for m in range(16):
                for t in range(NS):
                    p = ts.psum()
                    for j in range(G):
                        kb.mm(p[:, 0:SUB], wvs[j][:, m * 128:(m + 1) * 128], av[j][:, t * SUB:(t + 1) * SUB], j == 0, j == G - 1,
                              [wvs[j], av[j]], [p])
                    for (lo, hi, w) in rngs:
                        a, b_ = max(lo, t * SUB), min(hi, (t + 1) * SUB)
                        if a >= b_:
                            continue
                        kb.op("dve", lambda e: e.scalar_tensor_tensor(out=hk[m][:, a:b_], in0=p[:, a - t * SUB:b_ - t * SUB],
                                                                      scalar=mt[:, w, 5, m:m + 1], in1=hk[m][:, a:b_],
                                                                      op0=ALU.mult, op1=ALU.add),
                              reads=[p, mt, hk[m]], writes=[hk[m]])
        if last:
            fo = ts.tmp
            fouts = []
            class _O:
                pass
            _final_norm(kb, ts, hk, blk, fA, zer, HO, cs, fo)
        else:
            for k in range(16):
                kb.dma("sp", HO.ap()[k][:, cs], hk[k][:], sbuf=hk[k], load=False)
    kb.finish()
    return kb


def _final_norm(kb, ts, hk, blk, fA, zer, HO, cs, fo):
    pb = [ts.psum() for _ in range(TB // SUB)]
    for k in range(16):
        s = ts.sq[k % 2]
        kb.act(s[:], hk[k][:], AF.Square, [hk[k]], [s])
        for t in range(TB // SUB):
            kb.mm(pb[t][:, 0:SUB], ts.ones[:], s[:, t * SUB:(t + 1) * SUB], k == 0, k == 15, [ts.ones, s], [pb[t]],
                  inc=(k == 15 or t == TB // SUB - 1))
    for t in range(TB // SUB):
        sl = slice(t * SUB, (t + 1) * SUB)
        kb.act(ts.rstd[:, sl], pb[t][:, 0:SUB], AF.Sqrt, [pb[t], ts.epsb], [ts.rstd], bias=ts.epsb[:], scale=1.0 / D)
    kb.op("dve", lambda e: e.reciprocal(out=ts.rstd[:], in_=ts.rstd[:]), reads=[ts.rstd], writes=[ts.rstd])
    for k in range(16):
        o = fo[k % 2]
        kb.op("dve", lambda e: e.scalar_tensor_tensor(out=o[:], in0=hk[k][:], scalar=fA[:, 0, k:k + 1], in1=ts.rstd[:],
                                                      op0=ALU.mult, op1=ALU.mult), reads=[hk[k], fA, ts.rstd], writes=[o])
        kb.dma("sp", HO.ap()[k][:, cs], o[:], sbuf=o, load=False)


def na_tables(rpb):
    col = np.arange(64)
    cs = np.clip(col - 8, 0, 48)
    cmask = (col[None, :] >= cs[:, None]) & (col[None, :] < cs[:, None] + 16)
    dc = np.clip(col[None, :] - col[:, None], -15, 15) + 15
    t = rpb[:, :, dc]
    rpbT = np.ascontiguousarray(t.transpose(0, 3, 1, 2))
    mask = np.ascontiguousarray(np.broadcast_to(cmask.T[:, None, :], (64, 15, 64)).astype(np.float32))
    return rpbT, mask


def build_na(kb_=None):
    kb = kb_ if kb_ is not None else KB()
    H = 16
    SC = 128 ** -0.5
    QK = kb.din("QK", [32, 128, NT])
    Vr = kb.din("Vr", [64, 36, D])
    Vc = kb.din("Vc", [128, 2, D])
    RP = kb.din("RP", [H, 64, 15 * 64])
    MK = kb.din("MK", [64, 15 * 64])
    OT = kb.dout("OT", [H, 128, NT])
    ones = kb.sb("ones", [128, 128], BF16)
    kb.op("dve", lambda e: e.memset(ones[:], 1.0), writes=[ones])
    mk = kb.sb("mk", [64, 960])
    kb.dma("sp", mk[:], MK.ap(), sbuf=mk, load=True)
    qst = [kb.sb(f"qst{i}", [128, NT]) for i in range(2)]
    qb = [kb.sb(f"qb{i}", [128, NT], BF16) for i in range(2)]
    kbb = [kb.sb(f"kbb{i}", [128, NT], BF16) for i in range(2)]
    vst = kb.sb("vst", [64, 36, 128])
    vb = [kb.sb(f"vb{i}", [64, 36, 128], BF16) for i in range(2)]
    vcst = kb.sb("vcst", [128, 2, 128])
    vcb = [kb.sb(f"vcb{i}", [128, 2, 128], BF16) for i in range(2)]
    rst = kb.sb("rst", [64, 960])
    eb = [kb.sb(f"eb{i}", [64, 960]) for i in range(2)]
    E = [kb.sb(f"E{i}", [64, 512]) for i in range(2)]
    E2 = [kb.sb(f"E2{i}", [64, 512], BF16) for i in range(3)]
    Ec = [kb.sb(f"Ec{i}", [128, 512], BF16) for i in range(3)]
    rec = [kb.sb(f"rec{i}", [128, 512]) for i in range(2)]
    oo = [kb.sb(f"oo{i}", [128, 512]) for i in range(2)]
    PO = [kb.ps(f"PO{i}", [128, 512]) for i in range(2)]
    PS = [kb.ps(f"PS{i}", [128, 512]) for i in range(2)]
    PT = [kb.ps(f"PT{i}", [128, 512]) for i in range(2)]
    PC = [kb.ps(f"PC{i}", [128, 512]) for i in range(2)]
    ie = 0
    for h in range(H):
        q_, k_, v_, vc_, eb_ = qb[h % 2], kbb[h % 2], vb[h % 2], vcb[h % 2], eb[h % 2]
        kb.dma("sp", qst[0][:], QK.ap()[h], sbuf=qst[0], load=True)
        kb.op("pool", lambda e: e.tensor_copy(out=q_[:], in_=qst[0][:]), reads=[qst[0]], writes=[q_])
        kb.dma("sp", qst[1][:], QK.ap()[16 + h], sbuf=qst[1], load=True)
        kb.op("pool", lambda e: e.tensor_copy(out=k_[:], in_=qst[1][:]), reads=[qst[1]], writes=[k_])
        kb.dma("act", vst[:], Vr.ap()[:, :, h * 128:(h + 1) * 128], sbuf=vst, load=True)
        kb.op("pool", lambda e: e.tensor_copy(out=v_[:], in_=vst[:]), reads=[vst], writes=[v_])
        kb.dma("act", vcst[:], Vc.ap()[:, :, h * 128:(h + 1) * 128], sbuf=vcst, load=True)
        kb.op("pool", lambda e: e.tensor_copy(out=vc_[:], in_=vcst[:]), reads=[vcst], writes=[vc_])
        kb.dma("sp", rst[:], RP.ap()[h], sbuf=rst, load=True)
        kb.act(rst[:], rst[:], AF.Exp, [rst], [rst])
        kb.op("dve", lambda e: e.tensor_tensor(out=eb_[:], in0=rst[:], in1=mk[:], op=ALU.mult), reads=[rst, mk], writes=[eb_])
        for rg in range(5):
            po, ps_ = PO[rg % 2], PS[rg % 2]
            if rg < 4:
                for rr in range(8):
                    r = rg * 8 + rr
                    rs = min(max(r - 4, 0), 24)
                    dr0 = rs - r + 7
                    pt, pc = PT[ie % 2], PC[ie % 2]
                    e1, e2, ec = E[ie % 2], E2[ie % 3], Ec[ie % 3]
                    ie += 1
                    qs = q_[:, r * 64:(r + 1) * 64]
                    for j in range(8):
                        kb.mm(pt[0:64, j * 64:(j + 1) * 64], k_[:, (rs + j) * 64:(rs + j + 1) * 64], qs, True, True, [k_, q_], [pt],
                              inc=(j == 7))
                    for t in range(2):
                        kb.mm(pc[:, t * 64:(t + 1) * 64], k_[:, 2048 + t * 128:2048 + (t + 1) * 128], qs, True, True, [k_, q_], [pc],
                              inc=(t == 1))
                    kb.act(e1[:], pt[0:64, :], AF.Exp, [pt], [e1], scale=SC)
                    kb.op("pool" if ie % 2 else "dve",
                          lambda e: e.tensor_tensor(out=e2[:], in0=e1[:], in1=eb_[:, dr0 * 64:dr0 * 64 + 512], op=ALU.mult),
                          reads=[e1, eb_], writes=[e2])
                    kb.act(ec[:, 0:128], pc[:, 0:128], AF.Exp, [pc], [ec], scale=SC)
                    cs = slice(rr * 64, (rr + 1) * 64)
                    for j in range(8):
                        kb.mm(po[:, cs], v_[0:64, rs + j, :], e2[:, j * 64:(j + 1) * 64], j == 0, False, [v_, e2], [po])
                    for t in range(2):
                        kb.mm(po[:, cs], vc_[:, t, :], ec[:, t * 64:(t + 1) * 64], False, t == 1, [vc_, ec], [po], inc=False)
                    for j in range(8):
                        kb.mm(ps_[:, cs], ones[0:64, :], e2[:, j * 64:(j + 1) * 64], j == 0, False, [ones, e2], [ps_])
                    for t in range(2):
                        kb.mm(ps_[:, cs], ones[:, :], ec[:, t * 64:(t + 1) * 64], False, t == 1, [ones, ec], [ps_], inc=(t == 1))
                ncol, c0 = 512, rg * 512
            else:
                pc = PC[ie % 2]
                ec = Ec[ie % 3]
                ie += 1
                qs = q_[:, 2048:2304]
                for t in range(2):
                    kb.mm(pc[:, t * 256:(t + 1) * 256], k_[:, 2048 + t * 128:2048 + (t + 1) * 128], qs, True, True, [k_, q_], [pc],
                          inc=(t == 1))
                kb.act(ec[:], pc[:], AF.Exp, [pc], [ec], scale=SC)
                for t in range(2):
                    kb.mm(po[:, 0:256], vc_[:, t, :], ec[:, t * 256:(t + 1) * 256], t == 0, t == 1, [vc_, ec], [po], inc=False)
                for t in range(2):
                    kb.mm(ps_[:, 0:256], ones[:, :], ec[:, t * 256:(t + 1) * 256], t == 0, t == 1, [ones, ec], [ps_], inc=(t == 1))
                ncol, c0 = 256, 2048
            rc, o_ = rec[rg % 2], oo[rg % 2]
            kb.op("dve", lambda e: e.reciprocal(out=rc[:, 0:ncol], in_=ps_[:, 0:ncol]), reads=[ps_], writes=[rc])
            kb.op("dve", lambda e: e.tensor_tensor(out=o_[:, 0:ncol], in0=po[:, 0:ncol], in1=rc[:, 0:ncol], op=ALU.mult),
                  reads=[po, rc], writes=[o_])
            kb.dma("sp", OT.ap()[h][:, c0:c0 + ncol], o_[:, 0:ncol], sbuf=o_, load=False)
    kb.finish()
    return kb


def na_inputs(Y, rpb):
    YT = to_fm(Y)
    V = Y[:, 4096:6144]
    Vr = np.ascontiguousarray(V.reshape(36, 64, D).transpose(1, 0, 2))
    Vc = np.ascontiguousarray(V[2048:].reshape(2, 128, D).transpose(1, 0, 2))
    rpbT, mask = na_tables(rpb)
    return {"QK": np.ascontiguousarray(YT[0:32]), "Vr": Vr, "Vc": Vc,
            "RP": rpbT.reshape(16, 64, 960), "MK": mask.reshape(64, 960)}


def rope_tables():
    t = np.arange(2048)
    pos = np.stack([t // 64, t % 64], -1).astype(np.float32)
    inv = (10000.0 ** (-np.arange(32, dtype=np.float32) / 32)).astype(np.float32)
    ang = pos[:, :, None] * inv
    cos, sin = np.cos(ang).astype(np.float32), np.sin(ang).astype(np.float32)
    C = np.zeros((128, 2048), np.float32)
    S = np.zeros((128, 2048), np.float32)
    for a in range(2):
        for b in range(2):
            sl = slice(a * 64 + b * 32, a * 64 + b * 32 + 32)
            C[sl] = cos[:, a, :].T
            S[sl] = (-1.0 if b == 0 else 1.0) * sin[:, a, :].T
    P = np.zeros((128, 128), np.float32)
    for d in range(128):
        P[d ^ 32, d] = 1.0
    return C, S, P


def build_da(lambda_init, kb_=None):
    kb = kb_ if kb_ is not None else KB()
    SC = 128 ** -0.5
    QK = kb.din("QK", [32, 128, NT])
    Vt = kb.din("Vt", [128, 18, D])
    COS = kb.din("COS", [128, 2048])
    SIN = kb.din("SIN", [128, 2048])
    PM = kb.din("PM", [128, 128])
    LAM = kb.din("LAM", [1, 512])
    SW = kb.din("SW", [128, 2])
    OT = kb.dout("OT", [16, 128, NT])
    onesb = kb.sb("onesb", [128, 128], BF16)
    onesf = kb.sb("onesf", [128, 128])
    kb.op("dve", lambda e: e.memset(onesb[:], 1.0), writes=[onesb])
    kb.op("dve", lambda e: e.memset(onesf[:], 1.0), writes=[onesf])
    eps5 = kb.sb("eps5", [128, 1])
    kb.op("dve", lambda e: e.memset(eps5[:], 1e-5), writes=[eps5])
    cos = kb.sb("cos", [128, 2048]); sin = kb.sb("sin", [128, 2048]); pm = kb.sb("pm", [128, 128])
    kb.dma("sp", cos[:], COS.ap(), sbuf=cos, load=True)
    kb.dma("act", sin[:], SIN.ap(), sbuf=sin, load=True)
    kb.dma("sp", pm[:], PM.ap(), sbuf=pm, load=True)
    lam = kb.sb("lam", [1, 512]); sw = kb.sb("sw", [128, 2])
    kb.dma("sp", lam[:], LAM.ap(), sbuf=lam, load=True)
    kb.dma("sp", sw[:], SW.ap(), sbuf=sw, load=True)
    kb.op("dve", lambda e: e.tensor_scalar(out=sw[:], in0=sw[:], scalar1=1.0 - lambda_init, scalar2=None, op0=ALU.mult),
          reads=[sw], writes=[sw])
    lp = kb.sb("lp", [1, 256]); ls = kb.sb("ls", [1, 2]); lf = kb.sb("lf", [1, 2]); nl = kb.sb("nl", [128, 2])
    kb.op("dve", lambda e: e.tensor_tensor(out=lp[:, 0:128], in0=lam[:, 0:128], in1=lam[:, 128:256], op=ALU.mult), reads=[lam], writes=[lp])
    kb.op("dve", lambda e: e.tensor_tensor(out=lp[:, 128:256], in0=lam[:, 256:384], in1=lam[:, 384:512], op=ALU.mult), reads=[lam], writes=[lp])
    kb.op("dve", lambda e: e.reduce_sum(out=ls[:], in_=lp[:].rearrange("o (a b) -> o a b", a=2), axis=AX.X), reads=[lp], writes=[ls])
    kb.act(ls[:], ls[:], AF.Exp, [ls], [ls])
    for c in range(2):
        kb.op("dve", lambda e: e.scalar_tensor_tensor(out=lf[:, c:c + 1], in0=ls[:, 1:2], scalar=-lambda_init, in1=ls[:, 0:1],
                                                      op0=ALU.add, op1=ALU.subtract), reads=[ls], writes=[lf])
    PSb = [kb.ps(f"P{i}", [128, 512]) for i in range(8)]
    kb.mm(PSb[0][:, 0:2], onesf[0:1, :], lf[:], True, True, [onesf, lf], [PSb[0]])
    kb.op("dve", lambda e: e.tensor_copy(out=nl[:], in_=PSb[0][:, 0:2]), reads=[PSb[0]], writes=[nl])
    st = [kb.sb(f"st{i}", [128, NT]) for i in range(2)]
    t1 = [kb.sb(f"t1{i}", [128, 512]) for i in range(2)]
    t2 = [kb.sb(f"t2{i}", [128, 512]) for i in range(2)]
    qb = [kb.sb(f"qb{i}", [128, NT], BF16) for i in range(4)]
    kbb = [kb.sb(f"kbb{i}", [128, NT], BF16) for i in range(4)]
    vst = kb.sb("vst", [128, 18, 256])
    vb = [kb.sb(f"vb{i}", [128, 18, 256], BF16) for i in range(2)]
    E = [kb.sb(f"E{i}", [128, 512], BF16) for i in range(3)]
    rc = [kb.sb(f"rc{i}", [128, 512]) for i in range(2)]
    a0 = kb.sb("a0", [128, 512]); a1 = kb.sb("a1", [128, 512])
    oe = [kb.sb(f"oe{i}", [128, 512]) for i in range(2)]
    sq = kb.sb("sq", [128, 512]); rs_ = kb.sb("rs_", [128, 512])
    fo = [kb.sb(f"fo{i}", [128, 512]) for i in range(2)]
    ist = 0; ie = 0; it = 0

    def load_rope(dst, chunk):
        nonlocal ist, it
        s = st[ist % 2]; ist += 1
        kb.dma("sp", s[:], QK.ap()[chunk], sbuf=s, load=True)
        for b4 in range(4):
            cs = slice(b4 * 512, (b4 + 1) * 512)
            p = PSb[6 + (it % 2)]
            a, b_ = t1[it % 2], t2[it % 2]; it += 1
            kb.mm(p[:], pm[:], s[:, cs], True, True, [pm, s], [p])
            kb.op("dve", lambda e: e.tensor_tensor(out=a[:], in0=s[:, cs], in1=cos[:, cs], op=ALU.mult), reads=[s, cos], writes=[a])
            kb.op("dve", lambda e: e.tensor_tensor(out=b_[:], in0=p[:], in1=sin[:, cs], op=ALU.mult), reads=[p, sin], writes=[b_])
            kb.op("pool", lambda e: e.tensor_tensor(out=dst[:, cs], in0=a[:], in1=b_[:], op=ALU.add), reads=[a, b_], writes=[dst])
        kb.op("pool", lambda e: e.tensor_copy(out=dst[:, 2048:NT], in_=s[:, 2048:NT]), reads=[s], writes=[dst])

    for h in range(8):
        par = h % 2
        for c in range(2):
            load_rope(qb[par * 2 + c], 2 * h + c)
            load_rope(kbb[par * 2 + c], 16 + 2 * h + c)
        v_ = vb[par]
        kb.dma("act", vst[:], Vt.ap()[:, :, h * 256:(h + 1) * 256], sbuf=vst, load=True)
        kb.op("pool", lambda e: e.tensor_copy(out=v_[:], in_=vst[:]), reads=[vst], writes=[v_])
        for qblk in range(5):
            c0 = qblk * 512
            ncol = 512 if qblk < 4 else 256
            kts = list(range(18)) if qblk < 4 else [16, 17]
            for c in range(2):
                q_, k_ = qb[par * 2 + c], kbb[par * 2 + c]
                po = [PSb[c * 3], PSb[c * 3 + 1]]
                ps_ = PSb[c * 3 + 2]
                for i, kt in enumerate(kts):
                    pt = PSb[6 + (it % 2)]; it += 1
                    e_ = E[ie % 3]; ie += 1
                    kb.mm(pt[:, 0:ncol], k_[:, kt * 128:(kt + 1) * 128], q_[:, c0:c0 + ncol], True, True, [k_, q_], [pt])
                    kb.act(e_[:, 0:ncol], pt[:, 0:ncol], AF.Exp, [pt], [e_], scale=SC)
                    fst, lst = i == 0, i == len(kts) - 1
                    for e2 in range(2):
                        kb.mm(po[e2][:, 0:ncol], v_[:, kt, e2 * 128:(e2 + 1) * 128], e_[:, 0:ncol], fst, lst, [v_, e_], [po[e2]], inc=False)
                    kb.mm(ps_[:, 0:ncol], onesb[:], e_[:, 0:ncol], fst, lst, [onesb, e_], [ps_], inc=True)
            n = slice(0, ncol)
            for c in range(2):
                kb.op("dve", lambda e: e.reciprocal(out=rc[c][:, n], in_=PSb[c * 3 + 2][:, n]), reads=[PSb[c * 3 + 2]], writes=[rc[c]])
            pss = PSb[6 + (it % 2)]; it += 1
            for e2 in range(2):
                kb.op("dve", lambda e: e.tensor_tensor(out=a0[:, n], in0=PSb[e2][:, n], in1=rc[0][:, n], op=ALU.mult),
                      reads=[PSb[e2], rc[0]], writes=[a0])
                kb.op("dve", lambda e: e.tensor_tensor(out=a1[:, n], in0=PSb[3 + e2][:, n], in1=rc[1][:, n], op=ALU.mult),
                      reads=[PSb[3 + e2], rc[1]], writes=[a1])
                kb.op("dve", lambda e: e.scalar_tensor_tensor(out=oe[e2][:, n], in0=a1[:, n], scalar=nl[:, 0:1], in1=a0[:, n],
                                                               op0=ALU.mult, op1=ALU.add), reads=[a1, nl, a0], writes=[oe[e2]])
                kb.act(sq[:, n], oe[e2][:, n], AF.Square, [oe[e2]], [sq])
                kb.mm(pss[:, n], onesf[:], sq[:, n], e2 == 0, e2 == 1, [onesf, sq], [pss], inc=True)
            kb.act(rs_[:, n], pss[:, n], AF.Sqrt, [pss, eps5], [rs_], bias=eps5[:], scale=1.0 / 256)
            kb.op("dve", lambda e: e.reciprocal(out=rs_[:, n], in_=rs_[:, n]), reads=[rs_], writes=[rs_])
            for e2 in range(2):
                kb.op("dve", lambda e: e.scalar_tensor_tensor(out=fo[e2][:, n], in0=oe[e2][:, n], scalar=sw[:, e2:e2 + 1], in1=rs_[:, n],
                                                              op0=ALU.mult, op1=ALU.mult), reads=[oe[e2], sw, rs_], writes=[fo[e2]])
                kb.dma("sp", OT.ap()[2 * h + e2][:, c0:c0 + ncol], fo[e2][:, n], sbuf=fo[e2], load=False)
    kb.finish()
    return kb


def da_inputs(Y, lam, subw):
    YT = to_fm(Y)
    V = Y[:, 4096:6144]
    Vt = np.ascontiguousarray(V.reshape(18, 128, D).transpose(1, 0, 2))
    C, S, P = rope_tables()
    return {"QK": np.ascontiguousarray(YT[0:32]), "Vt": Vt, "COS": C, "SIN": S, "PM": P,
            "LAM": np.ascontiguousarray(lam.reshape(1, 512)), "SW": np.ascontiguousarray(subw.reshape(2, 128).T)}


def build_ssm_a(kb_=None):
    kb = kb_ if kb_ is not None else KB()
    XI = kb.din("XI", [49, 128, NT])
    CW = kb.din("CW", [128, 48, 5])
    CB = kb.din("CB", [128, 48])
    DTB = kb.din("DTB", [128, 1])
    ALOG = kb.din("ALOG", [128, 1])
    XO = kb.dout("XO", [48, 128, NT])
    DA_ = kb.dout("DA", [2, 128, NT])
    cw = kb.sb("cw", [128, 48, 5]); cb = kb.sb("cb", [128, 48]); dtb = kb.sb("dtb", [128, 1]); al = kb.sb("al", [128, 1])
    one = kb.sb("one", [128, 1])
    kb.op("dve", lambda e: e.memset(one[:], 1.0), writes=[one])
    for t_, d_ in ((cw, CW), (cb, CB), (dtb, DTB), (al, ALOG)):
        kb.dma("sp", t_[:], d_.ap(), sbuf=t_, load=True)
    kb.act(al[:], al[:], AF.Exp, [al], [al])
    kb.op("dve", lambda e: e.tensor_scalar(out=al[:], in0=al[:], scalar1=-1.0, scalar2=None, op0=ALU.mult), reads=[al], writes=[al])
    xin = [kb.sb(f"xin{i}", [128, NT]) for i in range(2)]
    acc = [kb.sb(f"acc{i}", [128, NT]) for i in range(2)]
    for c in range(48):
        xi, ac = xin[c % 2], acc[c % 2]
        kb.dma("sp" if c % 2 else "act", xi[:], XI.ap()[c], sbuf=xi, load=True)
        kb.op("dve", lambda e: e.tensor_scalar(out=ac[:], in0=xi[:], scalar1=cw[:, c, 2:3], scalar2=cb[:, c:c + 1],
                                               op0=ALU.mult, op1=ALU.add), reads=[xi, cw, cb], writes=[ac])
        for (lo, hi) in ((0, 2048), (2048, NT)):
            for k in (0, 1, 3, 4):
                o = k - 2
                a, b_ = lo + max(0, -o), hi - max(0, o)
                kb.op("dve", lambda e: e.scalar_tensor_tensor(out=ac[:, a:b_], in0=xi[:, a + o:b_ + o], scalar=cw[:, c, k:k + 1],
                                                              in1=ac[:, a:b_], op0=ALU.mult, op1=ALU.add),
                      reads=[xi, cw, ac], writes=[ac])
        kb.act(ac[:], ac[:], AF.Silu, [ac], [ac])
        kb.dma("sp", XO.ap()[c], ac[:], sbuf=ac, load=False)
    xi, ac = xin[0], acc[0]
    kb.dma("sp", xi[:], XI.ap()[48], sbuf=xi, load=True)
    kb.act(xi[:], xi[:], AF.Exp, [xi, dtb], [xi], bias=dtb[:])
    kb.act(xi[:], xi[:], AF.Ln, [xi, one], [xi], bias=one[:])
    kb.dma("sp", DA_.ap()[0], xi[:], sbuf=xi, load=False)
    kb.op("dve", lambda e: e.tensor_scalar(out=ac[:], in0=xi[:], scalar1=al[:, 0:1], scalar2=None, op0=ALU.mult), reads=[xi, al], writes=[ac])
    kb.dma("sp", DA_.ap()[1], ac[:], sbuf=ac, load=False)
    kb.finish()
    return kb


ORD_F = [16, 17] + list(range(16))
ORD_B = [17, 16] + list(range(15, -1, -1))


def ssm_consts():
    i = np.arange(128)
    tri_f = (i[:, None] <= i[None, :]).astype(np.float32)
    tri_b = (i[:, None] >= i[None, :]).astype(np.float32)
    t = np.arange(512)
    mk = np.zeros((8, 128, 512), np.float32)
    for q in range(4):
        mk[q] = ((t[None, :] - 128 * q) >= i[:, None])
        mk[4 + q] = ((t[None, :] - 128 * q) <= i[:, None])
    return tri_f, tri_b, np.eye(128, dtype=np.float32), np.ascontiguousarray(mk.transpose(1, 0, 2))


def build_ssm_b(kb_=None):
    kb = kb_ if kb_ is not None else KB()
    XT_ = kb.din("XTOK", [128, 18, 4096])
    BT = kb.din("BT", [8, 128, NT]); CT = kb.din("CT", [8, 128, NT])
    DTT = kb.din("DTT", [128, 18, 128]); ATT = kb.din("ATT", [128, 18, 128])
    XH = kb.din("XH", [64, 64, NT]); ZH = kb.din("ZH", [64, 64, NT])
    NWH = kb.din("NWH", [64, 64]); DSK = kb.din("DSK", [64, 64])
    TF = kb.din("TF", [128, 128]); TBm = kb.din("TB", [128, 128]); ID = kb.din("ID", [128, 128]); MK = kb.din("MK", [128, 8, 512])
    OT = kb.dout("OT", [64, 64, NT])
    onesf = kb.sb("onesf", [128, 128]); kb.op("dve", lambda e: e.memset(onesf[:], 1.0), writes=[onesf])
    eps = kb.sb("eps", [128, 1]); kb.op("dve", lambda e: e.memset(eps[:], EPS), writes=[eps])
    tf = kb.sb("tf", [128, 128]); tb_ = kb.sb("tb_", [128, 128]); idt = kb.sb("idt", [128, 128]); mk = kb.sb("mk", [128, 8, 512])
    dtt = kb.sb("dtt", [128, 18, 128]); att = kb.sb("att", [128, 18, 128]); nwh = kb.sb("nwh", [64, 64]); dsk = kb.sb("dsk", [64, 64])
    for t_, d_ in ((tf, TF), (tb_, TBm), (idt, ID), (mk, MK), (dtt, DTT), (att, ATT), (nwh, NWH), (dsk, DSK)):
        kb.dma("sp", t_[:], d_.ap(), sbuf=t_, load=True)
    P = [kb.ps(f"P{i}", [128, 512]) for i in range(8)]
    cum = [kb.sb(f"cum{d}", [128, 18, 128]) for d in range(2)]
    cumT = [kb.sb(f"cumT{d}", [128, NT]) for d in range(2)]
    ip = 0
    for d, (order, tri) in enumerate(((ORD_F, tf), (ORD_B, tb_))):
        for oi, n in enumerate(order):
            p = P[ip % 2]; ip += 1
            kb.mm(p[:, 0:128], tri[:], att[:, n, :], True, oi == 0, [tri, att], [p], inc=(oi == 0))
            for mi, m in enumerate(order[:oi]):
                kb.mm(p[:, 0:128], onesf[:], att[:, m, :], False, mi == oi - 1, [onesf, att], [p], inc=(mi == oi - 1))
            kb.op("dve", lambda e: e.tensor_copy(out=cum[d][:, n, :], in_=p[:, 0:128]), reads=[p], writes=[cum[d]])
        for n in range(18):
            p = P[ip % 2]; ip += 1
            kb.op("pe", lambda e: e.transpose(out=p[:, 0:128], in_=cum[d][:, n, :], identity=idt[:]), reads=[cum[d], idt], writes=[p])
            kb.act(cumT[d][:, n * 128:(n + 1) * 128], p[:, 0:128], AF.Copy, [p], [cumT[d]])
    xst = [kb.sb(f"xst{i}", [128, 512]) for i in range(3)]
    xdt = [kb.sb(f"xdt{d}", [128, 18, 512], BF16) for d in range(2)]
    bst = kb.sb("bst", [128, NT])
    bb = kb.sb("bb", [128, NT], BF16); cbf = kb.sb("cbf", [128, NT], BF16)
    ncum = [kb.sb(f"ncum{d}", [128, 18, 128]) for d in range(2)]
    for d in range(2):
        kb.op("dve", lambda e: e.tensor_scalar(out=ncum[d][:], in0=cum[d][:], scalar1=-1.0, scalar2=None, op0=ALU.mult),
              reads=[cum[d]], writes=[ncum[d]])
    sel = [kb.sb(f"sel{i}", [128, 128]) for i in range(2)]
    acb = [kb.sb(f"acb{i}", [128, 512]) for i in range(2)]
    Dt = [kb.sb(f"Dt{i}", [128, 512]) for i in range(4)]
    Mt = [kb.sb(f"Mt{i}", [128, 512], BF16) for i in range(3)]
    yz = kb.sb("yz", [64, 8, 512])
    xh = [kb.sb(f"xh{i}", [64, 512]) for i in range(2)]
    zh = [kb.sb(f"zh{i}", [64, 512]) for i in range(2)]
    sqs = kb.sb("sqs", [64, 512]); rst = kb.sb("rst", [64, 512])
    fo = [kb.sb(f"fo{i}", [64, 512]) for i in range(2)]
    PC = [P[6], P[7], P[0], P[1]]
    isel = 0; iw = 0; ih = 0; ix = 0
    for g in range(8):
        for n in range(18):
            xs_ = xst[ix % 3]; ix += 1
            kb.dma("sp" if n % 2 else "act", xs_[:], XT_.ap()[:, n, g * 512:(g + 1) * 512], sbuf=xs_, load=True)
            for d in range(2):
                for r in range(8):
                    col = d * 64 + g * 8 + r
                    kb.op("dve",
                          lambda e: e.tensor_scalar(out=xdt[d][:, n, r * 64:(r + 1) * 64], in0=xs_[:, r * 64:(r + 1) * 64],
                                                    scalar1=dtt[:, n, col:col + 1], scalar2=None, op0=ALU.mult),
                          reads=[xs_, dtt], writes=[xdt[d]])
        kb.dma("sp", bst[:], BT.ap()[g], sbuf=bst, load=True)
        kb.op("pool", lambda e: e.tensor_copy(out=bb[:], in_=bst[:]), reads=[bst], writes=[bb])
        kb.dma("sp", bst[:], CT.ap()[g], sbuf=bst, load=True)
        kb.op("pool", lambda e: e.tensor_copy(out=cbf[:], in_=bst[:]), reads=[bst], writes=[cbf])
        for tb in range(5):
            c0 = tb * 512
            ncol = 512 if tb < 4 else 256
            n_ = slice(0, ncol)
            ttiles = list(range(4 * tb, 4 * tb + 4)) if tb < 4 else [16, 17]
            for r in range(8):
                hd = g * 8 + r
                py = P[2 + (r % 2)]
                work = []
                for d in range(2):
                    col = d * 64 + hd
                    s_ = sel[isel % 2]; a_ = acb[isel % 2]; isel += 1
                    kb.op("dve", lambda e: e.tensor_scalar(out=s_[:], in0=onesf[:], scalar1=idt[:, col:col + 1], scalar2=None, op0=ALU.mult),
                          reads=[onesf, idt], writes=[s_])
                    pa = P[4 + (isel % 2)]
                    kb.mm(pa[:, n_], s_[:], cumT[d][:, c0:c0 + ncol], True, True, [s_, cumT[d]], [pa])
                    kb.act(a_[:, n_], pa[:, n_], AF.Copy, [pa], [a_])
                    if tb < 4:
                        full = [16, 17] + (list(range(0, 4 * tb)) if d == 0 else list(range(4 * tb + 4, 16)))
                    else:
                        full = []
                    work += [(d, col, a_, s, None) for s in full] + [(d, col, a_, s, qi) for qi, s in enumerate(ttiles)]
                pend = None

                def second(item, first, lastp):
                    d, col, a_, s, qi, pc, D_, M_ = item
                    kb.op("dve", lambda e: e.tensor_tensor(out=M_[:, n_], in0=pc[:, n_], in1=D_[:, n_], op=ALU.mult),
                          reads=[pc, D_], writes=[M_])
                    kb.mm(py[0:64, n_], xdt[d][:, s, r * 64:(r + 1) * 64], M_[:, n_], first, lastp, [xdt[d], M_], [py], inc=True)

                for wi, (d, col, a_, s, qi) in enumerate(work):
                    pc = PC[iw % 4]
                    D_ = Dt[iw % 4]; M_ = Mt[iw % 3]; iw += 1
                    kb.mm(pc[:, n_], bb[:, s * 128:(s + 1) * 128], cbf[:, c0:c0 + ncol], True, True, [bb, cbf], [pc])
                    if qi is None:
                        kb.act(D_[:, n_], a_[:, n_], AF.Exp, [a_, ncum[d]], [D_], bias=ncum[d][:, s, col:col + 1])
                    else:
                        kb.op("dve", lambda e: e.tensor_scalar(out=D_[:, n_], in0=a_[:, n_], scalar1=cum[d][:, s, col:col + 1], scalar2=0.0,
                                                                op0=ALU.subtract, op1=ALU.min), reads=[a_, cum[d]], writes=[D_])
                        kb.act(D_[:, n_], D_[:, n_], AF.Exp, [D_], [D_])
                        kb.op("pool", lambda e: e.tensor_tensor(out=D_[:, n_], in0=D_[:, n_], in1=mk[:, d * 4 + qi, n_], op=ALU.mult),
                              reads=[D_, mk], writes=[D_])
                    if pend is not None:
                        second(pend, wi == 1, False)
                    pend = (d, col, a_, s, qi, pc, D_, M_)
                second(pend, len(work) == 1, True)
                x_, z_ = xh[ih % 2], zh[ih % 2]; ih += 1
                kb.dma("sp", x_[:, n_], XH.ap()[hd][:, c0:c0 + ncol], sbuf=x_, load=True)
                kb.dma("act", z_[:, n_], ZH.ap()[hd][:, c0:c0 + ncol], sbuf=z_, load=True)
                kb.act(z_[:, n_], z_[:, n_], AF.Silu, [z_], [z_])
                kb.op("dve", lambda e: e.scalar_tensor_tensor(out=x_[:, n_], in0=x_[:, n_], scalar=dsk[:, hd:hd + 1], in1=py[0:64, n_],
                                                              op0=ALU.mult, op1=ALU.add), reads=[x_, dsk, py], writes=[x_])
                kb.op("pool", lambda e: e.tensor_tensor(out=yz[:, r, n_], in0=x_[:, n_], in1=z_[:, n_], op=ALU.mult),
                      reads=[x_, z_], writes=[yz])
            pn = P[4]
            for r in range(8):
                kb.act(sqs[:, n_], yz[:, r, n_], AF.Square, [yz], [sqs])
                kb.mm(pn[0:64, n_], onesf[0:64, 0:64], sqs[:, n_], r == 0, r == 7, [onesf, sqs], [pn], inc=True)
            kb.act(rst[:, n_], pn[0:64, n_], AF.Sqrt, [pn, eps], [rst], bias=eps[0:64, :], scale=1.0 / 512)
            kb.op("dve", lambda e: e.reciprocal(out=rst[:, n_], in_=rst[:, n_]), reads=[rst], writes=[rst])
            for r in range(8):
                hd = g * 8 + r
                f_ = fo[r % 2]
                kb.op("dve", lambda e: e.scalar_tensor_tensor(out=f_[:, n_], in0=yz[:, r, n_], scalar=nwh[:, hd:hd + 1], in1=rst[:, n_],
                                                              op0=ALU.mult, op1=ALU.mult), reads=[yz, nwh, rst], writes=[f_])
                kb.dma("sp", OT.ap()[hd][:, c0:c0 + ncol], f_[:, n_], sbuf=f_, load=False)
    kb.finish()
    return kb


class View:
    def __init__(self, ap):
        self._ap = ap

    def ap(self):
        return self._ap


def emit_xpose(kb, ident_d, get_in, put_out, nblk_p, nblk_f, pin=128):
    idt = kb.sb("idt", [128, 128])
    kb.dma("sp", idt[:], ident_d.ap(), sbuf=idt, load=True)
    W = nblk_f * 128
    src = [kb.sb(f"src{i}", [128, W]) for i in range(2)]
    ps = [kb.ps(f"tp{i}", [128, 512]) for i in range(4)]
    GB = 4
    dst = [kb.sb(f"dst{i}", [128, GB, nblk_p * pin]) for i in range(1)]
    ip = 0
    for b0 in range(0, nblk_f, GB):
        nb = min(GB, nblk_f - b0)
        d = dst[0]
        for a in range(nblk_p):
            s_ = src[a % 2]
            kb.dma("sp" if a % 2 else "act", s_[0:pin, 0:nb * 128], get_in(a)[:, b0 * 128:(b0 + nb) * 128], sbuf=s_, load=True)
            p = ps[ip % 4]; ip += 1
            for bb in range(nb):
                kb.op("pe", lambda e: e.transpose(out=p[:, bb * pin:(bb + 1) * pin], in_=s_[0:pin, bb * 128:(bb + 1) * 128],
                                                  identity=idt[0:pin, 0:pin]), reads=[s_, idt], writes=[p], inc=(bb == nb - 1))
            for bb in range(nb):
                kb.act(d[:, bb, a * pin:(a + 1) * pin], p[:, bb * pin:(bb + 1) * pin], AF.Copy, [p], [d])
        for bb in range(nb):
            kb.dma("sp", put_out(b0 + bb), d[:, bb, :], sbuf=d, load=False)
    kb.finish()


def emit_ada_fm(kb, condT_d, w_d, b_d, ident_d, MOD):
    ct = kb.sb("ct", [128, 16, 2]); cs = kb.sb("cs", [128, 16, 2])
    idt = kb.sb("idt", [128, 128])
    kb.dma("sp", idt[:], ident_d.ap(), sbuf=idt, load=True)
    kb.dma("sp", ct[:], condT_d.ap(), sbuf=ct, load=True)
    kb.act(cs[:], ct[:], AF.Silu, [ct], [cs])
    wst = [kb.sb(f"wst{i}", [128, 16, 128]) for i in range(3)]
    bst = kb.sb("bst", [96, 128]); bT = kb.sb("bT", [128, 96])
    msb = [kb.sb(f"msb{i}", [128, 2, 96]) for i in range(2)]
    ps = [kb.ps(f"ap{i}", [128, 512]) for i in range(4)]
    i = 0
    for l in range(4):
        kb.dma("act", bst[:], b_d.ap()[l].rearrange("(m p) -> m p", p=128), sbuf=bst, load=True)
        pb = ps[3]
        kb.op("pe", lambda e: e.transpose(out=pb[:, 0:96], in_=bst[:], identity=idt[0:96, 0:96]), reads=[bst, idt], writes=[pb])
        kb.op("dve", lambda e: e.tensor_copy(out=bT[:], in_=pb[:, 0:96]), reads=[pb], writes=[bT])
        ms = msb[l % 2]
        for m in range(96):
            st = wst[i % 3]
            p = ps[i % 3]; i += 1
            kb.dma("sp" if m % 2 else "act", st[:], w_d.ap()[l][:, m * 128:(m + 1) * 128].rearrange("(k p) c -> p k c", p=128),
                   sbuf=st, load=True)
            for k in range(16):
                kb.mm(p[:, 0:2], st[:, k, :], cs[:, k, :], k == 0, k == 15, [st, cs], [p])
            kb.op("dve", lambda e: e.tensor_scalar(out=ms[:, :, m], in0=p[:, 0:2], scalar1=bT[:, m:m + 1], scalar2=None, op0=ALU.add),
                  reads=[p, bT], writes=[ms])
        kb.dma("sp", MOD.ap()[l], ms[:], sbuf=ms, load=False)
    kb.finish()


def build_fused():
    kb = KB()
    kb.fused = True
    nc = kb.nc
    ext = lambda n, sh: nc.dram_tensor(n, list(sh), F32, kind="ExternalInput")
    XTM = ext("XTM", [NT, D]); CONDT = ext("CONDT", [128, 16, 2]); ADAW = ext("ADAW", [4, D, 6 * D]); ADAB = ext("ADAB", [4, 6 * D])
    NMIX = ext("NMIX", [4, 128, 16]); NMLP = ext("NMLP", [4, 128, 16]); FNW = ext("FNW", [128, 16])
    W1 = ext("W1", [4, D, 4 * D]); W2 = ext("W2", [4, 4 * D, D])
    NAQKV = ext("NAQKV", [2, D, 6144]); NAWO = ext("NAWO", [2, D, D]); RP = ext("RP", [2, 16, 64, 960]); MKN = ext("MKN", [64, 960])
    SSWIN = ext("SSWIN", [D, 10368]); SSWO = ext("SSWO", [4096, D]); CW = ext("CW", [128, 48, 5]); CB = ext("CB", [128, 48])
    DTB = ext("DTB", [128, 1]); ALOG = ext("ALOG", [128, 1]); NWH = ext("NWH", [64, 64]); DSK = ext("DSK", [64, 64])
    TF = ext("TF", [128, 128]); TBm = ext("TBM", [128, 128]); ID = ext("ID", [128, 128]); MKS = ext("MKS", [128, 8, 512])
    DAQKV = ext("DAQKV", [D, 6144]); DAWO = ext("DAWO", [D, D]); COS = ext("COS", [128, 2048]); SIN = ext("SIN", [128, 2048])
    PM = ext("PM", [128, 128]); LAM = ext("LAM", [1, 512]); SW = ext("SW", [128, 2])
    OUT = nc.dram_tensor("OUT", [2048, D], F32, kind="ExternalOutput")
    hA = kb.scratch("hA", [16, 128, NT]); hB = kb.scratch("hB", [16, 128, NT])
    YT = kb.scratch("YT", [81, 128, NT]); VTM = kb.scratch("VTM", [NT, 4096]); XO = kb.scratch("XO", [48, 128, NT])
    DAo = kb.scratch("DAo", [2, 128, NT]); DTM = kb.scratch("DTM", [NT, 256]); OTs = kb.scratch("OTs", [32, 128, NT])
    MOD = kb.scratch("MOD", [4, 128, 2, 96])

    def stage(prefix, fn):
        kb.push(prefix)
        fn()
        kb.pop()

    stage("ti_", lambda: emit_xpose(kb, ID, lambda a: XTM.ap()[a * 128:(a + 1) * 128, :],
                                    lambda b: hA.ap()[b], 18, 16))
    stage("ad_", lambda: emit_ada_fm(kb, CONDT, ADAW, ADAB, ID, MOD))
    hcur, hnxt = hA, hB
    import math
    for i in range(4):
        last = i == 3
        mixer, j = i % 3, i // 3
        modv = View(MOD.ap()[i].rearrange("p w (a c) -> p w a c", a=6))
        Wt = (View(NAQKV.ap()[j]), SSWIN, DAQKV)[mixer]
        F = (6144, 10368, 6144)[mixer]
        ytv = View(YT.ap()[0:F // 128])
        kb.io = {"hT": hcur, "modT": modv, "normw": View(NMIX.ap()[i]), "W": Wt, "YT": ytv}
        stage(f"pr{i}_", lambda: build_pre(F, kb_=kb))
        if mixer in (0, 2):
            stage(f"xv{i}_", lambda: emit_xpose(kb, ID, lambda a: YT.ap()[32 + a], lambda b: VTM.ap()[b * 128:(b + 1) * 128, 0:2048],
                                                16, 18))
            vt = VTM.ap()[:, 0:2048]
        if mixer == 0:
            kb.io = {"QK": View(YT.ap()[0:32]), "Vr": View(vt.rearrange("(r c) f -> c r f", c=64)),
                     "Vc": View(vt[2048:NT].rearrange("(t p) f -> p t f", p=128)), "RP": View(RP.ap()[j]), "MK": MKN,
                     "OT": View(OTs.ap()[0:16])}
            stage(f"na{i}_", lambda: build_na(kb_=kb))
            Wo, Fin = View(NAWO.ap()[j]), 2048
        elif mixer == 2:
            kb.io = {"QK": View(YT.ap()[0:32]), "Vt": View(vt.rearrange("(t p) f -> p t f", p=128)), "COS": COS, "SIN": SIN, "PM": PM,
                     "LAM": LAM, "SW": SW, "OT": View(OTs.ap()[0:16])}
            li = 0.8 - 0.6 * math.exp(-0.3 * i)
            stage(f"da{i}_", lambda: build_da(li, kb_=kb))
            Wo, Fin = DAWO, 2048
        else:
            kb.io = {"XI": View(YT.ap()[32:81]), "CW": CW, "CB": CB, "DTB": DTB, "ALOG": ALOG, "XO": XO, "DA": DAo}
            stage(f"sa{i}_", lambda: build_ssm_a(kb_=kb))
            stage(f"xx{i}_", lambda: emit_xpose(kb, ID, lambda a: XO.ap()[a], lambda b: VTM.ap()[b * 128:(b + 1) * 128, :], 32, 18))
            stage(f"xd{i}_", lambda: emit_xpose(kb, ID, lambda a: DAo.ap()[a], lambda b: DTM.ap()[b * 128:(b + 1) * 128, :], 2, 18))
            tokv = lambda ap_: View(ap_.rearrange("(t p) f -> p t f", p=128))
            kb.io = {"XTOK": tokv(VTM.ap()), "BT": View(XO.ap()[32:40]), "CT": View(XO.ap()[40:48]),
                     "DTT": tokv(DTM.ap()[:, 0:128]), "ATT": tokv(DTM.ap()[:, 128:256]),
                     "XH": View(XO.ap()[0:32].rearrange("c (two p) t -> (c two) p t", two=2)),
                     "ZH": View(YT.ap()[0:32].rearrange("c (two p) t -> (c two) p t", two=2)),
                     "NWH": NWH, "DSK": DSK, "TF": TF, "TB": TBm, "ID": ID, "MK": MKS,
                     "OT": View(OTs.ap().rearrange("c (two p) t -> (c two) p t", two=2))}
            stage(f"sb{i}_", lambda: build_ssm_b(kb_=kb))
            Wo, Fin = SSWO, 4096
        kb.io = {"hT": hcur, "OT": View(OTs.ap()[0:Fin // 128]), "modT": modv, "normw": View(NMLP.ap()[i]), "Wo": Wo,
                 "W1": View(W1.ap()[i]), "W2": View(W2.ap()[i]), "fnw": FNW, "HO": hnxt}
        stage(f"po{i}_", lambda: build_post(Fin, last, kb_=kb))
        hcur, hnxt = hnxt, hcur
    kb.io = {}
    stage("to_", lambda: emit_xpose(kb, ID, lambda a: hcur.ap()[a][:, 0:2048], lambda b: OUT.ap()[b * 128:(b + 1) * 128, :], 16, 16))
    kb.fused = False
    kb.finish()
    return kb


def fused_inputs(inp, b):
    import math
    cond = np.stack([inp["c"][b], inp["c_ctx"]], 0)
    condT = np.ascontiguousarray(cond.T.reshape(16, 128, 2).transpose(1, 0, 2))
    tri_f, tri_b, ident, masks = ssm_consts()
    C, S, P = rope_tables()
    rp = np.stack([na_tables(inp["na_rpb"][j])[0].reshape(16, 64, 960) for j in range(2)], 0)
    mkn = na_tables(inp["na_rpb"][0])[1].reshape(64, 960)
    vl = lambda a: np.stack([vec_layout(a[i]) for i in range(a.shape[0])], 0)
    return {
        "XTM": np.ascontiguousarray(np.concatenate([inp["x"][b], inp["ctx"][b]], 0)), "CONDT": condT,
        "ADAW": inp["ada_w"], "ADAB": inp["ada_b"], "NMIX": vl(inp["norm_mix_w"]), "NMLP": vl(inp["norm_mlp_w"]),
        "FNW": vec_layout(inp["final_norm_w"]), "W1": inp["mlp_w1"], "W2": inp["mlp_w2"],
        "NAQKV": inp["na_w_qkv"], "NAWO": inp["na_w_o"], "RP": np.ascontiguousarray(rp), "MKN": np.ascontiguousarray(mkn),
        "SSWIN": inp["ssm_w_in"][0], "SSWO": inp["ssm_w_out"][0],
        "CW": np.ascontiguousarray(inp["ssm_conv_w"][0].reshape(5, 48, 128).transpose(2, 1, 0)),
        "CB": np.ascontiguousarray(inp["ssm_conv_b"][0].reshape(48, 128).T),
        "DTB": np.ascontiguousarray(inp["ssm_dt_bias"][0].reshape(128, 1)), "ALOG": np.ascontiguousarray(inp["ssm_a_log"][0].reshape(128, 1)),
        "NWH": np.ascontiguousarray(inp["ssm_norm_w"][0].reshape(64, 64).T),
        "DSK": np.ascontiguousarray(np.broadcast_to(inp["ssm_d"][0][None, :], (64, 64))),
        "TF": tri_f, "TBM": tri_b, "ID": ident, "MKS": masks,
        "DAQKV": inp["da_w_qkv"][0], "DAWO": inp["da_w_o"][0], "COS": C, "SIN": S, "PM": P,
        "LAM": np.ascontiguousarray(inp["da_lambda"][0].reshape(1, 512)), "SW": np.ascontiguousarray(inp["da_subln_w"][0].reshape(2, 128).T),
    }


def kernel(**inputs):
    inp = {k: np.asarray(v) for k, v in inputs.items()}
    NB = 4
    kb = build_fused()
    res = run(kb, [fused_inputs(inp, b) for b in range(NB)], n=NB)
    return np.stack([res.results[b]["OUT"] for b in range(NB)], 0).astype(np.float32)
```

```python
import numpy as np
from contextlib import ExitStack
import concourse.bass as bass
import concourse.mybir as mybir
from concourse.bass_utils import run_bass_kernel_spmd

F32 = mybir.dt.float32
F32R = mybir.dt.float32r
BF16 = mybir.dt.bfloat16
AF = mybir.ActivationFunctionType
ALU = mybir.AluOpType
AX = mybir.AxisListType


class Buf:
    def __init__(self, kb, t, name):
        self.kb, self.t, self.name = kb, t, name
        self.last_w = None
        self.readers = []
        self.dsem = None
        self.dcnt = 0

    def __getitem__(self, idx):
        return self.t[idx]


class KB:
    def __init__(self):
        self.nc = bass.Bass("TRN2", target_bir_lowering=False)
        self.es = ExitStack()
        nc = self.nc
        self.E = {"pe": nc.tensor, "act": nc.scalar, "dve": nc.vector, "pool": nc.gpsimd, "sp": nc.sync}
        self.sems = {}
        self.cnt = {}
        for e in self.E:
            self.sems[e] = self.es.enter_context(nc.semaphore("s_" + e))
            self.cnt[e] = 0
        self.seen = {e: {} for e in self.E}
        self.nbuf = 0
        self.out_deps = []
        self.stack = [self.es]
        self.io = {}
        self.prefix = ""
        self.dpool = []
        self.scope_bufs = []
        self.fused = False
        self.bar = self.es.enter_context(nc.semaphore("s_bar"))
        self.barcnt = 0
        self.nsem = 0

    def push(self, prefix):
        self.prefix = prefix
        self.stack.append(ExitStack())
        self.scope_bufs = []

    def pop(self):
        for b in self.scope_bufs:
            if b.dsem is not None:
                self.dpool.append((b.dsem, b.dcnt))
                b.dsem = None
        self.scope_bufs = []
        self.stack.pop().close()
        self.prefix = ""

    def barrier(self):
        for d in self.out_deps:
            self._wait("sp", d)
        self.out_deps = []
        for e in self.E:
            if e != "sp" and self.cnt[e] > 0:
                self._wait("sp", (e, self.cnt[e]))
        self.nc.sync.sem_inc(self.bar, 1)
        self.barcnt += 1
        for e in self.E:
            if e != "sp":
                self.E[e].wait_ge(self.bar, self.barcnt)

    def sb(self, name, shape, dtype=F32):
        t = self.stack[-1].enter_context(self.nc.sbuf_tensor(self.prefix + name, list(shape), dtype))
        b = Buf(self, t, self.prefix + name)
        self.scope_bufs.append(b)
        return b

    def ps(self, name, shape, dtype=F32):
        t = self.stack[-1].enter_context(self.nc.psum_tensor(self.prefix + name, list(shape), dtype))
        b = Buf(self, t, self.prefix + name)
        self.scope_bufs.append(b)
        return b

    def din(self, name, shape, dtype=F32):
        if name in self.io:
            return self.io[name]
        return self.nc.dram_tensor(name, list(shape), dtype, kind="ExternalInput")

    def dout(self, name, shape, dtype=F32):
        if name in self.io:
            return self.io[name]
        return self.nc.dram_tensor(name, list(shape), dtype, kind="ExternalOutput")

    def scratch(self, name, shape, dtype=F32):
        return self.nc.dram_tensor(name, list(shape), dtype)

    def _semobj(self, key):
        return self.sems[key] if isinstance(key, str) else key

    def _wait(self, eng, dep):
        key, val = dep
        kid = key if isinstance(key, str) else id(key)
        if not isinstance(key, str) and not isinstance(key, Buf):
            kid = id(key)
        if isinstance(key, str):
            assert val <= self.cnt[key], f"wait on unissued inc {key} {val}>{self.cnt[key]}"
        if self.seen[eng].get(kid, 0) >= val:
            return
        self.E[eng].wait_ge(self._semobj(key), val)
        self.seen[eng][kid] = val

    def _sync(self, eng, reads, writes):
        deps = []
        for b in reads:
            if b.last_w is not None:
                deps.append(b.last_w)
        for b in writes:
            if b.last_w is not None:
                deps.append(b.last_w)
            deps.extend(b.readers)
        for d in deps:
            if eng == "pe" and d[0] == "pe":
                continue
            self._wait(eng, d)

    def op(self, eng, fn, reads=(), writes=(), inc=True):
        self._sync(eng, reads, writes)
        inst = fn(self.E[eng])
        if inc:
            inst.then_inc(self.sems[eng], 1)
            self.cnt[eng] += 1
            val = self.cnt[eng]
        else:
            val = self.cnt[eng] + 1
        dep = (eng, val)
        for b in reads:
            b.readers.append(dep)
        for b in writes:
            b.last_w = dep
            b.readers = []
        return inst

    def dram(self, name, shape, dtype=F32):
        t = self.nc.dram_tensor(name, list(shape), dtype)
        return Buf(self, t, name)

    def _dsem(self, b):
        if b.dsem is None:
            if self.dpool:
                b.dsem, b.dcnt = self.dpool.pop()
            else:
                self.nsem += 1
                b.dsem = self.es.enter_context(self.nc.semaphore(f"d_{self.nsem}"))
                b.dcnt = 0
        return b.dsem

    def dma(self, q, out, in_, sbuf=None, load=None, reads=(), writes=(), is_out=False):
        reads, writes = list(reads), list(writes)
        if sbuf is not None:
            if load:
                writes = [sbuf] + writes
            else:
                reads = [sbuf] + reads
                is_out = True
        owner = writes[0] if writes else reads[0]
        self._dsem(owner)
        self._sync(q, reads, writes)
        inst = self.E[q].dma_start(out=out, in_=in_)
        inst.then_inc(owner.dsem, 16)
        owner.dcnt += 16
        dep = (owner.dsem, owner.dcnt)
        for b in reads:
            b.readers.append(dep)
        for b in writes:
            b.last_w = dep
            b.readers = []
        if is_out:
            self.out_deps.append(dep)
        return inst

    def allgather(self, out_buf, in_buf, groups):
        self._dsem(out_buf)
        self._sync("pool", [in_buf], [out_buf])
        inst = self.nc.gpsimd.collective_compute("AllGather", ALU.bypass, replica_groups=groups,
                                                 ins=[in_buf[:]], outs=[out_buf[:]])
        inst.then_inc(out_buf.dsem, 16)
        out_buf.dcnt += 16
        dep = (out_buf.dsem, out_buf.dcnt)
        in_buf.readers.append(dep)
        out_buf.last_w = dep
        out_buf.readers = []
        return inst

    def finish(self, eng="sp"):
        if self.fused:
            return self.barrier()
        for d in self.out_deps:
            self._wait(eng, d)
        for e in self.E:
            if e != eng and self.cnt[e] > 0:
                self._wait(eng, (e, self.cnt[e]))

    def mm(self, out_ap, lhsT, rhs, start, stop, reads, writes, inc=None):
        return self.op("pe", lambda e: e.matmul(out_ap, lhsT, rhs, start=start, stop=stop),
                       reads=reads, writes=writes, inc=stop if inc is None else inc)

    def act(self, out_ap, in_ap, func, reads, writes, bias=None, scale=1.0, eng="act", **kw):
        def f(e):
            kws = dict(kw)
            if bias is not None:
                kws["bias"] = bias
            return e.activation(out=out_ap, in_=in_ap, func=func, scale=scale, **kws)
        return self.op(eng, f, reads=reads, writes=writes)


def run(kb, in_maps, n=8, trace=False):
    res = run_bass_kernel_spmd(kb.nc, in_maps, core_ids=list(range(n)), trace=trace)
    return res


D = 2048
NCORES = 8


def build_ada(kb_=None):
    kb = kb_ if kb_ is not None else KB()
    CW = 1536
    condT = kb.din("condT", [128, 16, 5])
    w = kb.din("w", [4, D, CW])
    b = kb.din("b", [4, CW])
    o = kb.dout("o", [4, 5, CW])
    ct = kb.sb("ct", [128, 16, 5])
    cs = kb.sb("cs", [128, 16, 5])
    ones = kb.sb("ones", [1, 8])
    bt = kb.sb("bt", [1, 4, CW])
    wt = [kb.sb(f"wt{i}", [128, CW]) for i in range(4)]
    res = [kb.sb(f"res{i}", [5, CW]) for i in range(2)]
    pss = [kb.ps(f"ps{i}", [128, 512]) for i in range(6)]
    kb.dma("sp", ct[:], condT.ap(), ct, True)
    kb.dma("sp", bt[:], b.ap().rearrange("(o l) c -> o l c", o=1), bt, True)
    kb.op("dve", lambda e: e.memset(ones[:], 1.0), writes=[ones])
    kb.act(cs[:], ct[:], AF.Silu, [ct], [cs])
    i = 0
    for l in range(4):
        pb = pss[(l % 2) * 3:(l % 2) * 3 + 3]
        for t in range(3):
            kb.mm(pb[t][0:5, :], ones[0:1, 0:5], bt[0:1, l, t * 512:(t + 1) * 512], True, False, [ones, bt], [pb[t]])
        for k in range(16):
            wb_ = wt[i % 4]
            i += 1
            kb.dma("sp" if k % 2 == 0 else "act", wb_[:], w.ap()[l, k * 128:(k + 1) * 128, :], wb_, True)
            for t in range(3):
                kb.mm(pb[t][0:5, :], cs[:, k, :], wb_[:, t * 512:(t + 1) * 512], False, k == 15, [cs, wb_], [pb[t]],
                      inc=(t == 2 or k == 15))
        r = res[l % 2]
        for t in range(3):
            kb.op("dve", lambda e: e.tensor_copy(out=r[:, t * 512:(t + 1) * 512], in_=pb[t][0:5, :]), reads=[pb[t]], writes=[r])
        kb.dma("sp", o.ap()[l], r[:], r, False)
    kb.finish()
    return kb


def run_ada(inp):
    cond = np.concatenate([inp["c"], inp["c_ctx"][None]], 0)
    condT = np.ascontiguousarray(cond.T.reshape(16, 128, 5).transpose(1, 0, 2))
    kb = build_ada()
    maps = []
    for c in range(NCORES):
        sl = slice(1536 * c, 1536 * (c + 1))
        maps.append({"condT": condT, "w": np.ascontiguousarray(inp["ada_w"][:, :, sl]),
                     "b": np.ascontiguousarray(inp["ada_b"][:, sl])})
    res = run(kb, maps)
    mod = np.concatenate([r["o"] for r in res.results], axis=-1)
    return mod


NT = 2304
TB = 768
SUB = 384
NBLK = NT // TB
EPS = 1e-6


def col_ranges(blk):
    lo, hi = blk * TB, (blk + 1) * TB
    out = []
    if lo < 2048:
        out.append((0, min(hi, 2048) - lo, 0))
    if hi > 2048:
        out.append((max(lo, 2048) - lo, hi - lo, 1))
    return out


class TokStage:
    def __init__(self, kb, nmod):
        self.kb = kb
        self.ones = kb.sb("ones", [128, 128])
        self.epsb = kb.sb("epsb", [128, 1])
        kb.op("dve", lambda e: e.memset(self.ones[:], 1.0), writes=[self.ones])
        kb.op("dve", lambda e: e.memset(self.epsb[:], EPS), writes=[self.epsb])
        self.pss = [kb.ps(f"ps{i}", [128, 512]) for i in range(8)]
        self.pi = 0
        self.sq = [kb.sb(f"sq{i}", [128, TB]) for i in range(2)]
        self.rstd = kb.sb("rstd", [128, TB])
        self.tmp = [kb.sb(f"tmp{i}", [128, TB]) for i in range(2)]
        self.ti = 0

    def psum(self):
        p = self.pss[self.pi % 8]
        self.pi += 1
        return p

    def load_mod(self, modT_d, normw_d, idx_shift, idx_scale):
        kb = self.kb
        mt = kb.sb(f"modT{idx_shift}", [128, 2, 6, 16])
        nw = kb.sb(f"nw{idx_shift}", [128, 16])
        kb.dma("sp", mt[:], modT_d.ap(), sbuf=mt, load=True)
        kb.dma("sp", nw[:], normw_d.ap(), sbuf=nw, load=True)
        A = kb.sb(f"A{idx_shift}", [128, 2, 16])
        for w in range(2):
            kb.op("dve", lambda e: e.scalar_tensor_tensor(out=A[:, w, :], in0=mt[:, w, idx_scale, :], scalar=1.0,
                                                          in1=nw[:], op0=ALU.add, op1=ALU.mult),
                  reads=[mt, nw], writes=[A])
        return mt, A

    def norm_mod(self, hk, blk, A, mt, idx_shift, outs):
        kb = self.kb
        pb = [self.psum() for _ in range(TB // SUB)]
        for k in range(16):
            s = self.sq[k % 2]
            kb.act(s[:], hk[k][:], AF.Square, [hk[k]], [s])
            for t in range(TB // SUB):
                kb.mm(pb[t][:, 0:SUB], self.ones[:], s[:, t * SUB:(t + 1) * SUB], k == 0, k == 15, [self.ones, s], [pb[t]],
                      inc=(k == 15 or t == TB // SUB - 1))
        for t in range(TB // SUB):
            sl = slice(t * SUB, (t + 1) * SUB)
            kb.act(self.rstd[:, sl], pb[t][:, 0:SUB], AF.Sqrt, [pb[t], self.epsb], [self.rstd], bias=self.epsb[:], scale=1.0 / D)
        kb.op("dve", lambda e: e.reciprocal(out=self.rstd[:], in_=self.rstd[:]), reads=[self.rstd], writes=[self.rstd])
        for k in range(16):
            tm = self.tmp[self.ti % 2]
            self.ti += 1
            for (lo, hi, w) in col_ranges(blk):
                kb.op("dve", lambda e: e.scalar_tensor_tensor(out=tm[:, lo:hi], in0=hk[k][:, lo:hi], scalar=A[:, w, k:k + 1],
                                                              in1=self.rstd[:, lo:hi], op0=ALU.mult, op1=ALU.mult),
                      reads=[hk[k], A, self.rstd], writes=[tm])
            for (lo, hi, w) in col_ranges(blk):
                kb.act(outs[k][:, lo:hi], tm[:, lo:hi], AF.Identity, [tm, mt], [outs[k]], bias=mt[:, w, idx_shift, k:k + 1])


class WStream:
    def __init__(self, kb, name, nk, nbuf=2):
        self.kb, self.nk = kb, nk
        self.st = [kb.sb(f"{name}_st{i}", [128, nk, 128]) for i in range(nbuf)]
        self.wb = [kb.sb(f"{name}_wb{i}", [128, nk, 128], BF16) for i in range(nbuf)]
        self.i = 0
        self.nbuf = nbuf

    def load(self, w_ap_2d, col0, q="sp", ceng="pool"):
        kb = self.kb
        st, wb = self.st[self.i % self.nbuf], self.wb[self.i % self.nbuf]
        self.i += 1
        src = w_ap_2d[:, col0:col0 + 128].rearrange("(k p) c -> p k c", p=128)
        kb.dma(q, st[:], src, sbuf=st, load=True)
        kb.op(ceng, lambda e: e.tensor_copy(out=wb[:], in_=st[:]), reads=[st], writes=[wb])
        return wb


class WGroup:
    def __init__(self, kb, name, nst=3, nbuf=2):
        self.kb = kb
        self.st = [kb.sb(f"{name}_s{i}", [128, 2048]) for i in range(nst)]
        self.wb = [kb.sb(f"{name}_g{i}", [128, 8192], BF16) for i in range(nbuf)]
        self.i = 0
        self.j = 0

    def load(self, w_ap_2d, nk, gcols, col0, ceng="act"):
        kb = self.kb
        wb = self.wb[self.i % len(self.wb)]
        self.i += 1
        view = wb[:, 0:nk * gcols].rearrange("p (k c) -> p k c", k=nk)
        KK = min(nk, 2048 // gcols)
        for kk in range(0, nk, KK):
            st = self.st[self.j % len(self.st)]
            self.j += 1
            sv = st[:, 0:KK * gcols].rearrange("p (k c) -> p k c", k=KK)
            src = w_ap_2d[kk * 128:(kk + KK) * 128, col0:col0 + gcols].rearrange("(k p) c -> p k c", p=128)
            kb.dma("sp", sv, src, sbuf=st, load=True)
            if ceng == "act":
                kb.act(view[:, kk:kk + KK, :], sv, AF.Copy, [st], [wb])
            else:
                kb.op(ceng, lambda e: e.tensor_copy(out=view[:, kk:kk + KK, :], in_=sv), reads=[st], writes=[wb])
        return wb, view


def build_pre(F, kb_=None):
    kb = kb_ if kb_ is not None else KB()
    hT = kb.din("hT", [16, 128, NT])
    modT = kb.din("modT", [128, 2, 6, 16])
    normw = kb.din("normw", [128, 16])
    W = kb.din("W", [D, F])
    YT = kb.dout("YT", [F // 128, 128, NT])
    ts = TokStage(kb, 1)
    mt, A = ts.load_mod(modT, normw, 0, 1)
    hk = [kb.sb(f"h{k}", [128, TB]) for k in range(16)]
    uk = [kb.sb(f"u{k}", [128, TB], BF16) for k in range(16)]
    ws = WStream(kb, "w", 16, nbuf=3)
    yo = [kb.sb(f"yo{i}", [128, TB]) for i in range(3)]
    for blk in range(NBLK):
        cs = slice(blk * TB, (blk + 1) * TB)
        for k in range(16):
            kb.dma("act" if k % 2 else "sp", hk[k][:], hT.ap()[k][:, cs], sbuf=hk[k], load=True)
        ts.norm_mod(hk, blk, A, mt, 0, uk)
        for m in range(F // 128):
            wb = ws.load(W.ap(), m * 128, q="sp", ceng=("pool" if m % 2 else "dve"))
            y = yo[m % 3]
            for t in range(TB // SUB):
                p = ts.psum()
                for k in range(16):
                    kb.mm(p[:, 0:SUB], wb[:, k, :], uk[k][:, t * SUB:(t + 1) * SUB], k == 0, k == 15, [wb, uk[k]], [p])
                kb.act(y[:, t * SUB:(t + 1) * SUB], p[:, 0:SUB], AF.Copy, [p], [y])
            kb.dma("act", YT.ap()[m][:, cs], y[:], sbuf=y, load=False)
    kb.finish()
    return kb


def to_fm(a):
    t, f = a.shape
    return np.ascontiguousarray(a.T.reshape(f // 128, 128, t))


def from_fm(a):
    c, p, t = a.shape
    return np.ascontiguousarray(a.reshape(c * p, t).T)


def mod_layout(mod_l, b):
    m = np.stack([mod_l[b], mod_l[4]], 0).reshape(2, 6, 16, 128)
    return np.ascontiguousarray(m.transpose(3, 0, 1, 2))


def vec_layout(v):
    return np.ascontiguousarray(v.reshape(-1, 128).T)


def build_post(Fin, last, kb_=None):
    kb = kb_ if kb_ is not None else KB()
    NKI = Fin // 128
    G = 4
    hT = kb.din("hT", [16, 128, NT])
    OT = kb.din("OT", [NKI, 128, NT])
    modT = kb.din("modT", [128, 2, 6, 16])
    normw = kb.din("normw", [128, 16])
    Wo = kb.din("Wo", [Fin, D])
    W1 = kb.din("W1", [D, 4 * D])
    W2 = kb.din("W2", [4 * D, D])
    if last:
        fnw = kb.din("fnw", [128, 16])
    HO = kb.dout("HO", [16, 128, NT])
    ts = TokStage(kb, 1)
    mt, A = ts.load_mod(modT, normw, 3, 4)
    hk = [kb.sb(f"h{k}", [128, TB]) for k in range(16)]
    ob = [kb.sb(f"ob{k}", [128, TB], BF16) for k in range(max(NKI, 16 + G))]
    ost = [kb.sb(f"ost{i}", [128, TB]) for i in range(2 if NKI <= 16 else 1)]
    wg = WGroup(kb, "wg", nst=2)
    GO = 8192 // NKI
    w2st = [kb.sb(f"w2st{i}", [128, D]) for i in range(2 if NKI <= 16 else 1)]
    w2b = [kb.sb(f"w2b{i}", [128, D], BF16) for i in range(2 * G)]
    if last:
        fw = kb.sb("fw", [128, 16])
        kb.dma("sp", fw[:], fnw.ap(), sbuf=fw, load=True)
        ones16 = kb.sb("ones16", [128, 2, 16])
        kb.op("dve", lambda e: e.memset(ones16[:], 0.0), writes=[ones16])
        zer = kb.sb("zer", [128, 2, 6, 16])
        kb.op("dve", lambda e: e.memset(zer[:], 0.0), writes=[zer])
        fA = kb.sb("fA", [128, 2, 16])
        for w in range(2):
            kb.op("dve", lambda e: e.tensor_copy(out=fA[:, w, :], in_=fw[:]), reads=[fw], writes=[fA])
    NS = TB // SUB
    i2 = 0
    for blk in range(NBLK):
        cs = slice(blk * TB, (blk + 1) * TB)
        rngs = col_ranges(blk)
        for k in range(16):
            kb.dma("act" if k % 2 else "sp", hk[k][:], hT.ap()[k][:, cs], sbuf=hk[k], load=True)
        for k in range(NKI):
            st = ost[k % len(ost)]
            kb.dma("sp", st[:], OT.ap()[k][:, cs], sbuf=st, load=True)
            kb.op("pool", lambda e: e.tensor_copy(out=ob[k][:], in_=st[:]), reads=[st], writes=[ob[k]])
        for m in range(16):
            if (m * 128) % GO == 0:
                wb, wview = wg.load(Wo.ap(), NKI, GO, m * 128, ceng="act")
            mo = (m * 128) % GO
            for t in range(NS):
                p = ts.psum()
                for k in range(NKI):
                    kb.mm(p[:, 0:SUB], wview[:, k, mo:mo + 128], ob[k][:, t * SUB:(t + 1) * SUB], k == 0, k == NKI - 1, [wb, ob[k]], [p])
                for (lo, hi, w) in rngs:
                    a, b_ = max(lo, t * SUB), min(hi, (t + 1) * SUB)
                    if a >= b_:
                        continue
                    kb.op("dve", lambda e: e.scalar_tensor_tensor(out=hk[m][:, a:b_], in0=p[:, a - t * SUB:b_ - t * SUB],
                                                                  scalar=mt[:, w, 2, m:m + 1], in1=hk[m][:, a:b_],
                                                                  op0=ALU.mult, op1=ALU.add),
                          reads=[p, mt, hk[m]], writes=[hk[m]])
        ts.norm_mod(hk, blk, A, mt, 3, ob[0:16])
        for g in range(4 * D // 128 // G):
            av = ob[16:16 + G]
            wb, wview = wg.load(W1.ap(), 16, 512, g * 512, ceng="act")
            for j in range(G):
                jj = g * G + j
                for t in range(NS):
                    p = ts.psum()
                    for k in range(16):
                        kb.mm(p[:, 0:SUB], wview[:, k, j * 128:(j + 1) * 128], ob[k][:, t * SUB:(t + 1) * SUB], k == 0, k == 15, [wb, ob[k]], [p])
                    tm = ts.tmp[ts.ti % 2]
                    ts.ti += 1
                    kb.act(tm[:, 0:SUB], p[:, 0:SUB], AF.Relu, [p], [tm])
                    kb.op("pool", lambda e: e.tensor_tensor(out=av[j][:, t * SUB:(t + 1) * SUB], in0=tm[:, 0:SUB], in1=tm[:, 0:SUB],
                                                            op=ALU.mult), reads=[tm], writes=[av[j]])
                st = w2st[i2 % len(w2st)]
                wv = w2b[i2 % (2 * G)]
                i2 += 1
                kb.dma("act", st[:], W2.ap()[jj * 128:(jj + 1) * 128, :], sbuf=st, load=True)
                kb.op("dve", lambda e: e.tensor_copy(out=wv[:], in_=st[:]), reads=[st], writes=[wv])
            wvs = [w2b[(i2 - G + j) % (2 * G)] for j in range(G)]
            for m in range(16):
                for t in range(NS):
                    p = ts.psum()
                    for j in range(G):
                        kb.mm(p[:, 0:SUB], wvs[j][:, m * 128:(m + 1) * 128], av[j][:, t * SUB:(t + 1) * SUB], j == 0, j == G - 1,
                              [wvs[j], av[j]], [p])
                    for (lo, hi, w) in rngs:
                        a, b_ = max(lo, t * SUB), min(hi, (t + 1) * SUB)
                        if a >= b_:
                            continue
                        kb.op("dve", lambda e: e.scalar_tensor_tensor(out=hk[m][:, a:b_], in0=p[:, a - t * SUB:b_ - t * SUB],
                                                                      scalar=mt[:, w, 5, m:m + 1], in1=hk[m][:, a:b_],
                                                                      op0=ALU.mult, op1=ALU.add),
                              reads=[p, mt, hk[m]], writes=[hk[m]])
        if last:
            fo = ts.tmp
            fouts = []
            class _O:
                pass
            _final_norm(kb, ts, hk, blk, fA, zer, HO, cs, fo)
        else:
            for k in range(16):
                kb.dma("sp", HO.ap()[k][:, cs], hk[k][:], sbuf=hk[k], load=False)
    kb.finish()
    return kb


def _final_norm(kb, ts, hk, blk, fA, zer, HO, cs, fo):
    pb = [ts.psum() for _ in range(TB // SUB)]
    for k in range(16):
        s = ts.sq[k % 2]
        kb.act(s[:], hk[k][:], AF.Square, [hk[k]], [s])
        for t in range(TB // SUB):
            kb.mm(pb[t][:, 0:SUB], ts.ones[:], s[:, t * SUB:(t + 1) * SUB], k == 0, k == 15, [ts.ones, s], [pb[t]],
                  inc=(k == 15 or t == TB // SUB - 1))
    for t in range(TB // SUB):
        sl = slice(t * SUB, (t + 1) * SUB)
        kb.act(ts.rstd[:, sl], pb[t][:, 0:SUB], AF.Sqrt, [pb[t], ts.epsb], [ts.rstd], bias=ts.epsb[:], scale=1.0 / D)
    kb.op("dve", lambda e: e.reciprocal(out=ts.rstd[:], in_=ts.rstd[:]), reads=[ts.rstd], writes=[ts.rstd])
    for k in range(16):
        o = fo[k % 2]
        kb.op("dve", lambda e: e.scalar_tensor_tensor(out=o[:], in0=hk[k][:], scalar=fA[:, 0, k:k + 1], in1=ts.rstd[:],
                                                      op0=ALU.mult, op1=ALU.mult), reads=[hk[k], fA, ts.rstd], writes=[o])
        kb.dma("sp", HO.ap()[k][:, cs], o[:], sbuf=o, load=False)


def na_tables(rpb):
    col = np.arange(64)
    cs = np.clip(col - 8, 0, 48)
    cmask = (col[None, :] >= cs[:, None]) & (col[None, :] < cs[:, None] + 16)
    dc = np.clip(col[None, :] - col[:, None], -15, 15) + 15
    t = rpb[:, :, dc]
    rpbT = np.ascontiguousarray(t.transpose(0, 3, 1, 2))
    mask = np.ascontiguousarray(np.broadcast_to(cmask.T[:, None, :], (64, 15, 64)).astype(np.float32))
    return rpbT, mask


def build_na(kb_=None):
    kb = kb_ if kb_ is not None else KB()
    H = 16
    SC = 128 ** -0.5
    QK = kb.din("QK", [32, 128, NT])
    Vr = kb.din("Vr", [64, 36, D])
    Vc = kb.din("Vc", [128, 2, D])
    RP = kb.din("RP", [H, 64, 15 * 64])
    MK = kb.din("MK", [64, 15 * 64])
    OT = kb.dout("OT", [H, 128, NT])
    ones = kb.sb("ones", [128, 128], BF16)
    kb.op("dve", lambda e: e.memset(ones[:], 1.0), writes=[ones])
    mk = kb.sb("mk", [64, 960])
    kb.dma("sp", mk[:], MK.ap(), sbuf=mk, load=True)
    qst = [kb.sb(f"qst{i}", [128, NT]) for i in range(2)]
    qb = [kb.sb(f"qb{i}", [128, NT], BF16) for i in range(2)]
    kbb = [kb.sb(f"kbb{i}", [128, NT], BF16) for i in range(2)]
    vst = kb.sb("vst", [64, 36, 128])
    vb = [kb.sb(f"vb{i}", [64, 36, 128], BF16) for i in range(2)]
    vcst = kb.sb("vcst", [128, 2, 128])
    vcb = [kb.sb(f"vcb{i}", [128, 2, 128], BF16) for i in range(2)]
    rst = kb.sb("rst", [64, 960])
    eb = [kb.sb(f"eb{i}", [64, 960]) for i in range(2)]
    E = [kb.sb(f"E{i}", [64, 512]) for i in range(2)]
    E2 = [kb.sb(f"E2{i}", [64, 512], BF16) for i in range(3)]
    Ec = [kb.sb(f"Ec{i}", [128, 512], BF16) for i in range(3)]
    rec = [kb.sb(f"rec{i}", [128, 512]) for i in range(2)]
    oo = [kb.sb(f"oo{i}", [128, 512]) for i in range(2)]
    PO = [kb.ps(f"PO{i}", [128, 512]) for i in range(2)]
    PS = [kb.ps(f"PS{i}", [128, 512]) for i in range(2)]
    PT = [kb.ps(f"PT{i}", [128, 512]) for i in range(2)]
    PC = [kb.ps(f"PC{i}", [128, 512]) for i in range(2)]
    ie = 0
    for h in range(H):
        q_, k_, v_, vc_, eb_ = qb[h % 2], kbb[h % 2], vb[h % 2], vcb[h % 2], eb[h % 2]
        kb.dma("sp", qst[0][:], QK.ap()[h], sbuf=qst[0], load=True)
        kb.op("pool", lambda e: e.tensor_copy(out=q_[:], in_=qst[0][:]), reads=[qst[0]], writes=[q_])
        kb.dma("sp", qst[1][:], QK.ap()[16 + h], sbuf=qst[1], load=True)
        kb.op("pool", lambda e: e.tensor_copy(out=k_[:], in_=qst[1][:]), reads=[qst[1]], writes=[k_])
        kb.dma("act", vst[:], Vr.ap()[:, :, h * 128:(h + 1) * 128], sbuf=vst, load=True)
        kb.op("pool", lambda e: e.tensor_copy(out=v_[:], in_=vst[:]), reads=[vst], writes=[v_])
        kb.dma("act", vcst[:], Vc.ap()[:, :, h * 128:(h + 1) * 128], sbuf=vcst, load=True)
        kb.op("pool", lambda e: e.tensor_copy(out=vc_[:], in_=vcst[:]), reads=[vcst], writes=[vc_])
        kb.dma("sp", rst[:], RP.ap()[h], sbuf=rst, load=True)
        kb.act(rst[:], rst[:], AF.Exp, [rst], [rst])
        kb.op("dve", lambda e: e.tensor_tensor(out=eb_[:], in0=rst[:], in1=mk[:], op=ALU.mult), reads=[rst, mk], writes=[eb_])
        def stage1(r):
            nonlocal ie
            rs = min(max(r - 4, 0), 24)
            dr0 = rs - r + 7
            pt, pc = PT[ie % 2], PC[ie % 2]
            e1, e2, ec = E[ie % 2], E2[ie % 3], Ec[ie % 3]
            ie += 1
            qs = q_[:, r * 64:(r + 1) * 64]
            for j in range(8):
                kb.mm(pt[0:64, j * 64:(j + 1) * 64], k_[:, (rs + j) * 64:(rs + j + 1) * 64], qs, True, True, [k_, q_], [pt],
                      inc=(j == 7))
            for t in range(2):
                kb.mm(pc[:, t * 64:(t + 1) * 64], k_[:, 2048 + t * 128:2048 + (t + 1) * 128], qs, True, True, [k_, q_], [pc],
                      inc=(t == 1))
            kb.act(e1[:], pt[0:64, :], AF.Exp, [pt], [e1], scale=SC)
            kb.op("pool" if ie % 2 else "dve",
                  lambda e: e.tensor_tensor(out=e2[:], in0=e1[:], in1=eb_[:, dr0 * 64:dr0 * 64 + 512], op=ALU.mult),
                  reads=[e1, eb_], writes=[e2])
            kb.act(ec[:, 0:128], pc[:, 0:128], AF.Exp, [pc], [ec], scale=SC)
            return (r, rs, e2, ec)

        def stage2(item):
            r, rs, e2, ec = item
            rg, rr = r // 8, r % 8
            po, ps_ = PO[rg % 2], PS[rg % 2]
            cs = slice(rr * 64, (rr + 1) * 64)
            for j in range(8):
                kb.mm(po[:, cs], v_[0:64, rs + j, :], e2[:, j * 64:(j + 1) * 64], j == 0, False, [v_, e2], [po])
            for t in range(2):
                kb.mm(po[:, cs], vc_[:, t, :], ec[:, t * 64:(t + 1) * 64], False, t == 1, [vc_, ec], [po], inc=False)
            for j in range(8):
                kb.mm(ps_[:, cs], ones[0:64, :], e2[:, j * 64:(j + 1) * 64], j == 0, False, [ones, e2], [ps_])
            for t in range(2):
                kb.mm(ps_[:, cs], ones[:, :], ec[:, t * 64:(t + 1) * 64], False, t == 1, [ones, ec], [ps_], inc=(t == 1))
            if rr == 7:
                finish_rg(rg, 512, rg * 512)

        def finish_rg(rg, ncol, c0):
            po, ps_ = PO[rg % 2], PS[rg % 2]
            rc, o_ = rec[rg % 2], oo[rg % 2]
            kb.op("dve", lambda e: e.reciprocal(out=rc[:, 0:ncol], in_=ps_[:, 0:ncol]), reads=[ps_], writes=[rc])
            kb.op("dve", lambda e: e.tensor_tensor(out=o_[:, 0:ncol], in0=po[:, 0:ncol], in1=rc[:, 0:ncol], op=ALU.mult),
                  reads=[po, rc], writes=[o_])
            kb.dma("sp", OT.ap()[h][:, c0:c0 + ncol], o_[:, 0:ncol], sbuf=o_, load=False)

        pend = None
        for r in range(32):
            cur = stage1(r)
            if pend is not None:
                stage2(pend)
            pend = cur
        stage2(pend)
        rg = 4
        po, ps_ = PO[rg % 2], PS[rg % 2]
        pc = PC[ie % 2]
        ec = Ec[ie % 3]
        ie += 1
        qs = q_[:, 2048:2304]
        for t in range(2):
            kb.mm(pc[:, t * 256:(t + 1) * 256], k_[:, 2048 + t * 128:2048 + (t + 1) * 128], qs, True, True, [k_, q_], [pc],
                  inc=(t == 1))
        kb.act(ec[:], pc[:], AF.Exp, [pc], [ec], scale=SC)
        for t in range(2):
            kb.mm(po[:, 0:256], vc_[:, t, :], ec[:, t * 256:(t + 1) * 256], t == 0, t == 1, [vc_, ec], [po], inc=False)
        for t in range(2):
            kb.mm(ps_[:, 0:256], ones[:, :], ec[:, t * 256:(t + 1) * 256], t == 0, t == 1, [ones, ec], [ps_], inc=(t == 1))
        finish_rg(rg, 256, 2048)
    kb.finish()
    return kb


def na_inputs(Y, rpb):
    YT = to_fm(Y)
    V = Y[:, 4096:6144]
    Vr = np.ascontiguousarray(V.reshape(36, 64, D).transpose(1, 0, 2))
    Vc = np.ascontiguousarray(V[2048:].reshape(2, 128, D).transpose(1, 0, 2))
    rpbT, mask = na_tables(rpb)
    return {"QK": np.ascontiguousarray(YT[0:32]), "Vr": Vr, "Vc": Vc,
            "RP": rpbT.reshape(16, 64, 960), "MK": mask.reshape(64, 960)}


def rope_tables():
    t = np.arange(2048)
    pos = np.stack([t // 64, t % 64], -1).astype(np.float32)
    inv = (10000.0 ** (-np.arange(32, dtype=np.float32) / 32)).astype(np.float32)
    ang = pos[:, :, None] * inv
    cos, sin = np.cos(ang).astype(np.float32), np.sin(ang).astype(np.float32)
    C = np.zeros((128, 2048), np.float32)
    S = np.zeros((128, 2048), np.float32)
    for a in range(2):
        for b in range(2):
            sl = slice(a * 64 + b * 32, a * 64 + b * 32 + 32)
            C[sl] = cos[:, a, :].T
            S[sl] = (-1.0 if b == 0 else 1.0) * sin[:, a, :].T
    P = np.zeros((128, 128), np.float32)
    for d in range(128):
        P[d ^ 32, d] = 1.0
    return C, S, P


def build_da(lambda_init, kb_=None):
    kb = kb_ if kb_ is not None else KB()
    SC = 128 ** -0.5
    QK = kb.din("QK", [32, 128, NT])
    Vt = kb.din("Vt", [128, 18, D])
    COS = kb.din("COS", [128, 2048])
    SIN = kb.din("SIN", [128, 2048])
    PM = kb.din("PM", [128, 128])
    LAM = kb.din("LAM", [1, 512])
    SW = kb.din("SW", [128, 2])
    OT = kb.dout("OT", [16, 128, NT])
    onesb = kb.sb("onesb", [128, 128], BF16)
    onesf = kb.sb("onesf", [128, 128])
    kb.op("dve", lambda e: e.memset(onesb[:], 1.0), writes=[onesb])
    kb.op("dve", lambda e: e.memset(onesf[:], 1.0), writes=[onesf])
    eps5 = kb.sb("eps5", [128, 1])
    kb.op("dve", lambda e: e.memset(eps5[:], 1e-5), writes=[eps5])
    cos = kb.sb("cos", [128, 2048]); sin = kb.sb("sin", [128, 2048]); pm = kb.sb("pm", [128, 128])
    kb.dma("sp", cos[:], COS.ap(), sbuf=cos, load=True)
    kb.dma("act", sin[:], SIN.ap(), sbuf=sin, load=True)
    kb.dma("sp", pm[:], PM.ap(), sbuf=pm, load=True)
    lam = kb.sb("lam", [1, 512]); sw = kb.sb("sw", [128, 2])
    kb.dma("sp", lam[:], LAM.ap(), sbuf=lam, load=True)
    kb.dma("sp", sw[:], SW.ap(), sbuf=sw, load=True)
    kb.op("dve", lambda e: e.tensor_scalar(out=sw[:], in0=sw[:], scalar1=1.0 - lambda_init, scalar2=None, op0=ALU.mult),
          reads=[sw], writes=[sw])
    lp = kb.sb("lp", [1, 256]); ls = kb.sb("ls", [1, 2]); lf = kb.sb("lf", [1, 2]); nl = kb.sb("nl", [128, 2])
    kb.op("dve", lambda e: e.tensor_tensor(out=lp[:, 0:128], in0=lam[:, 0:128], in1=lam[:, 128:256], op=ALU.mult), reads=[lam], writes=[lp])
    kb.op("dve", lambda e: e.tensor_tensor(out=lp[:, 128:256], in0=lam[:, 256:384], in1=lam[:, 384:512], op=ALU.mult), reads=[lam], writes=[lp])
    kb.op("dve", lambda e: e.reduce_sum(out=ls[:], in_=lp[:].rearrange("o (a b) -> o a b", a=2), axis=AX.X), reads=[lp], writes=[ls])
    kb.act(ls[:], ls[:], AF.Exp, [ls], [ls])
    for c in range(2):
        kb.op("dve", lambda e: e.scalar_tensor_tensor(out=lf[:, c:c + 1], in0=ls[:, 1:2], scalar=-lambda_init, in1=ls[:, 0:1],
                                                      op0=ALU.add, op1=ALU.subtract), reads=[ls], writes=[lf])
    PSb = [kb.ps(f"P{i}", [128, 512]) for i in range(8)]
    kb.mm(PSb[0][:, 0:2], onesf[0:1, :], lf[:], True, True, [onesf, lf], [PSb[0]])
    kb.op("dve", lambda e: e.tensor_copy(out=nl[:], in_=PSb[0][:, 0:2]), reads=[PSb[0]], writes=[nl])
    st = [kb.sb(f"st{i}", [128, NT]) for i in range(2)]
    t1 = [kb.sb(f"t1{i}", [128, 512]) for i in range(2)]
    t2 = [kb.sb(f"t2{i}", [128, 512]) for i in range(2)]
    qb = [kb.sb(f"qb{i}", [128, NT], BF16) for i in range(4)]
    kbb = [kb.sb(f"kbb{i}", [128, NT], BF16) for i in range(4)]
    vst = kb.sb("vst", [128, 18, 256])
    vb = [kb.sb(f"vb{i}", [128, 18, 256], BF16) for i in range(2)]
    E = [kb.sb(f"E{i}", [128, 512], BF16) for i in range(3)]
    rc = [kb.sb(f"rc{i}", [128, 512]) for i in range(2)]
    a0 = kb.sb("a0", [128, 512]); a1 = kb.sb("a1", [128, 512])
    oe = [kb.sb(f"oe{i}", [128, 512]) for i in range(2)]
    sq = kb.sb("sq", [128, 512]); rs_ = kb.sb("rs_", [128, 512])
    fo = [kb.sb(f"fo{i}", [128, 512]) for i in range(2)]
    ist = 0; ie = 0; it = 0

    def load_rope(dst, chunk):
        nonlocal ist, it
        s = st[ist % 2]; ist += 1
        kb.dma("sp", s[:], QK.ap()[chunk], sbuf=s, load=True)
        for b4 in range(4):
            cs = slice(b4 * 512, (b4 + 1) * 512)
            p = PSb[6 + (it % 2)]
            a, b_ = t1[it % 2], t2[it % 2]; it += 1
            kb.mm(p[:], pm[:], s[:, cs], True, True, [pm, s], [p])
            kb.op("dve", lambda e: e.tensor_tensor(out=a[:], in0=s[:, cs], in1=cos[:, cs], op=ALU.mult), reads=[s, cos], writes=[a])
            kb.op("dve", lambda e: e.tensor_tensor(out=b_[:], in0=p[:], in1=sin[:, cs], op=ALU.mult), reads=[p, sin], writes=[b_])
            kb.op("pool", lambda e: e.tensor_tensor(out=dst[:, cs], in0=a[:], in1=b_[:], op=ALU.add), reads=[a, b_], writes=[dst])
        kb.op("pool", lambda e: e.tensor_copy(out=dst[:, 2048:NT], in_=s[:, 2048:NT]), reads=[s], writes=[dst])

    for h in range(8):
        par = h % 2
        for c in range(2):
            load_rope(qb[par * 2 + c], 2 * h + c)
            load_rope(kbb[par * 2 + c], 16 + 2 * h + c)
        v_ = vb[par]
        kb.dma("act", vst[:], Vt.ap()[:, :, h * 256:(h + 1) * 256], sbuf=vst, load=True)
        kb.op("pool", lambda e: e.tensor_copy(out=v_[:], in_=vst[:]), reads=[vst], writes=[v_])
        for qblk in range(5):
            c0 = qblk * 512
            ncol = 512 if qblk < 4 else 256
            kts = list(range(18)) if qblk < 4 else [16, 17]
            for c in range(2):
                q_, k_ = qb[par * 2 + c], kbb[par * 2 + c]
                po = [PSb[c * 3], PSb[c * 3 + 1]]
                ps_ = PSb[c * 3 + 2]
                def pv(i, kt, e_):
                    fst, lst = i == 0, i == len(kts) - 1
                    for e2 in range(2):
                        kb.mm(po[e2][:, 0:ncol], v_[:, kt, e2 * 128:(e2 + 1) * 128], e_[:, 0:ncol], fst, lst, [v_, e_], [po[e2]], inc=False)
                    kb.mm(ps_[:, 0:ncol], onesb[:], e_[:, 0:ncol], fst, lst, [onesb, e_], [ps_], inc=True)

                pend = None
                for i, kt in enumerate(kts):
                    pt = PSb[6 + (it % 2)]; it += 1
                    e_ = E[ie % 3]; ie += 1
                    kb.mm(pt[:, 0:ncol], k_[:, kt * 128:(kt + 1) * 128], q_[:, c0:c0 + ncol], True, True, [k_, q_], [pt])
                    kb.act(e_[:, 0:ncol], pt[:, 0:ncol], AF.Exp, [pt], [e_], scale=SC)
                    if pend is not None:
                        pv(*pend)
                    pend = (i, kt, e_)
                pv(*pend)
            n = slice(0, ncol)
            for c in range(2):
                kb.op("dve", lambda e: e.reciprocal(out=rc[c][:, n], in_=PSb[c * 3 + 2][:, n]), reads=[PSb[c * 3 + 2]], writes=[rc[c]])
            pss = PSb[6 + (it % 2)]; it += 1
            for e2 in range(2):
                kb.op("dve", lambda e: e.tensor_tensor(out=a0[:, n], in0=PSb[e2][:, n], in1=rc[0][:, n], op=ALU.mult),
                      reads=[PSb[e2], rc[0]], writes=[a0])
                kb.op("dve", lambda e: e.tensor_tensor(out=a1[:, n], in0=PSb[3 + e2][:, n], in1=rc[1][:, n], op=ALU.mult),
                      reads=[PSb[3 + e2], rc[1]], writes=[a1])
                kb.op("dve", lambda e: e.scalar_tensor_tensor(out=oe[e2][:, n], in0=a1[:, n], scalar=nl[:, 0:1], in1=a0[:, n],
                                                               op0=ALU.mult, op1=ALU.add), reads=[a1, nl, a0], writes=[oe[e2]])
                kb.act(sq[:, n], oe[e2][:, n], AF.Square, [oe[e2]], [sq])
                kb.mm(pss[:, n], onesf[:], sq[:, n], e2 == 0, e2 == 1, [onesf, sq], [pss], inc=True)
            kb.act(rs_[:, n], pss[:, n], AF.Sqrt, [pss, eps5], [rs_], bias=eps5[:], scale=1.0 / 256)
            kb.op("dve", lambda e: e.reciprocal(out=rs_[:, n], in_=rs_[:, n]), reads=[rs_], writes=[rs_])
            for e2 in range(2):
                kb.op("dve", lambda e: e.scalar_tensor_tensor(out=fo[e2][:, n], in0=oe[e2][:, n], scalar=sw[:, e2:e2 + 1], in1=rs_[:, n],
                                                              op0=ALU.mult, op1=ALU.mult), reads=[oe[e2], sw, rs_], writes=[fo[e2]])
                kb.dma("sp", OT.ap()[2 * h + e2][:, c0:c0 + ncol], fo[e2][:, n], sbuf=fo[e2], load=False)
    kb.finish()
    return kb


def da_inputs(Y, lam, subw):
    YT = to_fm(Y)
    V = Y[:, 4096:6144]
    Vt = np.ascontiguousarray(V.reshape(18, 128, D).transpose(1, 0, 2))
    C, S, P = rope_tables()
    return {"QK": np.ascontiguousarray(YT[0:32]), "Vt": Vt, "COS": C, "SIN": S, "PM": P,
            "LAM": np.ascontiguousarray(lam.reshape(1, 512)), "SW": np.ascontiguousarray(subw.reshape(2, 128).T)}


def build_ssm_a(kb_=None):
    kb = kb_ if kb_ is not None else KB()
    XI = kb.din("XI", [49, 128, NT])
    CW = kb.din("CW", [128, 48, 5])
    CB = kb.din("CB", [128, 48])
    DTB = kb.din("DTB", [128, 1])
    ALOG = kb.din("ALOG", [128, 1])
    XO = kb.dout("XO", [48, 128, NT])
    DA_ = kb.dout("DA", [2, 128, NT])
    cw = kb.sb("cw", [128, 48, 5]); cb = kb.sb("cb", [128, 48]); dtb = kb.sb("dtb", [128, 1]); al = kb.sb("al", [128, 1])
    one = kb.sb("one", [128, 1])
    kb.op("dve", lambda e: e.memset(one[:], 1.0), writes=[one])
    for t_, d_ in ((cw, CW), (cb, CB), (dtb, DTB), (al, ALOG)):
        kb.dma("sp", t_[:], d_.ap(), sbuf=t_, load=True)
    kb.act(al[:], al[:], AF.Exp, [al], [al])
    kb.op("dve", lambda e: e.tensor_scalar(out=al[:], in0=al[:], scalar1=-1.0, scalar2=None, op0=ALU.mult), reads=[al], writes=[al])
    xin = [kb.sb(f"xin{i}", [128, NT]) for i in range(2)]
    acc = [kb.sb(f"acc{i}", [128, NT]) for i in range(2)]
    for c in range(48):
        xi, ac = xin[c % 2], acc[c % 2]
        kb.dma("sp" if c % 2 else "act", xi[:], XI.ap()[c], sbuf=xi, load=True)
        kb.op("dve", lambda e: e.tensor_scalar(out=ac[:], in0=xi[:], scalar1=cw[:, c, 2:3], scalar2=cb[:, c:c + 1],
                                               op0=ALU.mult, op1=ALU.add), reads=[xi, cw, cb], writes=[ac])
        for (lo, hi) in ((0, 2048), (2048, NT)):
            for k in (0, 1, 3, 4):
                o = k - 2
                a, b_ = lo + max(0, -o), hi - max(0, o)
                kb.op("dve", lambda e: e.scalar_tensor_tensor(out=ac[:, a:b_], in0=xi[:, a + o:b_ + o], scalar=cw[:, c, k:k + 1],
                                                              in1=ac[:, a:b_], op0=ALU.mult, op1=ALU.add),
                      reads=[xi, cw, ac], writes=[ac])
        kb.act(ac[:], ac[:], AF.Silu, [ac], [ac])
        kb.dma("sp", XO.ap()[c], ac[:], sbuf=ac, load=False)
    xi, ac = xin[0], acc[0]
    kb.dma("sp", xi[:], XI.ap()[48], sbuf=xi, load=True)
    kb.act(xi[:], xi[:], AF.Exp, [xi, dtb], [xi], bias=dtb[:])
    kb.act(xi[:], xi[:], AF.Ln, [xi, one], [xi], bias=one[:])
    kb.dma("sp", DA_.ap()[0], xi[:], sbuf=xi, load=False)
    kb.op("dve", lambda e: e.tensor_scalar(out=ac[:], in0=xi[:], scalar1=al[:, 0:1], scalar2=None, op0=ALU.mult), reads=[xi, al], writes=[ac])
    kb.dma("sp", DA_.ap()[1], ac[:], sbuf=ac, load=False)
    kb.finish()
    return kb


ORD_F = [16, 17] + list(range(16))
ORD_B = [17, 16] + list(range(15, -1, -1))


def ssm_consts():
    i = np.arange(128)
    tri_f = (i[:, None] <= i[None, :]).astype(np.float32)
    tri_b = (i[:, None] >= i[None, :]).astype(np.float32)
    t = np.arange(512)
    mk = np.zeros((8, 128, 512), np.float32)
    for q in range(4):
        mk[q] = ((t[None, :] - 128 * q) >= i[:, None])
        mk[4 + q] = ((t[None, :] - 128 * q) <= i[:, None])
    return tri_f, tri_b, np.eye(128, dtype=np.float32), np.ascontiguousarray(mk.transpose(1, 0, 2))


def build_ssm_b(kb_=None):
    kb = kb_ if kb_ is not None else KB()
    XT_ = kb.din("XTOK", [128, 18, 4096])
    BT = kb.din("BT", [8, 128, NT]); CT = kb.din("CT", [8, 128, NT])
    DTT = kb.din("DTT", [128, 18, 128]); ATT = kb.din("ATT", [128, 18, 128])
    XH = kb.din("XH", [64, 64, NT]); ZH = kb.din("ZH", [64, 64, NT])
    NWH = kb.din("NWH", [64, 64]); DSK = kb.din("DSK", [64, 64])
    TF = kb.din("TF", [128, 128]); TBm = kb.din("TB", [128, 128]); ID = kb.din("ID", [128, 128]); MK = kb.din("MK", [128, 8, 512])
    OT = kb.dout("OT", [64, 64, NT])
    onesf = kb.sb("onesf", [128, 128]); kb.op("dve", lambda e: e.memset(onesf[:], 1.0), writes=[onesf])
    eps = kb.sb("eps", [128, 1]); kb.op("dve", lambda e: e.memset(eps[:], EPS), writes=[eps])
    tf = kb.sb("tf", [128, 128]); tb_ = kb.sb("tb_", [128, 128]); idt = kb.sb("idt", [128, 128]); mk = kb.sb("mk", [128, 8, 512])
    dtt = kb.sb("dtt", [128, 18, 128]); att = kb.sb("att", [128, 18, 128]); nwh = kb.sb("nwh", [64, 64]); dsk = kb.sb("dsk", [64, 64])
    for t_, d_ in ((tf, TF), (tb_, TBm), (idt, ID), (mk, MK), (dtt, DTT), (att, ATT), (nwh, NWH), (dsk, DSK)):
        kb.dma("sp", t_[:], d_.ap(), sbuf=t_, load=True)
    P = [kb.ps(f"P{i}", [128, 512]) for i in range(8)]
    cum = [kb.sb(f"cum{d}", [128, 18, 128]) for d in range(2)]
    cumT = [kb.sb(f"cumT{d}", [128, NT]) for d in range(2)]
    ip = 0
    for d, (order, tri) in enumerate(((ORD_F, tf), (ORD_B, tb_))):
        for oi, n in enumerate(order):
            p = P[ip % 2]; ip += 1
            kb.mm(p[:, 0:128], tri[:], att[:, n, :], True, oi == 0, [tri, att], [p], inc=(oi == 0))
            for mi, m in enumerate(order[:oi]):
                kb.mm(p[:, 0:128], onesf[:], att[:, m, :], False, mi == oi - 1, [onesf, att], [p], inc=(mi == oi - 1))
            kb.op("dve", lambda e: e.tensor_copy(out=cum[d][:, n, :], in_=p[:, 0:128]), reads=[p], writes=[cum[d]])
        for n in range(18):
            p = P[ip % 2]; ip += 1
            kb.op("pe", lambda e: e.transpose(out=p[:, 0:128], in_=cum[d][:, n, :], identity=idt[:]), reads=[cum[d], idt], writes=[p])
            kb.act(cumT[d][:, n * 128:(n + 1) * 128], p[:, 0:128], AF.Copy, [p], [cumT[d]])
    xst = [kb.sb(f"xst{i}", [128, 512]) for i in range(3)]
    xdt = [kb.sb(f"xdt{d}", [128, 18, 512], BF16) for d in range(2)]
    bst = kb.sb("bst", [128, NT])
    bb = kb.sb("bb", [128, NT], BF16); cbf = kb.sb("cbf", [128, NT], BF16)
    for d in range(2):
        kb.op("dve", lambda e: e.tensor_scalar(out=cum[d][:], in0=cum[d][:], scalar1=-1.0, scalar2=None, op0=ALU.mult),
              reads=[cum[d]], writes=[cum[d]])
    ncum = cum
    sel = [kb.sb(f"sel{i}", [128, 128]) for i in range(2)]
    Aall = [kb.sb(f"Aall{i}", [128, 512]) for i in range(16)]
    Dt = [kb.sb(f"Dt{i}", [128, 512]) for i in range(3)]
    Mt = [kb.sb(f"Mt{i}", [128, 512], BF16) for i in range(3)]
    yz = kb.sb("yz", [64, 8, 512])
    xh = [kb.sb(f"xh{i}", [64, 512]) for i in range(2)]
    zh = [kb.sb(f"zh{i}", [64, 512]) for i in range(2)]
    sqs = kb.sb("sqs", [64, 512]); rst = kb.sb("rst", [64, 512])
    fo = [kb.sb(f"fo{i}", [64, 512]) for i in range(2)]
    PC = [P[6], P[7], P[0], P[1]]
    isel = 0; iw = 0; ih = 0; ix = 0
    for g in range(8):
        for n in range(18):
            xs_ = xst[ix % 3]; ix += 1
            kb.dma("sp" if n % 2 else "act", xs_[:], XT_.ap()[:, n, g * 512:(g + 1) * 512], sbuf=xs_, load=True)
            for d in range(2):
                for r in range(8):
                    col = d * 64 + g * 8 + r
                    kb.op("dve",
                          lambda e: e.tensor_scalar(out=xdt[d][:, n, r * 64:(r + 1) * 64], in0=xs_[:, r * 64:(r + 1) * 64],
                                                    scalar1=dtt[:, n, col:col + 1], scalar2=None, op0=ALU.mult),
                          reads=[xs_, dtt], writes=[xdt[d]])
        kb.dma("sp", bst[:], BT.ap()[g], sbuf=bst, load=True)
        kb.op("pool", lambda e: e.tensor_copy(out=bb[:], in_=bst[:]), reads=[bst], writes=[bb])
        kb.dma("sp", bst[:], CT.ap()[g], sbuf=bst, load=True)
        kb.op("pool", lambda e: e.tensor_copy(out=cbf[:], in_=bst[:]), reads=[bst], writes=[cbf])
        for tb in range(5):
            c0 = tb * 512
            ncol = 512 if tb < 4 else 256
            n_ = slice(0, ncol)
            ttiles = list(range(4 * tb, 4 * tb + 4)) if tb < 4 else [16, 17]
            for r in range(8):
                for d in range(2):
                    col = d * 64 + g * 8 + r
                    s_ = sel[isel % 2]; isel += 1
                    a_ = Aall[r * 2 + d]
                    kb.op("dve", lambda e: e.tensor_scalar(out=s_[:], in0=onesf[:], scalar1=idt[:, col:col + 1], scalar2=None, op0=ALU.mult),
                          reads=[onesf, idt], writes=[s_])
                    pa = P[4 + (isel % 2)]
                    kb.mm(pa[:, n_], s_[:], cumT[d][:, c0:c0 + ncol], True, True, [s_, cumT[d]], [pa])
                    kb.act(a_[:, n_], pa[:, n_], AF.Copy, [pa], [a_])
            for r in range(8):
                hd = g * 8 + r
                py = P[2 + (r % 2)]
                work = []
                for d in range(2):
                    col = d * 64 + hd
                    a_ = Aall[r * 2 + d]
                    if tb < 4:
                        full = [16, 17] + (list(range(0, 4 * tb)) if d == 0 else list(range(4 * tb + 4, 16)))
                    else:
                        full = []
                    work += [(d, col, a_, s, None) for s in full] + [(d, col, a_, s, qi) for qi, s in enumerate(ttiles)]
                pend = None

                def second(item, first, lastp):
                    d, col, a_, s, qi, pc, D_, M_ = item
                    kb.op("dve", lambda e: e.tensor_tensor(out=M_[:, n_], in0=pc[:, n_], in1=D_[:, n_], op=ALU.mult),
                          reads=[pc, D_], writes=[M_])
                    kb.mm(py[0:64, n_], xdt[d][:, s, r * 64:(r + 1) * 64], M_[:, n_], first, lastp, [xdt[d], M_], [py], inc=True)

                for wi, (d, col, a_, s, qi) in enumerate(work):
                    pc = PC[iw % 4]
                    D_ = Dt[iw % 3]; M_ = Mt[iw % 3]; iw += 1
                    kb.mm(pc[:, n_], bb[:, s * 128:(s + 1) * 128], cbf[:, c0:c0 + ncol], True, True, [bb, cbf], [pc])
                    if qi is None:
                        kb.act(D_[:, n_], a_[:, n_], AF.Exp, [a_, ncum[d]], [D_], bias=ncum[d][:, s, col:col + 1])
                    else:
                        kb.op("dve", lambda e: e.tensor_scalar(out=D_[:, n_], in0=a_[:, n_], scalar1=cum[d][:, s, col:col + 1], scalar2=0.0,
                                                                op0=ALU.add, op1=ALU.min), reads=[a_, cum[d]], writes=[D_])
                        kb.act(D_[:, n_], D_[:, n_], AF.Exp, [D_], [D_])
                        kb.op("pool", lambda e: e.tensor_tensor(out=D_[:, n_], in0=D_[:, n_], in1=mk[:, d * 4 + qi, n_], op=ALU.mult),
                              reads=[D_, mk], writes=[D_])
                    if pend is not None:
                        second(pend, wi == 1, False)
                    pend = (d, col, a_, s, qi, pc, D_, M_)
                second(pend, len(work) == 1, True)
                x_, z_ = xh[ih % 2], zh[ih % 2]; ih += 1
                kb.dma("sp", x_[:, n_], XH.ap()[hd][:, c0:c0 + ncol], sbuf=x_, load=True)
                kb.dma("act", z_[:, n_], ZH.ap()[hd][:, c0:c0 + ncol], sbuf=z_, load=True)
                kb.act(z_[:, n_], z_[:, n_], AF.Silu, [z_], [z_])
                kb.op("dve", lambda e: e.scalar_tensor_tensor(out=x_[:, n_], in0=x_[:, n_], scalar=dsk[:, hd:hd + 1], in1=py[0:64, n_],
                                                              op0=ALU.mult, op1=ALU.add), reads=[x_, dsk, py], writes=[x_])
                kb.op("pool", lambda e: e.tensor_tensor(out=yz[:, r, n_], in0=x_[:, n_], in1=z_[:, n_], op=ALU.mult),
                      reads=[x_, z_], writes=[yz])
            pn = P[4]
            for r in range(8):
                kb.act(sqs[:, n_], yz[:, r, n_], AF.Square, [yz], [sqs])
                kb.mm(pn[0:64, n_], onesf[0:64, 0:64], sqs[:, n_], r == 0, r == 7, [onesf, sqs], [pn], inc=True)
            kb.act(rst[:, n_], pn[0:64, n_], AF.Sqrt, [pn, eps], [rst], bias=eps[0:64, :], scale=1.0 / 512)
            kb.op("dve", lambda e: e.reciprocal(out=rst[:, n_], in_=rst[:, n_]), reads=[rst], writes=[rst])
            for r in range(8):
                hd = g * 8 + r
                f_ = fo[r % 2]
                kb.op("dve", lambda e: e.scalar_tensor_tensor(out=f_[:, n_], in0=yz[:, r, n_], scalar=nwh[:, hd:hd + 1], in1=rst[:, n_],
                                                              op0=ALU.mult, op1=ALU.mult), reads=[yz, nwh, rst], writes=[f_])
                kb.dma("sp", OT.ap()[hd][:, c0:c0 + ncol], f_[:, n_], sbuf=f_, load=False)
    kb.finish()
    return kb


class View:
    def __init__(self, ap):
        self._ap = ap

    def ap(self):
        return self._ap


def emit_xpose(kb, ident_d, get_in, put_out, nblk_p, nblk_f, pin=128):
    idt = kb.sb("idt", [128, 128])
    kb.dma("sp", idt[:], ident_d.ap(), sbuf=idt, load=True)
    W = nblk_f * 128
    src = [kb.sb(f"src{i}", [128, W]) for i in range(2)]
    ps = [kb.ps(f"tp{i}", [128, 512]) for i in range(4)]
    GB = 4
    dst = [kb.sb(f"dst{i}", [128, GB, nblk_p * pin]) for i in range(1)]
    ip = 0
    for b0 in range(0, nblk_f, GB):
        nb = min(GB, nblk_f - b0)
        d = dst[0]
        for a in range(nblk_p):
            s_ = src[a % 2]
            kb.dma("sp" if a % 2 else "act", s_[0:pin, 0:nb * 128], get_in(a)[:, b0 * 128:(b0 + nb) * 128], sbuf=s_, load=True)
            p = ps[ip % 4]; ip += 1
            for bb in range(nb):
                kb.op("pe", lambda e: e.transpose(out=p[:, bb * pin:(bb + 1) * pin], in_=s_[0:pin, bb * 128:(bb + 1) * 128],
                                                  identity=idt[0:pin, 0:pin]), reads=[s_, idt], writes=[p], inc=(bb == nb - 1))
            for bb in range(nb):
                kb.act(d[:, bb, a * pin:(a + 1) * pin], p[:, bb * pin:(bb + 1) * pin], AF.Copy, [p], [d])
        for bb in range(nb):
            kb.dma("sp", put_out(b0 + bb), d[:, bb, :], sbuf=d, load=False)
    kb.finish()


def emit_ada_fm(kb, condT_d, w_d, b_d, ident_d, MOD):
    ct = kb.sb("ct", [128, 16, 2]); cs = kb.sb("cs", [128, 16, 2])
    idt = kb.sb("idt", [128, 128])
    kb.dma("sp", idt[:], ident_d.ap(), sbuf=idt, load=True)
    kb.dma("sp", ct[:], condT_d.ap(), sbuf=ct, load=True)
    kb.act(cs[:], ct[:], AF.Silu, [ct], [cs])
    wst = [kb.sb(f"wst{i}", [128, 16, 128]) for i in range(3)]
    bst = kb.sb("bst", [96, 128]); bT = kb.sb("bT", [128, 96])
    msb = [kb.sb(f"msb{i}", [128, 2, 96]) for i in range(2)]
    ps = [kb.ps(f"ap{i}", [128, 512]) for i in range(4)]
    i = 0
    for l in range(4):
        kb.dma("act", bst[:], b_d.ap()[l].rearrange("(m p) -> m p", p=128), sbuf=bst, load=True)
        pb = ps[3]
        kb.op("pe", lambda e: e.transpose(out=pb[:, 0:96], in_=bst[:], identity=idt[0:96, 0:96]), reads=[bst, idt], writes=[pb])
        kb.op("dve", lambda e: e.tensor_copy(out=bT[:], in_=pb[:, 0:96]), reads=[pb], writes=[bT])
        ms = msb[l % 2]
        for m in range(96):
            st = wst[i % 3]
            p = ps[i % 3]; i += 1
            kb.dma("sp" if m % 2 else "act", st[:], w_d.ap()[l][:, m * 128:(m + 1) * 128].rearrange("(k p) c -> p k c", p=128),
                   sbuf=st, load=True)
            for k in range(16):
                kb.mm(p[:, 0:2], st[:, k, :], cs[:, k, :], k == 0, k == 15, [st, cs], [p])
            kb.op("dve", lambda e: e.tensor_scalar(out=ms[:, :, m], in0=p[:, 0:2], scalar1=bT[:, m:m + 1], scalar2=None, op0=ALU.add),
                  reads=[p, bT], writes=[ms])
        kb.dma("sp", MOD.ap()[l], ms[:], sbuf=ms, load=False)
    kb.finish()


def build_fused():
    kb = KB()
    kb.fused = True
    nc = kb.nc
    ext = lambda n, sh: nc.dram_tensor(n, list(sh), F32, kind="ExternalInput")
    XTM = ext("XTM", [NT, D]); CONDT = ext("CONDT", [128, 16, 2]); ADAW = ext("ADAW", [4, D, 6 * D]); ADAB = ext("ADAB", [4, 6 * D])
    NMIX = ext("NMIX", [4, 128, 16]); NMLP = ext("NMLP", [4, 128, 16]); FNW = ext("FNW", [128, 16])
    W1 = ext("W1", [4, D, 4 * D]); W2 = ext("W2", [4, 4 * D, D])
    NAQKV = ext("NAQKV", [2, D, 6144]); NAWO = ext("NAWO", [2, D, D]); RP = ext("RP", [2, 16, 64, 960]); MKN = ext("MKN", [64, 960])
    SSWIN = ext("SSWIN", [D, 10368]); SSWO = ext("SSWO", [4096, D]); CW = ext("CW", [128, 48, 5]); CB = ext("CB", [128, 48])
    DTB = ext("DTB", [128, 1]); ALOG = ext("ALOG", [128, 1]); NWH = ext("NWH", [64, 64]); DSK = ext("DSK", [64, 64])
    TF = ext("TF", [128, 128]); TBm = ext("TBM", [128, 128]); ID = ext("ID", [128, 128]); MKS = ext("MKS", [128, 8, 512])
    DAQKV = ext("DAQKV", [D, 6144]); DAWO = ext("DAWO", [D, D]); COS = ext("COS", [128, 2048]); SIN = ext("SIN", [128, 2048])
    PM = ext("PM", [128, 128]); LAM = ext("LAM", [1, 512]); SW = ext("SW", [128, 2])
    OUT = nc.dram_tensor("OUT", [2048, D], F32, kind="ExternalOutput")
    hA = kb.scratch("hA", [16, 128, NT]); hB = kb.scratch("hB", [16, 128, NT])
    YT = kb.scratch("YT", [81, 128, NT]); VTM = kb.scratch("VTM", [NT, 4096]); XO = kb.scratch("XO", [48, 128, NT])
    DAo = kb.scratch("DAo", [2, 128, NT]); DTM = kb.scratch("DTM", [NT, 256]); OTs = kb.scratch("OTs", [32, 128, NT])
    MOD = kb.scratch("MOD", [4, 128, 2, 96])

    def stage(prefix, fn):
        kb.push(prefix)
        fn()
        kb.pop()

    stage("ti_", lambda: emit_xpose(kb, ID, lambda a: XTM.ap()[a * 128:(a + 1) * 128, :],
                                    lambda b: hA.ap()[b], 18, 16))
    stage("ad_", lambda: emit_ada_fm(kb, CONDT, ADAW, ADAB, ID, MOD))
    hcur, hnxt = hA, hB
    import math
    for i in range(4):
        last = i == 3
        mixer, j = i % 3, i // 3
        modv = View(MOD.ap()[i].rearrange("p w (a c) -> p w a c", a=6))
        Wt = (View(NAQKV.ap()[j]), SSWIN, DAQKV)[mixer]
        F = (6144, 10368, 6144)[mixer]
        ytv = View(YT.ap()[0:F // 128])
        kb.io = {"hT": hcur, "modT": modv, "normw": View(NMIX.ap()[i]), "W": Wt, "YT": ytv}
        stage(f"pr{i}_", lambda: build_pre(F, kb_=kb))
        if mixer in (0, 2):
            stage(f"xv{i}_", lambda: emit_xpose(kb, ID, lambda a: YT.ap()[32 + a], lambda b: VTM.ap()[b * 128:(b + 1) * 128, 0:2048],
                                                16, 18))
            vt = VTM.ap()[:, 0:2048]
        if mixer == 0:
            kb.io = {"QK": View(YT.ap()[0:32]), "Vr": View(vt.rearrange("(r c) f -> c r f", c=64)),
                     "Vc": View(vt[2048:NT].rearrange("(t p) f -> p t f", p=128)), "RP": View(RP.ap()[j]), "MK": MKN,
                     "OT": View(OTs.ap()[0:16])}
            stage(f"na{i}_", lambda: build_na(kb_=kb))
            Wo, Fin = View(NAWO.ap()[j]), 2048
        elif mixer == 2:
            kb.io = {"QK": View(YT.ap()[0:32]), "Vt": View(vt.rearrange("(t p) f -> p t f", p=128)), "COS": COS, "SIN": SIN, "PM": PM,
                     "LAM": LAM, "SW": SW, "OT": View(OTs.ap()[0:16])}
            li = 0.8 - 0.6 * math.exp(-0.3 * i)
            stage(f"da{i}_", lambda: build_da(li, kb_=kb))
            Wo, Fin = DAWO, 2048
        else:
            kb.io = {"XI": View(YT.ap()[32:81]), "CW": CW, "CB": CB, "DTB": DTB, "ALOG": ALOG, "XO": XO, "DA": DAo}
            stage(f"sa{i}_", lambda: build_ssm_a(kb_=kb))
            stage(f"xx{i}_", lambda: emit_xpose(kb, ID, lambda a: XO.ap()[a], lambda b: VTM.ap()[b * 128:(b + 1) * 128, :], 32, 18))
            stage(f"xd{i}_", lambda: emit_xpose(kb, ID, lambda a: DAo.ap()[a], lambda b: DTM.ap()[b * 128:(b + 1) * 128, :], 2, 18))
            tokv = lambda ap_: View(ap_.rearrange("(t p) f -> p t f", p=128))
            kb.io = {"XTOK": tokv(VTM.ap()), "BT": View(XO.ap()[32:40]), "CT": View(XO.ap()[40:48]),
                     "DTT": tokv(DTM.ap()[:, 0:128]), "ATT": tokv(DTM.ap()[:, 128:256]),
                     "XH": View(XO.ap()[0:32].rearrange("c (two p) t -> (c two) p t", two=2)),
                     "ZH": View(YT.ap()[0:32].rearrange("c (two p) t -> (c two) p t", two=2)),
                     "NWH": NWH, "DSK": DSK, "TF": TF, "TB": TBm, "ID": ID, "MK": MKS,
                     "OT": View(OTs.ap().rearrange("c (two p) t -> (c two) p t", two=2))}
            stage(f"sb{i}_", lambda: build_ssm_b(kb_=kb))
            Wo, Fin = SSWO, 4096
        kb.io = {"hT": hcur, "OT": View(OTs.ap()[0:Fin // 128]), "modT": modv, "normw": View(NMLP.ap()[i]), "Wo": Wo,
                 "W1": View(W1.ap()[i]), "W2": View(W2.ap()[i]), "fnw": FNW, "HO": hnxt}
        stage(f"po{i}_", lambda: build_post(Fin, last, kb_=kb))
        hcur, hnxt = hnxt, hcur
    kb.io = {}
    stage("to_", lambda: emit_xpose(kb, ID, lambda a: hcur.ap()[a][:, 0:2048], lambda b: OUT.ap()[b * 128:(b + 1) * 128, :], 16, 16))
    kb.fused = False
    kb.finish()
    return kb


def fused_inputs(inp, b):
    import math
    cond = np.stack([inp["c"][b], inp["c_ctx"]], 0)
    condT = np.ascontiguousarray(cond.T.reshape(16, 128, 2).transpose(1, 0, 2))
    tri_f, tri_b, ident, masks = ssm_consts()
    C, S, P = rope_tables()
    rp = np.stack([na_tables(inp["na_rpb"][j])[0].reshape(16, 64, 960) for j in range(2)], 0)
    mkn = na_tables(inp["na_rpb"][0])[1].reshape(64, 960)
    vl = lambda a: np.stack([vec_layout(a[i]) for i in range(a.shape[0])], 0)
    return {
        "XTM": np.ascontiguousarray(np.concatenate([inp["x"][b], inp["ctx"][b]], 0)), "CONDT": condT,
        "ADAW": inp["ada_w"], "ADAB": inp["ada_b"], "NMIX": vl(inp["norm_mix_w"]), "NMLP": vl(inp["norm_mlp_w"]),
        "FNW": vec_layout(inp["final_norm_w"]), "W1": inp["mlp_w1"], "W2": inp["mlp_w2"],
        "NAQKV": inp["na_w_qkv"], "NAWO": inp["na_w_o"], "RP": np.ascontiguousarray(rp), "MKN": np.ascontiguousarray(mkn),
        "SSWIN": inp["ssm_w_in"][0], "SSWO": inp["ssm_w_out"][0],
        "CW": np.ascontiguousarray(inp["ssm_conv_w"][0].reshape(5, 48, 128).transpose(2, 1, 0)),
        "CB": np.ascontiguousarray(inp["ssm_conv_b"][0].reshape(48, 128).T),
        "DTB": np.ascontiguousarray(inp["ssm_dt_bias"][0].reshape(128, 1)), "ALOG": np.ascontiguousarray(inp["ssm_a_log"][0].reshape(128, 1)),
        "NWH": np.ascontiguousarray(inp["ssm_norm_w"][0].reshape(64, 64).T),
        "DSK": np.ascontiguousarray(np.broadcast_to(inp["ssm_d"][0][None, :], (64, 64))),
        "TF": tri_f, "TBM": tri_b, "ID": ident, "MKS": masks,
        "DAQKV": inp["da_w_qkv"][0], "DAWO": inp["da_w_o"][0], "COS": C, "SIN": S, "PM": P,
        "LAM": np.ascontiguousarray(inp["da_lambda"][0].reshape(1, 512)), "SW": np.ascontiguousarray(inp["da_subln_w"][0].reshape(2, 128).T),
    }


def kernel(**inputs):
    inp = {k: np.asarray(v) for k, v in inputs.items()}
    NB = 4
    kb = build_fused()
    res = run(kb, [fused_inputs(inp, b) for b in range(NB)], n=NB)
    return np.stack([res.results[b]["OUT"] for b in range(NB)], 0).astype(np.float32)
```

```python
import numpy as np
from contextlib import ExitStack
import concourse.bass as bass
import concourse.mybir as mybir
from concourse.bass_utils import run_bass_kernel_spmd

F32 = mybir.dt.float32
F32R = mybir.dt.float32r
BF16 = mybir.dt.bfloat16
AF = mybir.ActivationFunctionType
ALU = mybir.AluOpType
AX = mybir.AxisListType


class Buf:
    def __init__(self, kb, t, name):
        self.kb, self.t, self.name = kb, t, name
        self.last_w = None
        self.readers = []
        self.dsem = None
        self.dcnt = 0

    def __getitem__(self, idx):
        return self.t[idx]


class KB:
    def __init__(self):
        self.nc = bass.Bass("TRN2", target_bir_lowering=False)
        self.es = ExitStack()
        nc = self.nc
        self.E = {"pe": nc.tensor, "act": nc.scalar, "dve": nc.vector, "pool": nc.gpsimd, "sp": nc.sync}
        self.sems = {}
        self.cnt = {}
        for e in self.E:
            self.sems[e] = self.es.enter_context(nc.semaphore("s_" + e))
            self.cnt[e] = 0
        self.seen = {e: {} for e in self.E}
        self.nbuf = 0
        self.out_deps = []
        self.stack = [self.es]
        self.io = {}
        self.prefix = ""
        self.dpool = []
        self.scope_bufs = []
        self.fused = False
        self.bar = self.es.enter_context(nc.semaphore("s_bar"))
        self.barcnt = 0
        self.nsem = 0

    def push(self, prefix):
        self.prefix = prefix
        self.stack.append(ExitStack())
        self.scope_bufs = []

    def pop(self):
        for b in self.scope_bufs:
            if b.dsem is not None:
                self.dpool.append((b.dsem, b.dcnt))
                b.dsem = None
        self.scope_bufs = []
        self.stack.pop().close()
        self.prefix = ""

    def barrier(self):
        for d in self.out_deps:
            self._wait("sp", d)
        self.out_deps = []
        for e in self.E:
            if e != "sp" and self.cnt[e] > 0:
                self._wait("sp", (e, self.cnt[e]))
        self.nc.sync.sem_inc(self.bar, 1)
        self.barcnt += 1
        for e in self.E:
            if e != "sp":
                self.E[e].wait_ge(self.bar, self.barcnt)

    def sb(self, name, shape, dtype=F32):
        t = self.stack[-1].enter_context(self.nc.sbuf_tensor(self.prefix + name, list(shape), dtype))
        b = Buf(self, t, self.prefix + name)
        self.scope_bufs.append(b)
        return b

    def ps(self, name, shape, dtype=F32):
        t = self.stack[-1].enter_context(self.nc.psum_tensor(self.prefix + name, list(shape), dtype))
        b = Buf(self, t, self.prefix + name)
        self.scope_bufs.append(b)
        return b

    def din(self, name, shape, dtype=F32):
        if name in self.io:
            return self.io[name]
        return self.nc.dram_tensor(name, list(shape), dtype, kind="ExternalInput")

    def dout(self, name, shape, dtype=F32):
        if name in self.io:
            return self.io[name]
        return self.nc.dram_tensor(name, list(shape), dtype, kind="ExternalOutput")

    def scratch(self, name, shape, dtype=F32):
        return self.nc.dram_tensor(name, list(shape), dtype)

    def _semobj(self, key):
        return self.sems[key] if isinstance(key, str) else key

    def _wait(self, eng, dep):
        key, val = dep
        kid = key if isinstance(key, str) else id(key)
        if not isinstance(key, str) and not isinstance(key, Buf):
            kid = id(key)
        if isinstance(key, str):
            assert val <= self.cnt[key], f"wait on unissued inc {key} {val}>{self.cnt[key]}"
        if self.seen[eng].get(kid, 0) >= val:
            return
        self.E[eng].wait_ge(self._semobj(key), val)
        self.seen[eng][kid] = val

    def _sync(self, eng, reads, writes):
        deps = []
        for b in reads:
            if b.last_w is not None:
                deps.append(b.last_w)
        for b in writes:
            if b.last_w is not None:
                deps.append(b.last_w)
            deps.extend(b.readers)
        for d in deps:
            if eng == "pe" and d[0] == "pe":
                continue
            self._wait(eng, d)

    def op(self, eng, fn, reads=(), writes=(), inc=True):
        self._sync(eng, reads, writes)
        inst = fn(self.E[eng])
        if inc:
            inst.then_inc(self.sems[eng], 1)
            self.cnt[eng] += 1
            val = self.cnt[eng]
        else:
            val = self.cnt[eng] + 1
        dep = (eng, val)
        for b in reads:
            b.readers.append(dep)
        for b in writes:
            b.last_w = dep
            b.readers = []
        return inst

    def dram(self, name, shape, dtype=F32):
        t = self.nc.dram_tensor(name, list(shape), dtype)
        return Buf(self, t, name)

    def _dsem(self, b):
        if b.dsem is None:
            if self.dpool:
                b.dsem, b.dcnt = self.dpool.pop()
            else:
                self.nsem += 1
                b.dsem = self.es.enter_context(self.nc.semaphore(f"d_{self.nsem}"))
                b.dcnt = 0
        return b.dsem

    def dma(self, q, out, in_, sbuf=None, load=None, reads=(), writes=(), is_out=False):
        reads, writes = list(reads), list(writes)
        if sbuf is not None:
            if load:
                writes = [sbuf] + writes
            else:
                reads = [sbuf] + reads
                is_out = True
        owner = writes[0] if writes else reads[0]
        self._dsem(owner)
        self._sync(q, reads, writes)
        inst = self.E[q].dma_start(out=out, in_=in_)
        inst.then_inc(owner.dsem, 16)
        owner.dcnt += 16
        dep = (owner.dsem, owner.dcnt)
        for b in reads:
            b.readers.append(dep)
        for b in writes:
            b.last_w = dep
            b.readers = []
        if is_out:
            self.out_deps.append(dep)
        return inst

    def allgather(self, out_buf, in_buf, groups):
        self._dsem(out_buf)
        self._sync("pool", [in_buf], [out_buf])
        inst = self.nc.gpsimd.collective_compute("AllGather", ALU.bypass, replica_groups=groups,
                                                 ins=[in_buf[:]], outs=[out_buf[:]])
        inst.then_inc(out_buf.dsem, 16)
        out_buf.dcnt += 16
        dep = (out_buf.dsem, out_buf.dcnt)
        in_buf.readers.append(dep)
        out_buf.last_w = dep
        out_buf.readers = []
        return inst

    def finish(self, eng="sp"):
        if self.fused:
            return self.barrier()
        for d in self.out_deps:
            self._wait(eng, d)
        for e in self.E:
            if e != eng and self.cnt[e] > 0:
                self._wait(eng, (e, self.cnt[e]))

    def mm(self, out_ap, lhsT, rhs, start, stop, reads, writes, inc=None):
        return self.op("pe", lambda e: e.matmul(out_ap, lhsT, rhs, start=start, stop=stop),
                       reads=reads, writes=writes, inc=stop if inc is None else inc)

    def act(self, out_ap, in_ap, func, reads, writes, bias=None, scale=1.0, eng="act", **kw):
        def f(e):
            kws = dict(kw)
            if bias is not None:
                kws["bias"] = bias
            return e.activation(out=out_ap, in_=in_ap, func=func, scale=scale, **kws)
        return self.op(eng, f, reads=reads, writes=writes)


def run(kb, in_maps, n=8, trace=False):
    res = run_bass_kernel_spmd(kb.nc, in_maps, core_ids=list(range(n)), trace=trace)
    return res


D = 2048
NCORES = 8


def build_ada(kb_=None):
    kb = kb_ if kb_ is not None else KB()
    CW = 1536
    condT = kb.din("condT", [128, 16, 5])
    w = kb.din("w", [4, D, CW])
    b = kb.din("b", [4, CW])
    o = kb.dout("o", [4, 5, CW])
    ct = kb.sb("ct", [128, 16, 5])
    cs = kb.sb("cs", [128, 16, 5])
    ones = kb.sb("ones", [1, 8])
    bt = kb.sb("bt", [1, 4, CW])
    wt = [kb.sb(f"wt{i}", [128, CW]) for i in range(4)]
    res = [kb.sb(f"res{i}", [5, CW]) for i in range(2)]
    pss = [kb.ps(f"ps{i}", [128, 512]) for i in range(6)]
    kb.dma("sp", ct[:], condT.ap(), ct, True)
    kb.dma("sp", bt[:], b.ap().rearrange("(o l) c -> o l c", o=1), bt, True)
    kb.op("dve", lambda e: e.memset(ones[:], 1.0), writes=[ones])
    kb.act(cs[:], ct[:], AF.Silu, [ct], [cs])
    i = 0
    for l in range(4):
        pb = pss[(l % 2) * 3:(l % 2) * 3 + 3]
        for t in range(3):
            kb.mm(pb[t][0:5, :], ones[0:1, 0:5], bt[0:1, l, t * 512:(t + 1) * 512], True, False, [ones, bt], [pb[t]])
        for k in range(16):
            wb_ = wt[i % 4]
            i += 1
            kb.dma("sp" if k % 2 == 0 else "act", wb_[:], w.ap()[l, k * 128:(k + 1) * 128, :], wb_, True)
            for t in range(3):
                kb.mm(pb[t][0:5, :], cs[:, k, :], wb_[:, t * 512:(t + 1) * 512], False, k == 15, [cs, wb_], [pb[t]],
                      inc=(t == 2 or k == 15))
        r = res[l % 2]
        for t in range(3):
            kb.op("dve", lambda e: e.tensor_copy(out=r[:, t * 512:(t + 1) * 512], in_=pb[t][0:5, :]), reads=[pb[t]], writes=[r])
        kb.dma("sp", o.ap()[l], r[:], r, False)
    kb.finish()
    return kb


def run_ada(inp):
    cond = np.concatenate([inp["c"], inp["c_ctx"][None]], 0)
    condT = np.ascontiguousarray(cond.T.reshape(16, 128, 5).transpose(1, 0, 2))
    kb = build_ada()
    maps = []
    for c in range(NCORES):
        sl = slice(1536 * c, 1536 * (c + 1))
        maps.append({"condT": condT, "w": np.ascontiguousarray(inp["ada_w"][:, :, sl]),
                     "b": np.ascontiguousarray(inp["ada_b"][:, sl])})
    res = run(kb, maps)
    mod = np.concatenate([r["o"] for r in res.results], axis=-1)
    return mod


NT = 2304
TB = 768
SUB = 384
NBLK = NT // TB
EPS = 1e-6


def col_ranges(blk):
    lo, hi = blk * TB, (blk + 1) * TB
    out = []
    if lo < 2048:
        out.append((0, min(hi, 2048) - lo, 0))
    if hi > 2048:
        out.append((max(lo, 2048) - lo, hi - lo, 1))
    return out


class TokStage:
    def __init__(self, kb, nmod):
        self.kb = kb
        self.ones = kb.sb("ones", [128, 128])
        self.epsb = kb.sb("epsb", [128, 1])
        kb.op("dve", lambda e: e.memset(self.ones[:], 1.0), writes=[self.ones])
        kb.op("dve", lambda e: e.memset(self.epsb[:], EPS), writes=[self.epsb])
        self.pss = [kb.ps(f"ps{i}", [128, 512]) for i in range(8)]
        self.pi = 0
        self.sq = [kb.sb(f"sq{i}", [128, TB]) for i in range(2)]
        self.rstd = kb.sb("rstd", [128, TB])
        self.tmp = [kb.sb(f"tmp{i}", [128, TB]) for i in range(2)]
        self.ti = 0

    def psum(self):
        p = self.pss[self.pi % 8]
        self.pi += 1
        return p

    def load_mod(self, modT_d, normw_d, idx_shift, idx_scale):
        kb = self.kb
        mt = kb.sb(f"modT{idx_shift}", [128, 2, 6, 16])
        nw = kb.sb(f"nw{idx_shift}", [128, 16])
        kb.dma("sp", mt[:], modT_d.ap(), sbuf=mt, load=True)
        kb.dma("sp", nw[:], normw_d.ap(), sbuf=nw, load=True)
        A = kb.sb(f"A{idx_shift}", [128, 2, 16])
        for w in range(2):
            kb.op("dve", lambda e: e.scalar_tensor_tensor(out=A[:, w, :], in0=mt[:, w, idx_scale, :], scalar=1.0,
                                                          in1=nw[:], op0=ALU.add, op1=ALU.mult),
                  reads=[mt, nw], writes=[A])
        return mt, A

    def norm_mod(self, hk, blk, A, mt, idx_shift, outs):
        kb = self.kb
        pb = [self.psum() for _ in range(TB // SUB)]
        for k in range(16):
            s = self.sq[k % 2]
            kb.act(s[:], hk[k][:], AF.Square, [hk[k]], [s])
            for t in range(TB // SUB):
                kb.mm(pb[t][:, 0:SUB], self.ones[:], s[:, t * SUB:(t + 1) * SUB], k == 0, k == 15, [self.ones, s], [pb[t]],
                      inc=(k == 15 or t == TB // SUB - 1))
        for t in range(TB // SUB):
            sl = slice(t * SUB, (t + 1) * SUB)
            kb.act(self.rstd[:, sl], pb[t][:, 0:SUB], AF.Sqrt, [pb[t], self.epsb], [self.rstd], bias=self.epsb[:], scale=1.0 / D)
        kb.op("dve", lambda e: e.reciprocal(out=self.rstd[:], in_=self.rstd[:]), reads=[self.rstd], writes=[self.rstd])
        for k in range(16):
            tm = self.tmp[self.ti % 2]
            self.ti += 1
            for (lo, hi, w) in col_ranges(blk):
                kb.op("dve", lambda e: e.scalar_tensor_tensor(out=tm[:, lo:hi], in0=hk[k][:, lo:hi], scalar=A[:, w, k:k + 1],
                                                              in1=self.rstd[:, lo:hi], op0=ALU.mult, op1=ALU.mult),
                      reads=[hk[k], A, self.rstd], writes=[tm])
            for (lo, hi, w) in col_ranges(blk):
                kb.act(outs[k][:, lo:hi], tm[:, lo:hi], AF.Identity, [tm, mt], [outs[k]], bias=mt[:, w, idx_shift, k:k + 1])


class WStream:
    def __init__(self, kb, name, nk, nbuf=2):
        self.kb, self.nk = kb, nk
        self.st = [kb.sb(f"{name}_st{i}", [128, nk, 128]) for i in range(nbuf)]
        self.wb = [kb.sb(f"{name}_wb{i}", [128, nk, 128], BF16) for i in range(nbuf)]
        self.i = 0
        self.nbuf = nbuf

    def load(self, w_ap_2d, col0, q="sp", ceng="pool"):
        kb = self.kb
        st, wb = self.st[self.i % self.nbuf], self.wb[self.i % self.nbuf]
        self.i += 1
        src = w_ap_2d[:, col0:col0 + 128].rearrange("(k p) c -> p k c", p=128)
        kb.dma(q, st[:], src, sbuf=st, load=True)
        kb.op(ceng, lambda e: e.tensor_copy(out=wb[:], in_=st[:]), reads=[st], writes=[wb])
        return wb


class WGroup:
    def __init__(self, kb, name, nst=3, nbuf=2):
        self.kb = kb
        self.st = [kb.sb(f"{name}_s{i}", [128, 2048]) for i in range(nst)]
        self.wb = [kb.sb(f"{name}_g{i}", [128, 8192], BF16) for i in range(nbuf)]
        self.i = 0
        self.j = 0

    def load(self, w_ap_2d, nk, gcols, col0, ceng="act"):
        kb = self.kb
        wb = self.wb[self.i % len(self.wb)]
        self.i += 1
        view = wb[:, 0:nk * gcols].rearrange("p (k c) -> p k c", k=nk)
        KK = min(nk, 2048 // gcols)
        for kk in range(0, nk, KK):
            st = self.st[self.j % len(self.st)]
            self.j += 1
            sv = st[:, 0:KK * gcols].rearrange("p (k c) -> p k c", k=KK)
            src = w_ap_2d[kk * 128:(kk + KK) * 128, col0:col0 + gcols].rearrange("(k p) c -> p k c", p=128)
            kb.dma("sp", sv, src, sbuf=st, load=True)
            if ceng == "act":
                kb.act(view[:, kk:kk + KK, :], sv, AF.Copy, [st], [wb])
            else:
                kb.op(ceng, lambda e: e.tensor_copy(out=view[:, kk:kk + KK, :], in_=sv), reads=[st], writes=[wb])
        return wb, view


def build_pre(F, kb_=None):
    kb = kb_ if kb_ is not None else KB()
    hT = kb.din("hT", [16, 128, NT])
    modT = kb.din("modT", [128, 2, 6, 16])
    normw = kb.din("normw", [128, 16])
    W = kb.din("W", [D, F])
    YT = kb.dout("YT", [F // 128, 128, NT])
    ts = TokStage(kb, 1)
    mt, A = ts.load_mod(modT, normw, 0, 1)
    hk = [kb.sb(f"h{k}", [128, TB]) for k in range(16)]
    uk = [kb.sb(f"u{k}", [128, TB], BF16) for k in range(16)]
    ws = WStream(kb, "w", 16, nbuf=3)
    yo = [kb.sb(f"yo{i}", [128, TB]) for i in range(3)]
    for blk in range(NBLK):
        cs = slice(blk * TB, (blk + 1) * TB)
        for k in range(16):
            kb.dma("act" if k % 2 else "sp", hk[k][:], hT.ap()[k][:, cs], sbuf=hk[k], load=True)
        ts.norm_mod(hk, blk, A, mt, 0, uk)
        for m in range(F // 128):
            wb = ws.load(W.ap(), m * 128, q="sp", ceng=("pool" if m % 2 else "dve"))
            y = yo[m % 3]
            for t in range(TB // SUB):
                p = ts.psum()
                for k in range(16):
                    kb.mm(p[:, 0:SUB], wb[:, k, :], uk[k][:, t * SUB:(t + 1) * SUB], k == 0, k == 15, [wb, uk[k]], [p])
                kb.act(y[:, t * SUB:(t + 1) * SUB], p[:, 0:SUB], AF.Copy, [p], [y])
            kb.dma("act", YT.ap()[m][:, cs], y[:], sbuf=y, load=False)
    kb.finish()
    return kb


def to_fm(a):
    t, f = a.shape
    return np.ascontiguousarray(a.T.reshape(f // 128, 128, t))


def from_fm(a):
    c, p, t = a.shape
    return np.ascontiguousarray(a.reshape(c * p, t).T)


def mod_layout(mod_l, b):
    m = np.stack([mod_l[b], mod_l[4]], 0).reshape(2, 6, 16, 128)
    return np.ascontiguousarray(m.transpose(3, 0, 1, 2))


def vec_layout(v):
    return np.ascontiguousarray(v.reshape(-1, 128).T)


def build_post(Fin, last, kb_=None):
    kb = kb_ if kb_ is not None else KB()
    NKI = Fin // 128
    G = 4
    hT = kb.din("hT", [16, 128, NT])
    OT = kb.din("OT", [NKI, 128, NT])
    modT = kb.din("modT", [128, 2, 6, 16])
    normw = kb.din("normw", [128, 16])
    Wo = kb.din("Wo", [Fin, D])
    W1 = kb.din("W1", [D, 4 * D])
    W2 = kb.din("W2", [4 * D, D])
    if last:
        fnw = kb.din("fnw", [128, 16])
    HO = kb.dout("HO", [16, 128, NT])
    ts = TokStage(kb, 1)
    mt, A = ts.load_mod(modT, normw, 3, 4)
    hk = [kb.sb(f"h{k}", [128, TB]) for k in range(16)]
    ob = [kb.sb(f"ob{k}", [128, TB], BF16) for k in range(max(NKI, 16 + G))]
    ost = [kb.sb(f"ost{i}", [128, TB]) for i in range(2 if NKI <= 16 else 1)]
    wg = WGroup(kb, "wg", nst=2)
    GO = 8192 // NKI
    w2st = [kb.sb(f"w2st{i}", [128, D]) for i in range(2 if NKI <= 16 else 1)]
    w2b = [kb.sb(f"w2b{i}", [128, D], BF16) for i in range(2 * G)]
    if last:
        fw = kb.sb("fw", [128, 16])
        kb.dma("sp", fw[:], fnw.ap(), sbuf=fw, load=True)
        ones16 = kb.sb("ones16", [128, 2, 16])
        kb.op("dve", lambda e: e.memset(ones16[:], 0.0), writes=[ones16])
        zer = kb.sb("zer", [128, 2, 6, 16])
        kb.op("dve", lambda e: e.memset(zer[:], 0.0), writes=[zer])
        fA = kb.sb("fA", [128, 2, 16])
        for w in range(2):
            kb.op("dve", lambda e: e.tensor_copy(out=fA[:, w, :], in_=fw[:]), reads=[fw], writes=[fA])
    NS = TB // SUB
    i2 = 0
    for blk in range(NBLK):
        cs = slice(blk * TB, (blk + 1) * TB)
        rngs = col_ranges(blk)
        for k in range(16):
            kb.dma("act" if k % 2 else "sp", hk[k][:], hT.ap()[k][:, cs], sbuf=hk[k], load=True)
        for k in range(NKI):
            st = ost[k % len(ost)]
            kb.dma("sp", st[:], OT.ap()[k][:, cs], sbuf=st, load=True)
            kb.op("pool", lambda e: e.tensor_copy(out=ob[k][:], in_=st[:]), reads=[st], writes=[ob[k]])
        for m in range(16):
            if (m * 128) % GO == 0:
                wb, wview = wg.load(Wo.ap(), NKI, GO, m * 128, ceng="act")
            mo = (m * 128) % GO
            for t in range(NS):
                p = ts.psum()
                for k in range(NKI):
                    kb.mm(p[:, 0:SUB], wview[:, k, mo:mo + 128], ob[k][:, t * SUB:(t + 1) * SUB], k == 0, k == NKI - 1, [wb, ob[k]], [p])
                for (lo, hi, w) in rngs:
                    a, b_ = max(lo, t * SUB), min(hi, (t + 1) * SUB)
                    if a >= b_:
                        continue
                    kb.op("dve", lambda e: e.scalar_tensor_tensor(out=hk[m][:, a:b_], in0=p[:, a - t * SUB:b_ - t * SUB],
                                                                  scalar=mt[:, w, 2, m:m + 1], in1=hk[m][:, a:b_],
                                                                  op0=ALU.mult, op1=ALU.add),
                          reads=[p, mt, hk[m]], writes=[hk[m]])
        ts.norm_mod(hk, blk, A, mt, 3, ob[0:16])
        for g in range(4 * D // 128 // G):
            av = ob[16:16 + G]
            wb, wview = wg.load(W1.ap(), 16, 512, g * 512, ceng="act")
            for j in range(G):
                jj = g * G + j
                for t in range(NS):
                    p = ts.psum()
                    for k in range(16):
                        kb.mm(p[:, 0:SUB], wview[:, k, j * 128:(j + 1) * 128], ob[k][:, t * SUB:(t + 1) * SUB], k == 0, k == 15, [wb, ob[k]], [p])
                    tm = ts.tmp[ts.ti % 2]
                    ts.ti += 1
                    kb.act(tm[:, 0:SUB], p[:, 0:SUB], AF.Relu, [p], [tm])
                    kb.op("pool", lambda e: e.tensor_tensor(out=av[j][:, t * SUB:(t + 1) * SUB], in0=tm[:, 0:SUB], in1=tm[:, 0:SUB],
                                                            op=ALU.mult), reads=[tm], writes=[av[j]])
                st = w2st[i2 % len(w2st)]
                wv = w2b[i2 % (2 * G)]
                i2 += 1
                kb.dma("act", st[:], W2.ap()[jj * 128:(jj + 1) * 128, :], sbuf=st, load=True)
                kb.op("dve", lambda e: e.tensor_copy(out=wv[:], in_=st[:]), reads=[st], writes=[wv])
            wvs = [w2b[(i2 - G + j) % (2 * G)] for j in range(G)]
            for m in range(16):
                for t in range(NS):
                    p = ts.psum()
                    for j in range(G):
                        kb.mm(p[:, 0:SUB], wvs[j][:, m * 128:(m + 1) * 128], av[j][:, t * SUB:(t + 1) * SUB], j == 0, j == G - 1,
                              [wvs[j], av[j]], [p])
                    for (lo, hi, w) in rngs:
                        a, b_ = max(lo, t * SUB), min(hi, (t + 1) * SUB)
                        if a >= b_:
                            continue
                        kb.op("dve", lambda e: e.scalar_tensor_tensor(out=hk[m][:, a:b_], in0=p[:, a - t * SUB:b_ - t * SUB],
                                                                      scalar=mt[:, w, 5, m:m + 1], in1=hk[m][:, a:b_],
                                                                      op0=ALU.mult, op1=ALU.add),
                              reads=[p, mt, hk[m]], writes=[hk[m]])
        if last:
            fo = ts.tmp
            fouts = []
            class _O:
                pass
            _final_norm(kb, ts, hk, blk, fA, zer, HO, cs, fo)
        else:
            for k in range(16):
                kb.dma("sp", HO.ap()[k][:, cs], hk[k][:], sbuf=hk[k], load=False)
    kb.finish()
    return kb


def _final_norm(kb, ts, hk, blk, fA, zer, HO, cs, fo):
    pb = [ts.psum() for _ in range(TB // SUB)]
    for k in range(16):
        s = ts.sq[k % 2]
        kb.act(s[:], hk[k][:], AF.Square, [hk[k]], [s])
        for t in range(TB // SUB):
            kb.mm(pb[t][:, 0:SUB], ts.ones[:], s[:, t * SUB:(t + 1) * SUB], k == 0, k == 15, [ts.ones, s], [pb[t]],
                  inc=(k == 15 or t == TB // SUB - 1))
    for t in range(TB // SUB):
        sl = slice(t * SUB, (t + 1) * SUB)
        kb.act(ts.rstd[:, sl], pb[t][:, 0:SUB], AF.Sqrt, [pb[t], ts.epsb], [ts.rstd], bias=ts.epsb[:], scale=1.0 / D)
    kb.op("dve", lambda e: e.reciprocal(out=ts.rstd[:], in_=ts.rstd[:]), reads=[ts.rstd], writes=[ts.rstd])
    for k in range(16):
        o = fo[k % 2]
        kb.op("dve", lambda e: e.scalar_tensor_tensor(out=o[:], in0=hk[k][:], scalar=fA[:, 0, k:k + 1], in1=ts.rstd[:],
                                                      op0=ALU.mult, op1=ALU.mult), reads=[hk[k], fA, ts.rstd], writes=[o])
        kb.dma("sp", HO.ap()[k][:, cs], o[:], sbuf=o, load=False)


def na_tables(rpb):
    col = np.arange(64)
    cs = np.clip(col - 8, 0, 48)
    cmask = (col[None, :] >= cs[:, None]) & (col[None, :] < cs[:, None] + 16)
    dc = np.clip(col[None, :] - col[:, None], -15, 15) + 15
    t = rpb[:, :, dc]
    rpbT = np.ascontiguousarray(t.transpose(0, 3, 1, 2))
    mask = np.ascontiguousarray(np.broadcast_to(cmask.T[:, None, :], (64, 15, 64)).astype(np.float32))
    return rpbT, mask


def build_na(kb_=None):
    kb = kb_ if kb_ is not None else KB()
    H = 16
    SC = 128 ** -0.5
    QK = kb.din("QK", [32, 128, NT])
    Vr = kb.din("Vr", [64, 36, D])
    Vc = kb.din("Vc", [128, 2, D])
    RP = kb.din("RP", [H, 64, 15 * 64])
    MK = kb.din("MK", [64, 15 * 64])
    OT = kb.dout("OT", [H, 128, NT])
    ones = kb.sb("ones", [128, 128], BF16)
    kb.op("dve", lambda e: e.memset(ones[:], 1.0), writes=[ones])
    mk = kb.sb("mk", [64, 960])
    kb.dma("sp", mk[:], MK.ap(), sbuf=mk, load=True)
    qst = [kb.sb(f"qst{i}", [128, NT]) for i in range(2)]
    qb = [kb.sb(f"qb{i}", [128, NT], BF16) for i in range(2)]
    kbb = [kb.sb(f"kbb{i}", [128, NT], BF16) for i in range(2)]
    vst = kb.sb("vst", [64, 36, 128])
    vb = [kb.sb(f"vb{i}", [64, 36, 128], BF16) for i in range(2)]
    vcst = kb.sb("vcst", [128, 2, 128])
    vcb = [kb.sb(f"vcb{i}", [128, 2, 128], BF16) for i in range(2)]
    rst = kb.sb("rst", [64, 960])
    eb = [kb.sb(f"eb{i}", [64, 960]) for i in range(2)]
    E = [kb.sb(f"E{i}", [64, 512]) for i in range(2)]
    E2 = [kb.sb(f"E2{i}", [64, 512], BF16) for i in range(3)]
    Ec = [kb.sb(f"Ec{i}", [128, 512], BF16) for i in range(3)]
    rec = [kb.sb(f"rec{i}", [128, 512]) for i in range(2)]
    oo = [kb.sb(f"oo{i}", [128, 512]) for i in range(2)]
    PO = [kb.ps(f"PO{i}", [128, 512]) for i in range(2)]
    PS = [kb.ps(f"PS{i}", [128, 512]) for i in range(2)]
    PT = [kb.ps(f"PT{i}", [128, 512]) for i in range(2)]
    PC = [kb.ps(f"PC{i}", [128, 512]) for i in range(2)]
    ie = 0
    for h in range(H):
        q_, k_, v_, vc_, eb_ = qb[h % 2], kbb[h % 2], vb[h % 2], vcb[h % 2], eb[h % 2]
        kb.dma("sp", qst[0][:], QK.ap()[h], sbuf=qst[0], load=True)
        kb.op("pool", lambda e: e.tensor_copy(out=q_[:], in_=qst[0][:]), reads=[qst[0]], writes=[q_])
        kb.dma("sp", qst[1][:], QK.ap()[16 + h], sbuf=qst[1], load=True)
        kb.op("pool", lambda e: e.tensor_copy(out=k_[:], in_=qst[1][:]), reads=[qst[1]], writes=[k_])
        kb.dma("act", vst[:], Vr.ap()[:, :, h * 128:(h + 1) * 128], sbuf=vst, load=True)
        kb.op("pool", lambda e: e.tensor_copy(out=v_[:], in_=vst[:]), reads=[vst], writes=[v_])
        kb.dma("act", vcst[:], Vc.ap()[:, :, h * 128:(h + 1) * 128], sbuf=vcst, load=True)
        kb.op("pool", lambda e: e.tensor_copy(out=vc_[:], in_=vcst[:]), reads=[vcst], writes=[vc_])
        kb.dma("sp", rst[:], RP.ap()[h], sbuf=rst, load=True)
        kb.act(rst[:], rst[:], AF.Exp, [rst], [rst])
        kb.op("dve", lambda e: e.tensor_tensor(out=eb_[:], in0=rst[:], in1=mk[:], op=ALU.mult), reads=[rst, mk], writes=[eb_])
        def stage1(r):
            nonlocal ie
            rs = min(max(r - 4, 0), 24)
            dr0 = rs - r + 7
            pt, pc = PT[ie % 2], PC[ie % 2]
            e1, e2, ec = E[ie % 2], E2[ie % 3], Ec[ie % 3]
            ie += 1
            qs = q_[:, r * 64:(r + 1) * 64]
            for j in range(8):
                kb.mm(pt[0:64, j * 64:(j + 1) * 64], k_[:, (rs + j) * 64:(rs + j + 1) * 64], qs, True, True, [k_, q_], [pt],
                      inc=(j == 7))
            for t in range(2):
                kb.mm(pc[:, t * 64:(t + 1) * 64], k_[:, 2048 + t * 128:2048 + (t + 1) * 128], qs, True, True, [k_, q_], [pc],
                      inc=(t == 1))
            kb.act(e1[:], pt[0:64, :], AF.Exp, [pt], [e1], scale=SC)
            kb.op("pool" if ie % 2 else "dve",
                  lambda e: e.tensor_tensor(out=e2[:], in0=e1[:], in1=eb_[:, dr0 * 64:dr0 * 64 + 512], op=ALU.mult),
                  reads=[e1, eb_], writes=[e2])
            kb.act(ec[:, 0:128], pc[:, 0:128], AF.Exp, [pc], [ec], scale=SC)
            return (r, rs, e2, ec)

        def stage2(item):
            r, rs, e2, ec = item
            rg, rr = r // 8, r % 8
            po, ps_ = PO[rg % 2], PS[rg % 2]
            cs = slice(rr * 64, (rr + 1) * 64)
            for j in range(8):
                kb.mm(po[:, cs], v_[0:64, rs + j, :], e2[:, j * 64:(j + 1) * 64], j == 0, False, [v_, e2], [po])
            for t in range(2):
                kb.mm(po[:, cs], vc_[:, t, :], ec[:, t * 64:(t + 1) * 64], False, t == 1, [vc_, ec], [po], inc=False)
            for j in range(8):
                kb.mm(ps_[:, cs], ones[0:64, :], e2[:, j * 64:(j + 1) * 64], j == 0, False, [ones, e2], [ps_])
            for t in range(2):
                kb.mm(ps_[:, cs], ones[:, :], ec[:, t * 64:(t + 1) * 64], False, t == 1, [ones, ec], [ps_], inc=(t == 1))
            if rr == 7:
                finish_rg(rg, 512, rg * 512)

        def finish_rg(rg, ncol, c0):
            po, ps_ = PO[rg % 2], PS[rg % 2]
            rc, o_ = rec[rg % 2], oo[rg % 2]
            kb.op("dve", lambda e: e.reciprocal(out=rc[:, 0:ncol], in_=ps_[:, 0:ncol]), reads=[ps_], writes=[rc])
            kb.op("dve", lambda e: e.tensor_tensor(out=o_[:, 0:ncol], in0=po[:, 0:ncol], in1=rc[:, 0:ncol], op=ALU.mult),
                  reads=[po, rc], writes=[o_])
            kb.dma("sp", OT.ap()[h][:, c0:c0 + ncol], o_[:, 0:ncol], sbuf=o_, load=False)

        pend = None
        for r in range(32):
            cur = stage1(r)
            if pend is not None:
                stage2(pend)
            pend = cur
        stage2(pend)
        rg = 4
        po, ps_ = PO[rg % 2], PS[rg % 2]
        pc = PC[ie % 2]
        ec = Ec[ie % 3]
        ie += 1
        qs = q_[:, 2048:2304]
        for t in range(2):
            kb.mm(pc[:, t * 256:(t + 1) * 256], k_[:, 2048 + t * 128:2048 + (t + 1) * 128], qs, True, True, [k_, q_], [pc],
                  inc=(t == 1))
        kb.act(ec[:], pc[:], AF.Exp, [pc], [ec], scale=SC)
        for t in range(2):
            kb.mm(po[:, 0:256], vc_[:, t, :], ec[:, t * 256:(t + 1) * 256], t == 0, t == 1, [vc_, ec], [po], inc=False)
        for t in range(2):
            kb.mm(ps_[:, 0:256], ones[:, :], ec[:, t * 256:(t + 1) * 256], t == 0, t == 1, [ones, ec], [ps_], inc=(t == 1))
        finish_rg(rg, 256, 2048)
    kb.finish()
    return kb


def na_inputs(Y, rpb):
    YT = to_fm(Y)
    V = Y[:, 4096:6144]
    Vr = np.ascontiguousarray(V.reshape(36, 64, D).transpose(1, 0, 2))
    Vc = np.ascontiguousarray(V[2048:].reshape(2, 128, D).transpose(1, 0, 2))
    rpbT, mask = na_tables(rpb)
    return {"QK": np.ascontiguousarray(YT[0:32]), "Vr": Vr, "Vc": Vc,
            "RP": rpbT.reshape(16, 64, 960), "MK": mask.reshape(64, 960)}


def rope_tables():
    t = np.arange(2048)
    pos = np.stack([t // 64, t % 64], -1).astype(np.float32)
    inv = (10000.0 ** (-np.arange(32, dtype=np.float32) / 32)).astype(np.float32)
    ang = pos[:, :, None] * inv
    cos, sin = np.cos(ang).astype(np.float32), np.sin(ang).astype(np.float32)
    C = np.zeros((128, 2048), np.float32)
    S = np.zeros((128, 2048), np.float32)
    for a in range(2):
        for b in range(2):
            sl = slice(a * 64 + b * 32, a * 64 + b * 32 + 32)
            C[sl] = cos[:, a, :].T
            S[sl] = (-1.0 if b == 0 else 1.0) * sin[:, a, :].T
    P = np.zeros((128, 128), np.float32)
    for d in range(128):
        P[d ^ 32, d] = 1.0
    return C, S, P


def build_da(lambda_init, kb_=None):
    kb = kb_ if kb_ is not None else KB()
    SC = 128 ** -0.5
    QK = kb.din("QK", [32, 128, NT])
    Vt = kb.din("Vt", [128, 18, D])
    COS = kb.din("COS", [128, 2048])
    SIN = kb.din("SIN", [128, 2048])
    PM = kb.din("PM", [128, 128])
    LAM = kb.din("LAM", [1, 512])
    SW = kb.din("SW", [128, 2])
    OT = kb.dout("OT", [16, 128, NT])
    onesb = kb.sb("onesb", [128, 128], BF16)
    onesf = kb.sb("onesf", [128, 128])
    kb.op("dve", lambda e: e.memset(onesb[:], 1.0), writes=[onesb])
    kb.op("dve", lambda e: e.memset(onesf[:], 1.0), writes=[onesf])
    eps5 = kb.sb("eps5", [128, 1])
    kb.op("dve", lambda e: e.memset(eps5[:], 1e-5), writes=[eps5])
    cos = kb.sb("cos", [128, 2048]); sin = kb.sb("sin", [128, 2048]); pm = kb.sb("pm", [128, 128])
    kb.dma("sp", cos[:], COS.ap(), sbuf=cos, load=True)
    kb.dma("act", sin[:], SIN.ap(), sbuf=sin, load=True)
    kb.dma("sp", pm[:], PM.ap(), sbuf=pm, load=True)
    lam = kb.sb("lam", [1, 512]); sw = kb.sb("sw", [128, 2])
    kb.dma("sp", lam[:], LAM.ap(), sbuf=lam, load=True)
    kb.dma("sp", sw[:], SW.ap(), sbuf=sw, load=True)
    kb.op("dve", lambda e: e.tensor_scalar(out=sw[:], in0=sw[:], scalar1=1.0 - lambda_init, scalar2=None, op0=ALU.mult),
          reads=[sw], writes=[sw])
    lp = kb.sb("lp", [1, 256]); ls = kb.sb("ls", [1, 2]); lf = kb.sb("lf", [1, 2]); nl = kb.sb("nl", [128, 2])
    kb.op("dve", lambda e: e.tensor_tensor(out=lp[:, 0:128], in0=lam[:, 0:128], in1=lam[:, 128:256], op=ALU.mult), reads=[lam], writes=[lp])
    kb.op("dve", lambda e: e.tensor_tensor(out=lp[:, 128:256], in0=lam[:, 256:384], in1=lam[:, 384:512], op=ALU.mult), reads=[lam], writes=[lp])
    kb.op("dve", lambda e: e.reduce_sum(out=ls[:], in_=lp[:].rearrange("o (a b) -> o a b", a=2), axis=AX.X), reads=[lp], writes=[ls])
    kb.act(ls[:], ls[:], AF.Exp, [ls], [ls])
    for c in range(2):
        kb.op("dve", lambda e: e.scalar_tensor_tensor(out=lf[:, c:c + 1], in0=ls[:, 1:2], scalar=-lambda_init, in1=ls[:, 0:1],
                                                      op0=ALU.add, op1=ALU.subtract), reads=[ls], writes=[lf])
    PSb = [kb.ps(f"P{i}", [128, 512]) for i in range(8)]
    kb.mm(PSb[0][:, 0:2], onesf[0:1, :], lf[:], True, True, [onesf, lf], [PSb[0]])
    kb.op("dve", lambda e: e.tensor_copy(out=nl[:], in_=PSb[0][:, 0:2]), reads=[PSb[0]], writes=[nl])
    st = [kb.sb(f"st{i}", [128, NT]) for i in range(2)]
    t1 = [kb.sb(f"t1{i}", [128, 512]) for i in range(2)]
    t2 = [kb.sb(f"t2{i}", [128, 512]) for i in range(2)]
    qb = [kb.sb(f"qb{i}", [128, NT], BF16) for i in range(4)]
    kbb = [kb.sb(f"kbb{i}", [128, NT], BF16) for i in range(4)]
    vst = kb.sb("vst", [128, 18, 256])
    vb = [kb.sb(f"vb{i}", [128, 18, 256], BF16) for i in range(2)]
    E = [kb.sb(f"E{i}", [128, 512], BF16) for i in range(3)]
    rc = [kb.sb(f"rc{i}", [128, 512]) for i in range(2)]
    a0 = kb.sb("a0", [128, 512]); a1 = kb.sb("a1", [128, 512])
    oe = [kb.sb(f"oe{i}", [128, 512]) for i in range(2)]
    sq = kb.sb("sq", [128, 512]); rs_ = kb.sb("rs_", [128, 512])
    fo = [kb.sb(f"fo{i}", [128, 512]) for i in range(2)]
    ist = 0; ie = 0; it = 0

    def load_rope(dst, chunk):
        nonlocal ist, it
        s = st[ist % 2]; ist += 1
        kb.dma("sp", s[:], QK.ap()[chunk], sbuf=s, load=True)
        for b4 in range(4):
            cs = slice(b4 * 512, (b4 + 1) * 512)
            p = PSb[6 + (it % 2)]
            a, b_ = t1[it % 2], t2[it % 2]; it += 1
            kb.mm(p[:], pm[:], s[:, cs], True, True, [pm, s], [p])
            kb.op("dve", lambda e: e.tensor_tensor(out=a[:], in0=s[:, cs], in1=cos[:, cs], op=ALU.mult), reads=[s, cos], writes=[a])
            kb.op("dve", lambda e: e.tensor_tensor(out=b_[:], in0=p[:], in1=sin[:, cs], op=ALU.mult), reads=[p, sin], writes=[b_])
            kb.op("pool", lambda e: e.tensor_tensor(out=dst[:, cs], in0=a[:], in1=b_[:], op=ALU.add), reads=[a, b_], writes=[dst])
        kb.op("pool", lambda e: e.tensor_copy(out=dst[:, 2048:NT], in_=s[:, 2048:NT]), reads=[s], writes=[dst])

    for h in range(8):
        par = h % 2
        for c in range(2):
            load_rope(qb[par * 2 + c], 2 * h + c)
            load_rope(kbb[par * 2 + c], 16 + 2 * h + c)
        v_ = vb[par]
        kb.dma("act", vst[:], Vt.ap()[:, :, h * 256:(h + 1) * 256], sbuf=vst, load=True)
        kb.op("pool", lambda e: e.tensor_copy(out=v_[:], in_=vst[:]), reads=[vst], writes=[v_])
        for qblk in range(5):
            c0 = qblk * 512
            ncol = 512 if qblk < 4 else 256
            kts = list(range(18)) if qblk < 4 else [16, 17]
            for c in range(2):
                q_, k_ = qb[par * 2 + c], kbb[par * 2 + c]
                po = [PSb[c * 3], PSb[c * 3 + 1]]
                ps_ = PSb[c * 3 + 2]
                def pv(i, kt, e_):
                    fst, lst = i == 0, i == len(kts) - 1
                    for e2 in range(2):
                        kb.mm(po[e2][:, 0:ncol], v_[:, kt, e2 * 128:(e2 + 1) * 128], e_[:, 0:ncol], fst, lst, [v_, e_], [po[e2]], inc=False)
                    kb.mm(ps_[:, 0:ncol], onesb[:], e_[:, 0:ncol], fst, lst, [onesb, e_], [ps_], inc=True)

                pend = None
                for i, kt in enumerate(kts):
                    pt = PSb[6 + (it % 2)]; it += 1
                    e_ = E[ie % 3]; ie += 1
                    kb.mm(pt[:, 0:ncol], k_[:, kt * 128:(kt + 1) * 128], q_[:, c0:c0 + ncol], True, True, [k_, q_], [pt])
                    kb.act(e_[:, 0:ncol], pt[:, 0:ncol], AF.Exp, [pt], [e_], scale=SC)
                    if pend is not None:
                        pv(*pend)
                    pend = (i, kt, e_)
                pv(*pend)
            n = slice(0, ncol)
            for c in range(2):
                kb.op("dve", lambda e: e.reciprocal(out=rc[c][:, n], in_=PSb[c * 3 + 2][:, n]), reads=[PSb[c * 3 + 2]], writes=[rc[c]])
            pss = PSb[6 + (it % 2)]; it += 1
            for e2 in range(2):
                kb.op("dve", lambda e: e.tensor_tensor(out=a0[:, n], in0=PSb[e2][:, n], in1=rc[0][:, n], op=ALU.mult),
                      reads=[PSb[e2], rc[0]], writes=[a0])
                kb.op("dve", lambda e: e.tensor_tensor(out=a1[:, n], in0=PSb[3 + e2][:, n], in1=rc[1][:, n], op=ALU.mult),
                      reads=[PSb[3 + e2], rc[1]], writes=[a1])
                kb.op("dve", lambda e: e.scalar_tensor_tensor(out=oe[e2][:, n], in0=a1[:, n], scalar=nl[:, 0:1], in1=a0[:, n],
                                                               op0=ALU.mult, op1=ALU.add), reads=[a1, nl, a0], writes=[oe[e2]])
                kb.act(sq[:, n], oe[e2][:, n], AF.Square, [oe[e2]], [sq])
                kb.mm(pss[:, n], onesf[:], sq[:, n], e2 == 0, e2 == 1, [onesf, sq], [pss], inc=True)
            kb.act(rs_[:, n], pss[:, n], AF.Sqrt, [pss, eps5], [rs_], bias=eps5[:], scale=1.0 / 256)
            kb.op("dve", lambda e: e.reciprocal(out=rs_[:, n], in_=rs_[:, n]), reads=[rs_], writes=[rs_])
            for e2 in range(2):
                kb.op("dve", lambda e: e.scalar_tensor_tensor(out=fo[e2][:, n], in0=oe[e2][:, n], scalar=sw[:, e2:e2 + 1], in1=rs_[:, n],
                                                              op0=ALU.mult, op1=ALU.mult), reads=[oe[e2], sw, rs_], writes=[fo[e2]])
                kb.dma("sp", OT.ap()[2 * h + e2][:, c0:c0 + ncol], fo[e2][:, n], sbuf=fo[e2], load=False)
    kb.finish()
    return kb


def da_inputs(Y, lam, subw):
    YT = to_fm(Y)
    V = Y[:, 4096:6144]
    Vt = np.ascontiguousarray(V.reshape(18, 128, D).transpose(1, 0, 2))
    C, S, P = rope_tables()
    return {"QK": np.ascontiguousarray(YT[0:32]), "Vt": Vt, "COS": C, "SIN": S, "PM": P,
            "LAM": np.ascontiguousarray(lam.reshape(1, 512)), "SW": np.ascontiguousarray(subw.reshape(2, 128).T)}


def build_ssm_a(kb_=None):
    kb = kb_ if kb_ is not None else KB()
    XI = kb.din("XI", [49, 128, NT])
    CW = kb.din("CW", [128, 48, 5])
    CB = kb.din("CB", [128, 48])
    DTB = kb.din("DTB", [128, 1])
    ALOG = kb.din("ALOG", [128, 1])
    XO = kb.dout("XO", [48, 128, NT])
    DA_ = kb.dout("DA", [2, 128, NT])
    cw = kb.sb("cw", [128, 48, 5]); cb = kb.sb("cb", [128, 48]); dtb = kb.sb("dtb", [128, 1]); al = kb.sb("al", [128, 1])
    one = kb.sb("one", [128, 1])
    kb.op("dve", lambda e: e.memset(one[:], 1.0), writes=[one])
    for t_, d_ in ((cw, CW), (cb, CB), (dtb, DTB), (al, ALOG)):
        kb.dma("sp", t_[:], d_.ap(), sbuf=t_, load=True)
    kb.act(al[:], al[:], AF.Exp, [al], [al])
    kb.op("dve", lambda e: e.tensor_scalar(out=al[:], in0=al[:], scalar1=-1.0, scalar2=None, op0=ALU.mult), reads=[al], writes=[al])
    xin = [kb.sb(f"xin{i}", [128, NT]) for i in range(2)]
    acc = [kb.sb(f"acc{i}", [128, NT]) for i in range(2)]
    for c in range(48):
        xi, ac = xin[c % 2], acc[c % 2]
        kb.dma("sp" if c % 2 else "act", xi[:], XI.ap()[c], sbuf=xi, load=True)
        kb.op("dve", lambda e: e.tensor_scalar(out=ac[:], in0=xi[:], scalar1=cw[:, c, 2:3], scalar2=cb[:, c:c + 1],
                                               op0=ALU.mult, op1=ALU.add), reads=[xi, cw, cb], writes=[ac])
        for (lo, hi) in ((0, 2048), (2048, NT)):
            for k in (0, 1, 3, 4):
                o = k - 2
                a, b_ = lo + max(0, -o), hi - max(0, o)
                kb.op("dve", lambda e: e.scalar_tensor_tensor(out=ac[:, a:b_], in0=xi[:, a + o:b_ + o], scalar=cw[:, c, k:k + 1],
                                                              in1=ac[:, a:b_], op0=ALU.mult, op1=ALU.add),
                      reads=[xi, cw, ac], writes=[ac])
        kb.act(ac[:], ac[:], AF.Silu, [ac], [ac])
        kb.dma("sp", XO.ap()[c], ac[:], sbuf=ac, load=False)
    xi, ac = xin[0], acc[0]
    kb.dma("sp", xi[:], XI.ap()[48], sbuf=xi, load=True)
    kb.act(xi[:], xi[:], AF.Exp, [xi, dtb], [xi], bias=dtb[:])
    kb.act(xi[:], xi[:], AF.Ln, [xi, one], [xi], bias=one[:])
    kb.dma("sp", DA_.ap()[0], xi[:], sbuf=xi, load=False)
    kb.op("dve", lambda e: e.tensor_scalar(out=ac[:], in0=xi[:], scalar1=al[:, 0:1], scalar2=None, op0=ALU.mult), reads=[xi, al], writes=[ac])
    kb.dma("sp", DA_.ap()[1], ac[:], sbuf=ac, load=False)
    kb.finish()
    return kb


ORD_F = [16, 17] + list(range(16))
ORD_B = [17, 16] + list(range(15, -1, -1))


def ssm_consts():
    i = np.arange(128)
    tri_f = (i[:, None] <= i[None, :]).astype(np.float32)
    tri_b = (i[:, None] >= i[None, :]).astype(np.float32)
    t = np.arange(512)
    mk = np.zeros((8, 128, 512), np.float32)
    for q in range(4):
        mk[q] = ((t[None, :] - 128 * q) >= i[:, None])
        mk[4 + q] = ((t[None, :] - 128 * q) <= i[:, None])
    return tri_f, tri_b, np.eye(128, dtype=np.float32), np.ascontiguousarray(mk.transpose(1, 0, 2))


def build_ssm_b(kb_=None):
    kb = kb_ if kb_ is not None else KB()
    XT_ = kb.din("XTOK", [128, 18, 4096])
    BT = kb.din("BT", [8, 128, NT]); CT = kb.din("CT", [8, 128, NT])
    DTT = kb.din("DTT", [128, 18, 128]); ATT = kb.din("ATT", [128, 18, 128])
    XH = kb.din("XH", [64, 64, NT]); ZH = kb.din("ZH", [64, 64, NT])
    NWH = kb.din("NWH", [64, 64]); DSK = kb.din("DSK", [64, 64])
    TF = kb.din("TF", [128, 128]); TBm = kb.din("TB", [128, 128]); ID = kb.din("ID", [128, 128]); MK = kb.din("MK", [128, 8, 512])
    OT = kb.dout("OT", [64, 64, NT])
    onesf = kb.sb("onesf", [128, 128]); kb.op("dve", lambda e: e.memset(onesf[:], 1.0), writes=[onesf])
    eps = kb.sb("eps", [128, 1]); kb.op("dve", lambda e: e.memset(eps[:], EPS), writes=[eps])
    tf = kb.sb("tf", [128, 128]); tb_ = kb.sb("tb_", [128, 128]); idt = kb.sb("idt", [128, 128]); mk = kb.sb("mk", [128, 8, 512])
    dtt = kb.sb("dtt", [128, 18, 128]); att = kb.sb("att", [128, 18, 128]); nwh = kb.sb("nwh", [64, 64]); dsk = kb.sb("dsk", [64, 64])
    for t_, d_ in ((tf, TF), (tb_, TBm), (idt, ID), (mk, MK), (dtt, DTT), (att, ATT), (nwh, NWH), (dsk, DSK)):
        kb.dma("sp", t_[:], d_.ap(), sbuf=t_, load=True)
    P = [kb.ps(f"P{i}", [128, 512]) for i in range(8)]
    cum = [kb.sb(f"cum{d}", [128, 18, 128]) for d in range(2)]
    cumT = [kb.sb(f"cumT{d}", [128, NT]) for d in range(2)]
    ip = 0
    for d, (order, tri) in enumerate(((ORD_F, tf), (ORD_B, tb_))):
        for oi, n in enumerate(order):
            p = P[ip % 2]; ip += 1
            kb.mm(p[:, 0:128], tri[:], att[:, n, :], True, oi == 0, [tri, att], [p], inc=(oi == 0))
            for mi, m in enumerate(order[:oi]):
                kb.mm(p[:, 0:128], onesf[:], att[:, m, :], False, mi == oi - 1, [onesf, att], [p], inc=(mi == oi - 1))
            kb.op("dve", lambda e: e.tensor_copy(out=cum[d][:, n, :], in_=p[:, 0:128]), reads=[p], writes=[cum[d]])
        for n in range(18):
            p = P[ip % 2]; ip += 1
            kb.op("pe", lambda e: e.transpose(out=p[:, 0:128], in_=cum[d][:, n, :], identity=idt[:]), reads=[cum[d], idt], writes=[p])
            kb.act(cumT[d][:, n * 128:(n + 1) * 128], p[:, 0:128], AF.Copy, [p], [cumT[d]])
    xst = [kb.sb(f"xst{i}", [128, 512]) for i in range(3)]
    xdt = [kb.sb(f"xdt{d}", [128, 18, 512], BF16) for d in range(2)]
    bst = kb.sb("bst", [128, NT])
    bb = kb.sb("bb", [128, NT], BF16); cbf = kb.sb("cbf", [128, NT], BF16)
    for d in range(2):
        kb.op("dve", lambda e: e.tensor_scalar(out=cum[d][:], in0=cum[d][:], scalar1=-1.0, scalar2=None, op0=ALU.mult),
              reads=[cum[d]], writes=[cum[d]])
    ncum = cum
    sel = [kb.sb(f"sel{i}", [128, 128]) for i in range(2)]
    Aall = [kb.sb(f"Aall{i}", [128, 512]) for i in range(16)]
    Dt = [kb.sb(f"Dt{i}", [128, 512]) for i in range(3)]
    Mt = [kb.sb(f"Mt{i}", [128, 512], BF16) for i in range(3)]
    yz = kb.sb("yz", [64, 8, 512])
    xh = [kb.sb(f"xh{i}", [64, 512]) for i in range(2)]
    zh = [kb.sb(f"zh{i}", [64, 512]) for i in range(2)]
    sqs = kb.sb("sqs", [64, 512]); rst = kb.sb("rst", [64, 512])
    fo = [kb.sb(f"fo{i}", [64, 512]) for i in range(2)]
    PC = [P[6], P[7], P[0], P[1]]
    isel = 0; iw = 0; ih = 0; ix = 0
    for g in range(8):
        for n in range(18):
            xs_ = xst[ix % 3]; ix += 1
            kb.dma("sp" if n % 2 else "act", xs_[:], XT_.ap()[:, n, g * 512:(g + 1) * 512], sbuf=xs_, load=True)
            for d in range(2):
                for r in range(8):
                    col = d * 64 + g * 8 + r
                    kb.op("dve",
                          lambda e: e.tensor_scalar(out=xdt[d][:, n, r * 64:(r + 1) * 64], in0=xs_[:, r * 64:(r + 1) * 64],
                                                    scalar1=dtt[:, n, col:col + 1], scalar2=None, op0=ALU.mult),
                          reads=[xs_, dtt], writes=[xdt[d]])
        kb.dma("sp", bst[:], BT.ap()[g], sbuf=bst, load=True)
        kb.op("pool", lambda e: e.tensor_copy(out=bb[:], in_=bst[:]), reads=[bst], writes=[bb])
        kb.dma("sp", bst[:], CT.ap()[g], sbuf=bst, load=True)
        kb.op("pool", lambda e: e.tensor_copy(out=cbf[:], in_=bst[:]), reads=[bst], writes=[cbf])
        for tb in range(5):
            c0 = tb * 512
            ncol = 512 if tb < 4 else 256
            n_ = slice(0, ncol)
            ttiles = list(range(4 * tb, 4 * tb + 4)) if tb < 4 else [16, 17]
            for r in range(8):
                for d in range(2):
                    col = d * 64 + g * 8 + r
                    s_ = sel[isel % 2]; isel += 1
                    a_ = Aall[r * 2 + d]
                    kb.op("dve", lambda e: e.tensor_scalar(out=s_[:], in0=onesf[:], scalar1=idt[:, col:col + 1], scalar2=None, op0=ALU.mult),
                          reads=[onesf, idt], writes=[s_])
                    pa = P[4 + (isel % 2)]
                    kb.mm(pa[:, n_], s_[:], cumT[d][:, c0:c0 + ncol], True, True, [s_, cumT[d]], [pa])
                    kb.act(a_[:, n_], pa[:, n_], AF.Copy, [pa], [a_])
            for r in range(8):
                hd = g * 8 + r
                py = P[2 + (r % 2)]
                work = []
                for d in range(2):
                    col = d * 64 + hd
                    a_ = Aall[r * 2 + d]
                    if tb < 4:
                        full = [16, 17] + (list(range(0, 4 * tb)) if d == 0 else list(range(4 * tb + 4, 16)))
                    else:
                        full = []
                    work += [(d, col, a_, s, None) for s in full] + [(d, col, a_, s, qi) for qi, s in enumerate(ttiles)]
                pend = None

                def second(item, first, lastp):
                    d, col, a_, s, qi, pc, D_, M_ = item
                    kb.op("dve", lambda e: e.tensor_tensor(out=M_[:, n_], in0=pc[:, n_], in1=D_[:, n_], op=ALU.mult),
                          reads=[pc, D_], writes=[M_])
                    kb.mm(py[0:64, n_], xdt[d][:, s, r * 64:(r + 1) * 64], M_[:, n_], first, lastp, [xdt[d], M_], [py], inc=True)

                for wi, (d, col, a_, s, qi) in enumerate(work):
                    pc = PC[iw % 4]
                    D_ = Dt[iw % 3]; M_ = Mt[iw % 3]; iw += 1
                    kb.mm(pc[:, n_], bb[:, s * 128:(s + 1) * 128], cbf[:, c0:c0 + ncol], True, True, [bb, cbf], [pc])
                    if qi is None:
                        kb.act(D_[:, n_], a_[:, n_], AF.Exp, [a_, ncum[d]], [D_], bias=ncum[d][:, s, col:col + 1])
                    else:
                        kb.op("dve", lambda e: e.tensor_scalar(out=D_[:, n_], in0=a_[:, n_], scalar1=cum[d][:, s, col:col + 1], scalar2=0.0,
                                                                op0=ALU.add, op1=ALU.min), reads=[a_, cum[d]], writes=[D_])
                        kb.act(D_[:, n_], D_[:, n_], AF.Exp, [D_], [D_])
                        kb.op("pool", lambda e: e.tensor_tensor(out=D_[:, n_], in0=D_[:, n_], in1=mk[:, d * 4 + qi, n_], op=ALU.mult),
                              reads=[D_, mk], writes=[D_])
                    if pend is not None:
                        second(pend, wi == 1, False)
                    pend = (d, col, a_, s, qi, pc, D_, M_)
                second(pend, len(work) == 1, True)
                x_, z_ = xh[ih % 2], zh[ih % 2]; ih += 1
                kb.dma("sp", x_[:, n_], XH.ap()[hd][:, c0:c0 + ncol], sbuf=x_, load=True)
                kb.dma("act", z_[:, n_], ZH.ap()[hd][:, c0:c0 + ncol], sbuf=z_, load=True)
                kb.act(z_[:, n_], z_[:, n_], AF.Silu, [z_], [z_])
                kb.op("dve", lambda e: e.scalar_tensor_tensor(out=x_[:, n_], in0=x_[:, n_], scalar=dsk[:, hd:hd + 1], in1=py[0:64, n_],
                                                              op0=ALU.mult, op1=ALU.add), reads=[x_, dsk, py], writes=[x_])
                kb.op("pool", lambda e: e.tensor_tensor(out=yz[:, r, n_], in0=x_[:, n_], in1=z_[:, n_], op=ALU.mult),
                      reads=[x_, z_], writes=[yz])
            pn = P[4]
            for r in range(8):
                kb.act(sqs[:, n_], yz[:, r, n_], AF.Square, [yz], [sqs])
                kb.mm(pn[0:64, n_], onesf[0:64, 0:64], sqs[:, n_], r == 0, r == 7, [onesf, sqs], [pn], inc=True)
            kb.act(rst[:, n_], pn[0:64, n_], AF.Sqrt, [pn, eps], [rst], bias=eps[0:64, :], scale=1.0 / 512)
            kb.op("dve", lambda e: e.reciprocal(out=rst[:, n_], in_=rst[:, n_]), reads=[rst], writes=[rst])
            for r in range(8):
                hd = g * 8 + r
                f_ = fo[r % 2]
                kb.op("dve", lambda e: e.scalar_tensor_tensor(out=f_[:, n_], in0=yz[:, r, n_], scalar=nwh[:, hd:hd + 1], in1=rst[:, n_],
                                                              op0=ALU.mult, op1=ALU.mult), reads=[yz, nwh, rst], writes=[f_])
                kb.dma("sp", OT.ap()[hd][:, c0:c0 + ncol], f_[:, n_], sbuf=f_, load=False)
    kb.finish()
    return kb


class View:
    def __init__(self, ap):
        self._ap = ap

    def ap(self):
        return self._ap


def emit_xpose(kb, ident_d, get_in, put_out, nblk_p, nblk_f, pin=128):
    idt = kb.sb("idt", [128, 128])
    kb.dma("sp", idt[:], ident_d.ap(), sbuf=idt, load=True)
    W = nblk_f * 128
    src = [kb.sb(f"src{i}", [128, W]) for i in range(2)]
    ps = [kb.ps(f"tp{i}", [128, 512]) for i in range(4)]
    GB = 4
    dst = [kb.sb(f"dst{i}", [128, GB, nblk_p * pin]) for i in range(1)]
    ip = 0
    for b0 in range(0, nblk_f, GB):
        nb = min(GB, nblk_f - b0)
        d = dst[0]
        for a in range(nblk_p):
            s_ = src[a % 2]
            kb.dma("sp" if a % 2 else "act", s_[0:pin, 0:nb * 128], get_in(a)[:, b0 * 128:(b0 + nb) * 128], sbuf=s_, load=True)
            p = ps[ip % 4]; ip += 1
            for bb in range(nb):
                kb.op("pe", lambda e: e.transpose(out=p[:, bb * pin:(bb + 1) * pin], in_=s_[0:pin, bb * 128:(bb + 1) * 128],
                                                  identity=idt[0:pin, 0:pin]), reads=[s_, idt], writes=[p], inc=(bb == nb - 1))
            for bb in range(nb):
                kb.act(d[:, bb, a * pin:(a + 1) * pin], p[:, bb * pin:(bb + 1) * pin], AF.Copy, [p], [d])
        for bb in range(nb):
            kb.dma("sp", put_out(b0 + bb), d[:, bb, :], sbuf=d, load=False)
    kb.finish()


def emit_ada_fm(kb, condT_d, w_d, b_d, ident_d, MOD):
    ct = kb.sb("ct", [128, 16, 2]); cs = kb.sb("cs", [128, 16, 2])
    idt = kb.sb("idt", [128, 128])
    kb.dma("sp", idt[:], ident_d.ap(), sbuf=idt, load=True)
    kb.dma("sp", ct[:], condT_d.ap(), sbuf=ct, load=True)
    kb.act(cs[:], ct[:], AF.Silu, [ct], [cs])
    wst = [kb.sb(f"wst{i}", [128, 16, 128]) for i in range(3)]
    wbf = [kb.sb(f"wbf{i}", [128, 16, 128], BF16) for i in range(3)]
    csb = kb.sb("csb", [128, 16, 2], BF16)
    kb.op("dve", lambda e: e.tensor_copy(out=csb[:], in_=cs[:]), reads=[cs], writes=[csb])
    bst = kb.sb("bst", [96, 128]); bT = kb.sb("bT", [128, 96])
    msb = [kb.sb(f"msb{i}", [128, 2, 96]) for i in range(2)]
    ps = [kb.ps(f"ap{i}", [128, 512]) for i in range(4)]
    i = 0
    for l in range(4):
        kb.dma("act", bst[:], b_d.ap()[l].rearrange("(m p) -> m p", p=128), sbuf=bst, load=True)
        pb = ps[3]
        kb.op("pe", lambda e: e.transpose(out=pb[:, 0:96], in_=bst[:], identity=idt[0:96, 0:96]), reads=[bst, idt], writes=[pb])
        kb.op("dve", lambda e: e.tensor_copy(out=bT[:], in_=pb[:, 0:96]), reads=[pb], writes=[bT])
        ms = msb[l % 2]
        for m in range(96):
            st = wst[i % 3]
            p = ps[i % 3]; i += 1
            kb.dma("sp" if m % 2 else "act", st[:], w_d.ap()[l][:, m * 128:(m + 1) * 128].rearrange("(k p) c -> p k c", p=128),
                   sbuf=st, load=True)
            wb_ = wbf[i % 3]
            if m % 2:
                kb.act(wb_[:], st[:], AF.Copy, [st], [wb_])
            else:
                kb.op("dve", lambda e: e.tensor_copy(out=wb_[:], in_=st[:]), reads=[st], writes=[wb_])
            for k in range(16):
                kb.mm(p[:, 0:2], wb_[:, k, :], csb[:, k, :], k == 0, k == 15, [wb_, csb], [p])
            kb.op("dve", lambda e: e.tensor_scalar(out=ms[:, :, m], in0=p[:, 0:2], scalar1=bT[:, m:m + 1], scalar2=None, op0=ALU.add),
                  reads=[p, bT], writes=[ms])
        kb.dma("sp", MOD.ap()[l], ms[:], sbuf=ms, load=False)
    kb.finish()


def build_fused():
    kb = KB()
    kb.fused = True
    nc = kb.nc
    ext = lambda n, sh: nc.dram_tensor(n, list(sh), F32, kind="ExternalInput")
    XTM = ext("XTM", [NT, D]); CONDT = ext("CONDT", [128, 16, 2]); ADAW = ext("ADAW", [4, D, 6 * D]); ADAB = ext("ADAB", [4, 6 * D])
    NMIX = ext("NMIX", [4, 128, 16]); NMLP = ext("NMLP", [4, 128, 16]); FNW = ext("FNW", [128, 16])
    W1 = ext("W1", [4, D, 4 * D]); W2 = ext("W2", [4, 4 * D, D])
    NAQKV = ext("NAQKV", [2, D, 6144]); NAWO = ext("NAWO", [2, D, D]); RP = ext("RP", [2, 16, 64, 960]); MKN = ext("MKN", [64, 960])
    SSWIN = ext("SSWIN", [D, 10368]); SSWO = ext("SSWO", [4096, D]); CW = ext("CW", [128, 48, 5]); CB = ext("CB", [128, 48])
    DTB = ext("DTB", [128, 1]); ALOG = ext("ALOG", [128, 1]); NWH = ext("NWH", [64, 64]); DSK = ext("DSK", [64, 64])
    TF = ext("TF", [128, 128]); TBm = ext("TBM", [128, 128]); ID = ext("ID", [128, 128]); MKS = ext("MKS", [128, 8, 512])
    DAQKV = ext("DAQKV", [D, 6144]); DAWO = ext("DAWO", [D, D]); COS = ext("COS", [128, 2048]); SIN = ext("SIN", [128, 2048])
    PM = ext("PM", [128, 128]); LAM = ext("LAM", [1, 512]); SW = ext("SW", [128, 2])
    OUT = nc.dram_tensor("OUT", [2048, D], F32, kind="ExternalOutput")
    hA = kb.scratch("hA", [16, 128, NT]); hB = kb.scratch("hB", [16, 128, NT])
    YT = kb.scratch("YT", [81, 128, NT]); VTM = kb.scratch("VTM", [NT, 4096]); XO = kb.scratch("XO", [48, 128, NT])
    DAo = kb.scratch("DAo", [2, 128, NT]); DTM = kb.scratch("DTM", [NT, 256]); OTs = kb.scratch("OTs", [32, 128, NT])
    MOD = kb.scratch("MOD", [4, 128, 2, 96])

    def stage(prefix, fn):
        kb.push(prefix)
        fn()
        kb.pop()

    stage("ti_", lambda: emit_xpose(kb, ID, lambda a: XTM.ap()[a * 128:(a + 1) * 128, :],
                                    lambda b: hA.ap()[b], 18, 16))
    stage("ad_", lambda: emit_ada_fm(kb, CONDT, ADAW, ADAB, ID, MOD))
    hcur, hnxt = hA, hB
    import math
    for i in range(4):
        last = i == 3
        mixer, j = i % 3, i // 3
        modv = View(MOD.ap()[i].rearrange("p w (a c) -> p w a c", a=6))
        Wt = (View(NAQKV.ap()[j]), SSWIN, DAQKV)[mixer]
        F = (6144, 10368, 6144)[mixer]
        ytv = View(YT.ap()[0:F // 128])
        kb.io = {"hT": hcur, "modT": modv, "normw": View(NMIX.ap()[i]), "W": Wt, "YT": ytv}
        stage(f"pr{i}_", lambda: build_pre(F, kb_=kb))
        if mixer in (0, 2):
            stage(f"xv{i}_", lambda: emit_xpose(kb, ID, lambda a: YT.ap()[32 + a], lambda b: VTM.ap()[b * 128:(b + 1) * 128, 0:2048],
                                                16, 18))
            vt = VTM.ap()[:, 0:2048]
        if mixer == 0:
            kb.io = {"QK": View(YT.ap()[0:32]), "Vr": View(vt.rearrange("(r c) f -> c r f", c=64)),
                     "Vc": View(vt[2048:NT].rearrange("(t p) f -> p t f", p=128)), "RP": View(RP.ap()[j]), "MK": MKN,
                     "OT": View(OTs.ap()[0:16])}
            stage(f"na{i}_", lambda: build_na(kb_=kb))
            Wo, Fin = View(NAWO.ap()[j]), 2048
        elif mixer == 2:
            kb.io = {"QK": View(YT.ap()[0:32]), "Vt": View(vt.rearrange("(t p) f -> p t f", p=128)), "COS": COS, "SIN": SIN, "PM": PM,
                     "LAM": LAM, "SW": SW, "OT": View(OTs.ap()[0:16])}
            li = 0.8 - 0.6 * math.exp(-0.3 * i)
            stage(f"da{i}_", lambda: build_da(li, kb_=kb))
            Wo, Fin = DAWO, 2048
        else:
            kb.io = {"XI": View(YT.ap()[32:81]), "CW": CW, "CB": CB, "DTB": DTB, "ALOG": ALOG, "XO": XO, "DA": DAo}
            stage(f"sa{i}_", lambda: build_ssm_a(kb_=kb))
            stage(f"xx{i}_", lambda: emit_xpose(kb, ID, lambda a: XO.ap()[a], lambda b: VTM.ap()[b * 128:(b + 1) * 128, :], 32, 18))
            stage(f"xd{i}_", lambda: emit_xpose(kb, ID, lambda a: DAo.ap()[a], lambda b: DTM.ap()[b * 128:(b + 1) * 128, :], 2, 18))
            tokv = lambda ap_: View(ap_.rearrange("(t p) f -> p t f", p=128))
            kb.io = {"XTOK": tokv(VTM.ap()), "BT": View(XO.ap()[32:40]), "CT": View(XO.ap()[40:48]),
                     "DTT": tokv(DTM.ap()[:, 0:128]), "ATT": tokv(DTM.ap()[:, 128:256]),
                     "XH": View(XO.ap()[0:32].rearrange("c (two p) t -> (c two) p t", two=2)),
                     "ZH": View(YT.ap()[0:32].rearrange("c (two p) t -> (c two) p t", two=2)),
                     "NWH": NWH, "DSK": DSK, "TF": TF, "TB": TBm, "ID": ID, "MK": MKS,
                     "OT": View(OTs.ap().rearrange("c (two p) t -> (c two) p t", two=2))}
            stage(f"sb{i}_", lambda: build_ssm_b(kb_=kb))
            Wo, Fin = SSWO, 4096
        kb.io = {"hT": hcur, "OT": View(OTs.ap()[0:Fin // 128]), "modT": modv, "normw": View(NMLP.ap()[i]), "Wo": Wo,
                 "W1": View(W1.ap()[i]), "W2": View(W2.ap()[i]), "fnw": FNW, "HO": hnxt}
        stage(f"po{i}_", lambda: build_post(Fin, last, kb_=kb))
        hcur, hnxt = hnxt, hcur
    kb.io = {}
    stage("to_", lambda: emit_xpose(kb, ID, lambda a: hcur.ap()[a][:, 0:2048], lambda b: OUT.ap()[b * 128:(b + 1) * 128, :], 16, 16))
    kb.fused = False
    kb.finish()
    return kb


def fused_inputs(inp, b):
    import math
    cond = np.stack([inp["c"][b], inp["c_ctx"]], 0)
    condT = np.ascontiguousarray(cond.T.reshape(16, 128, 2).transpose(1, 0, 2))
    tri_f, tri_b, ident, masks = ssm_consts()
    C, S, P = rope_tables()
    rp = np.stack([na_tables(inp["na_rpb"][j])[0].reshape(16, 64, 960) for j in range(2)], 0)
    mkn = na_tables(inp["na_rpb"][0])[1].reshape(64, 960)
    vl = lambda a: np.stack([vec_layout(a[i]) for i in range(a.shape[0])], 0)
    return {
        "XTM": np.ascontiguousarray(np.concatenate([inp["x"][b], inp["ctx"][b]], 0)), "CONDT": condT,
        "ADAW": inp["ada_w"], "ADAB": inp["ada_b"], "NMIX": vl(inp["norm_mix_w"]), "NMLP": vl(inp["norm_mlp_w"]),
        "FNW": vec_layout(inp["final_norm_w"]), "W1": inp["mlp_w1"], "W2": inp["mlp_w2"],
        "NAQKV": inp["na_w_qkv"], "NAWO": inp["na_w_o"], "RP": np.ascontiguousarray(rp), "MKN": np.ascontiguousarray(mkn),
        "SSWIN": inp["ssm_w_in"][0], "SSWO": inp["ssm_w_out"][0],
        "CW": np.ascontiguousarray(inp["ssm_conv_w"][0].reshape(5, 48, 128).transpose(2, 1, 0)),
        "CB": np.ascontiguousarray(inp["ssm_conv_b"][0].reshape(48, 128).T),
        "DTB": np.ascontiguousarray(inp["ssm_dt_bias"][0].reshape(128, 1)), "ALOG": np.ascontiguousarray(inp["ssm_a_log"][0].reshape(128, 1)),
        "NWH": np.ascontiguousarray(inp["ssm_norm_w"][0].reshape(64, 64).T),
        "DSK": np.ascontiguousarray(np.broadcast_to(inp["ssm_d"][0][None, :], (64, 64))),
        "TF": tri_f, "TBM": tri_b, "ID": ident, "MKS": masks,
        "DAQKV": inp["da_w_qkv"][0], "DAWO": inp["da_w_o"][0], "COS": C, "SIN": S, "PM": P,
        "LAM": np.ascontiguousarray(inp["da_lambda"][0].reshape(1, 512)), "SW": np.ascontiguousarray(inp["da_subln_w"][0].reshape(2, 128).T),
    }


def kernel(**inputs):
    inp = {k: np.asarray(v) for k, v in inputs.items()}
    NB = 4
    kb = build_fused()
    res = run(kb, [fused_inputs(inp, b) for b in range(NB)], n=NB)
    return np.stack([res.results[b]["OUT"] for b in range(NB)], 0).astype(np.float32)
```
